# Optimizing a Trainium2 kernel written in Bass

```python
import jax, jax.numpy as jnp
from jax import lax
import numpy as np

D_MODEL = 1024
BATCH = 4
SEQ = 8192
DEPTH = 2

CHUNK = 64
EPS = 1e-6

A_HEADS = 8
A_HEAD_DIM = 64
A_WIDTH = A_HEADS * A_HEAD_DIM
A_LEFT_CHUNKS = 8
A_MAX_REL = 128

B_HEADS = 8
B_KEY_DIM = 64
B_VAL_DIM = 64
B_KWIDTH = B_HEADS * B_KEY_DIM
B_VWIDTH = B_HEADS * B_VAL_DIM

C_HEADS = 8
C_Q_RANK = 256
C_KV_RANK = 128
C_NOPE = 64
C_ROPE = 32
C_V = 64
C_WIDTH = C_HEADS * C_V
C_QBLOCK = 128
ROPE_BASE = 10000.0

N_BRANCH = 3
BRANCH_WIDTH = 512
D_FF = 4 * D_MODEL

IN_SPLIT_SIZES = (A_WIDTH, A_WIDTH, A_WIDTH,
                  B_KWIDTH, B_KWIDTH, B_VWIDTH, B_VWIDTH,
                  C_Q_RANK, C_KV_RANK, C_ROPE,
                  N_BRANCH * D_MODEL)
IN_COLS = sum(IN_SPLIT_SIZES)

kernel_name = 'hybrid_gated_streaming_block'


def rmsnorm(x, g):
    x32 = x.astype(jnp.float32)
    y = x32 * lax.rsqrt(jnp.mean(jnp.square(x32), axis=-1, keepdims=True) + EPS)
    return (y * g.astype(jnp.float32)).astype(x.dtype)


def rope(x, cos, sin):
    x1, x2 = jnp.split(x.astype(jnp.float32), 2, axis=-1)
    return jnp.concatenate([x1 * cos - x2 * sin, x2 * cos + x1 * sin], axis=-1).astype(x.dtype)


def chunked_relpos_attention(q, k, v, rel_table):
    bsz, seq, heads, dh = q.shape
    n_chunks = seq // CHUNK
    pad = A_LEFT_CHUNKS * CHUNK
    band = pad + CHUNK
    k_pad = jnp.pad(k, ((0, 0), (pad, 0), (0, 0), (0, 0)))
    v_pad = jnp.pad(v, ((0, 0), (pad, 0), (0, 0), (0, 0)))
    qi = jnp.arange(CHUNK)[:, None]
    kj = jnp.arange(band)[None, :]
    rel = jnp.clip(qi + pad - kj, -A_MAX_REL, A_MAX_REL) + A_MAX_REL
    bias = rel_table.astype(jnp.float32)[:, rel]
    q_chunks = q.reshape(bsz, n_chunks, CHUNK, heads, dh).transpose(1, 0, 2, 3, 4)
    scale = dh ** -0.5

    def one_chunk(args):
        c, q_blk = args
        k_band = lax.dynamic_slice_in_dim(k_pad, c * CHUNK, band, axis=1)
        v_band = lax.dynamic_slice_in_dim(v_pad, c * CHUNK, band, axis=1)
        s = jnp.einsum('bqhd,bkhd->bhqk', q_blk, k_band).astype(jnp.float32) * scale + bias
        valid = kj >= pad - c * CHUNK
        s = jnp.where(valid[None, None], s, -jnp.inf)
        p = jax.nn.softmax(s, axis=-1).astype(v.dtype)
        return jnp.einsum('bhqk,bkhd->bqhd', p, v_band)

    out = lax.map(one_chunk, (jnp.arange(n_chunks), q_chunks))
    return out.transpose(1, 0, 2, 3, 4).reshape(bsz, seq, heads * dh)


def hgrn2_recurrence(q, f_logit, i_val, g, lower_bound, norm_g):
    bsz, seq, _ = q.shape
    n_chunks = seq // CHUNK
    f32 = jnp.float32

    def to_chunks(t, d):
        return t.astype(f32).reshape(bsz, n_chunks, CHUNK, B_HEADS, d).transpose(1, 0, 3, 2, 4)

    z = to_chunks(f_logit, B_KEY_DIM)
    lb = lower_bound.reshape(1, B_HEADS, 1, B_KEY_DIM)
    log_f = jnp.logaddexp(jnp.log(lb), jnp.log1p(-lb) + jax.nn.log_sigmoid(z))
    k = (1.0 - lb) * jax.nn.sigmoid(-z)
    qc = jax.nn.silu(to_chunks(q, B_KEY_DIM))
    vc = to_chunks(i_val, B_VAL_DIM)
    causal = jnp.tril(jnp.ones((CHUNK, CHUNK), dtype=bool))[None, None, :, :, None]

    def step(state, blk):
        q_b, k_b, lf_b, v_b = blk
        cum = jnp.cumsum(lf_b, axis=2)
        diff = cum[:, :, :, None, :] - cum[:, :, None, :, :]
        decay = jnp.exp(jnp.where(causal, diff, -jnp.inf))
        scores = jnp.einsum('bhid,bhjd,bhijd->bhij', q_b, k_b, decay)
        o = (jnp.einsum('bhij,bhjv->bhiv', scores, v_b)
             + jnp.einsum('bhid,bhdv->bhiv', q_b * jnp.exp(cum), state))
        last = cum[:, :, -1:, :]
        state = (jnp.exp(last[:, :, 0, :])[..., None] * state
                 + jnp.einsum('bhjd,bhjv->bhdv', k_b * jnp.exp(last - cum), v_b))
        return state, o

    s0 = jnp.zeros((bsz, B_HEADS, B_KEY_DIM, B_VAL_DIM), f32)
    _, o = lax.scan(step, s0, (qc, k, log_f, vc))
    o = o.transpose(1, 0, 3, 2, 4).reshape(bsz, seq, B_HEADS, B_VAL_DIM)
    o = rmsnorm(o, norm_g) * jax.nn.silu(g.astype(f32).reshape(bsz, seq, B_HEADS, B_VAL_DIM))
    return o.reshape(bsz, seq, B_VWIDTH).astype(q.dtype)


def mla_attention(c_q, c_kv, k_rope_raw, positions, q_norm_g, kv_norm_g, w_uq, w_ukv):
    bsz, seq, _ = c_q.shape
    q = (rmsnorm(c_q, q_norm_g) @ w_uq).reshape(bsz, seq, C_HEADS, C_NOPE + C_ROPE)
    kv = (rmsnorm(c_kv, kv_norm_g) @ w_ukv).reshape(bsz, seq, C_HEADS, C_NOPE + C_V)
    q_nope, q_rope = q[..., :C_NOPE], q[..., C_NOPE:]
    k_nope, v = kv[..., :C_NOPE], kv[..., C_NOPE:]
    inv_freq = ROPE_BASE ** (-jnp.arange(0, C_ROPE, 2, dtype=jnp.float32) / C_ROPE)
    ang = positions.astype(jnp.float32)[..., None] * inv_freq
    cos, sin = jnp.cos(ang)[:, :, None, :], jnp.sin(ang)[:, :, None, :]
    q_rope = rope(q_rope, cos, sin)
    k_rope = rope(k_rope_raw[:, :, None, :], cos, sin)[:, :, 0, :]
    n_blocks = seq // C_QBLOCK

    def blocks(t):
        return t.reshape(bsz, n_blocks, C_QBLOCK, C_HEADS, t.shape[-1]).transpose(1, 0, 2, 3, 4)

    key_chunk = jnp.arange(seq) // CHUNK
    scale = (C_NOPE + C_ROPE) ** -0.5

    def one_block(args):
        blk, qn, qr = args
        s = (jnp.einsum('bqhd,bkhd->bhqk', qn, k_nope)
             + jnp.einsum('bqhr,bkr->bhqk', qr, k_rope)).astype(jnp.float32) * scale
        q_chunk = (blk * C_QBLOCK + jnp.arange(C_QBLOCK)) // CHUNK
        mask = key_chunk[None, :] <= q_chunk[:, None]
        s = jnp.where(mask[None, None], s, -jnp.inf)
        p = jax.nn.softmax(s, axis=-1).astype(v.dtype)
        return jnp.einsum('bhqk,bkhd->bqhd', p, v)

    out = lax.map(one_block, (jnp.arange(n_blocks), blocks(q_nope), blocks(q_rope)))
    return out.transpose(1, 0, 2, 3, 4).reshape(bsz, seq, C_WIDTH)


def setup_inputs(seed: int = 0) -> dict:
    key = jax.random.key(seed)
    ks = jax.random.split(key, 20)
    f32 = jnp.float32

    def nrm(k, shape, fan_in):
        return jax.random.normal(k, shape, f32) * (fan_in ** -0.5)

    def gain(k, shape):
        return 1.0 + 0.05 * jax.random.normal(k, shape, f32)

    x = jax.random.normal(ks[0], (BATCH, SEQ, D_MODEL), f32)
    offset = jax.random.randint(ks[1], (BATCH, 1), 0, 4096, dtype=jnp.int32)
    positions = offset + jnp.arange(SEQ, dtype=jnp.int32)[None, :]
    return {
        'x': x,
        'positions': positions,
        'norm_mix_g': gain(ks[2], (DEPTH, D_MODEL)),
        'w_in': nrm(ks[3], (DEPTH, D_MODEL, IN_COLS), D_MODEL),
        'rel_bias': 0.2 * jax.random.normal(ks[4], (DEPTH, A_HEADS, 2 * A_MAX_REL + 1), f32),
        'hgrn_lb_logits': jax.random.normal(ks[5], (DEPTH, B_KWIDTH), f32),
        'hgrn_norm_g': gain(ks[6], (DEPTH, B_VAL_DIM)),
        'mla_q_norm_g': gain(ks[7], (DEPTH, C_Q_RANK)),
        'mla_kv_norm_g': gain(ks[8], (DEPTH, C_KV_RANK)),
        'mla_w_uq': nrm(ks[9], (DEPTH, C_Q_RANK, C_HEADS * (C_NOPE + C_ROPE)), C_Q_RANK),
        'mla_w_ukv': nrm(ks[10], (DEPTH, C_KV_RANK, C_HEADS * (C_NOPE + C_V)), C_KV_RANK),
        'w_branch': nrm(ks[11], (DEPTH, N_BRANCH, BRANCH_WIDTH, D_MODEL), BRANCH_WIDTH),
        'w_out': nrm(ks[12], (DEPTH, D_MODEL, D_MODEL), D_MODEL),
        'norm_ffn_g': gain(ks[13], (DEPTH, D_MODEL)),
        'w_ff1': nrm(ks[14], (DEPTH, D_MODEL, D_FF), D_MODEL),
        'w_ff2': nrm(ks[15], (DEPTH, D_FF, D_MODEL), D_FF),
        'final_norm_g': gain(ks[16], (D_MODEL,)),
    }


def reference(x, positions, norm_mix_g, w_in, rel_bias, hgrn_lb_logits, hgrn_norm_g,
              mla_q_norm_g, mla_kv_norm_g, mla_w_uq, mla_w_ukv, w_branch, w_out,
              norm_ffn_g, w_ff1, w_ff2, final_norm_g):
    bsz, seq, _ = x.shape
    p_lb = jax.nn.softmax(hgrn_lb_logits.astype(jnp.float32), axis=0)
    lb_all = jnp.cumsum(p_lb, axis=0)
    lb_all = lb_all - lb_all[0:1]

    for l in range(DEPTH):
        h = rmsnorm(x, norm_mix_g[l])
        proj = h @ w_in[l]
        parts = []
        start = 0
        for size in IN_SPLIT_SIZES:
            parts.append(proj[..., start:start + size])
            start += size
        a_q, a_k, a_v, b_q, b_f, b_i, b_g, c_q, c_kv, c_kr, gate_logits = parts

        y_a = chunked_relpos_attention(
            a_q.reshape(bsz, seq, A_HEADS, A_HEAD_DIM),
            a_k.reshape(bsz, seq, A_HEADS, A_HEAD_DIM),
            a_v.reshape(bsz, seq, A_HEADS, A_HEAD_DIM),
            rel_bias[l])
        y_b = hgrn2_recurrence(b_q, b_f, b_i, b_g, lb_all[l], hgrn_norm_g[l])
        y_c = mla_attention(c_q, c_kv, c_kr, positions, mla_q_norm_g[l], mla_kv_norm_g[l],
                            mla_w_uq[l], mla_w_ukv[l])

        branches = jnp.stack([y_a.astype(h.dtype), y_b.astype(h.dtype), y_c.astype(h.dtype)], axis=2)
        up = jnp.einsum('bsnw,nwd->bsnd', branches, w_branch[l])
        gates = jax.nn.sigmoid(gate_logits.reshape(bsz, seq, N_BRANCH, D_MODEL))
        merged = jnp.sum(gates * up, axis=2)
        x = x + merged @ w_out[l]

        h2 = rmsnorm(x, norm_ffn_g[l])
        x = x + jnp.square(jax.nn.relu(h2 @ w_ff1[l])) @ w_ff2[l]

    return rmsnorm(x, final_norm_g)
```

```python
import contextlib
import math
import numpy as np
import ml_dtypes
import concourse.bass as bass
import concourse.mybir as mybir
from concourse.bass_utils import run_bass_kernel_spmd

F32 = mybir.dt.float32
BF16 = mybir.dt.bfloat16
I32 = mybir.dt.int32
AF = mybir.ActivationFunctionType
ALU = mybir.AluOpType

D = 1024
KC = 8
DEPTH = 2
IN_COLS = 7072
EPS = 1e-6
SEM_CHUNK = 30000
NDMA_SEM = 12
NEG = -30000.0


class Prog:
    ENGS = ("pe", "act", "dve", "pool", "sp")

    def __init__(self, nc):
        self.nc = nc
        self.ops = []
        self.sync_same = {"act", "dve", "pool"}
        self.barriers = []

    def op(self, eng, fn, reads=(), writes=(), dma=False):
        self.ops.append((eng, fn, tuple(reads), tuple(writes), dma))

    def pe(self, fn, r=(), w=()): self.op("pe", fn, r, w)
    def act(self, fn, r=(), w=()): self.op("act", fn, r, w)
    def dve(self, fn, r=(), w=()): self.op("dve", fn, r, w)
    def pool(self, fn, r=(), w=()): self.op("pool", fn, r, w)
    def dma(self, fn, r=(), w=(), q="sp"): self.op(q, fn, r, w, True)

    def barrier(self):
        self.barriers.append(len(self.ops))

    def emit(self):
        nc = self.nc
        ops = self.ops
        n = len(ops)
        last_w = {}
        readers = {}
        deps = [None] * n
        bset = sorted(set(self.barriers))
        bi = 0
        pending_bar = {}
        last_op = {}
        recent_dma = {e: [] for e in self.ENGS}
        for i, (eng, fn, rd, wr, dma) in enumerate(ops):
            while bi < len(bset) and bset[bi] <= i:
                s = set(last_op.values())
                for q in self.ENGS:
                    s |= set(recent_dma[q][-NDMA_SEM:])
                for e in self.ENGS:
                    pending_bar[e] = set(pending_bar.get(e, set())) | s
                last_w.clear(); readers.clear()
                bi += 1
            d = set()
            for k in rd:
                if k in last_w: d.add(last_w[k])
            for k in wr:
                if k in last_w: d.add(last_w[k])
                for r in readers.get(k, ()): d.add(r)
            if eng in pending_bar:
                d |= pending_bar.pop(eng)
            d.discard(i)
            deps[i] = d
            for k in rd:
                lst = readers.setdefault(k, [])
                if not dma:
                    lst[:] = [r_ for r_ in lst if ops[r_][4] or ops[r_][0] != eng]
                lst.append(i)
            for k in wr:
                last_w[k] = i
                readers[k] = []
            if dma:
                recent_dma[eng].append(i)
            else:
                last_op[eng] = i
        needed = set()
        for i in range(n):
            eng = ops[i][0]
            nd = set()
            for p in deps[i]:
                peng, _, _, _, pdma = ops[p]
                if (not pdma) and peng == eng and eng not in self.sync_same and not ops[i][4]:
                    continue
                nd.add(p)
            deps[i] = nd
            needed |= nd
        cnt = {e: 0 for e in self.ENGS}
        dcnt = {e: 0 for e in self.ENGS}
        sig = {}
        for i, (eng, fn, rd, wr, dma) in enumerate(ops):
            if dma:
                sig[i] = ("d", eng, dcnt[eng]); dcnt[eng] += 1
            elif i in needed:
                cnt[eng] += 1; sig[i] = ("c", eng, cnt[eng])
        stack = contextlib.ExitStack()
        csem = {}
        for e in self.ENGS:
            k = (cnt[e] + SEM_CHUNK - 1) // SEM_CHUNK
            csem[e] = [stack.enter_context(nc.semaphore(f"c_{e}_{j}")) for j in range(k)]
        dsem = {}
        for e in self.ENGS:
            if dcnt[e]:
                dsem[e] = [stack.enter_context(nc.semaphore(f"d_{e}_{j}"))
                           for j in range(min(NDMA_SEM, dcnt[e]))]
        per_eng = {e: [i for i in range(n) if ops[i][0] == e] for e in self.ENGS}
        engobj = {"pe": "tensor", "act": "scalar", "dve": "vector", "pool": "gpsimd", "sp": "sync"}
        self.stats = {"ops": {e: len(per_eng[e]) for e in self.ENGS}, "sig": cnt, "dma": dcnt}

        def target(p):
            s = sig[p]
            if s[0] == "c":
                _, e, v = s
                j = (v - 1) // SEM_CHUNK
                return (("c", e, j), csem[e][j], v - j * SEM_CHUNK)
            _, q, idx = s
            ns = len(dsem[q])
            return (("d", q, idx % ns), dsem[q][idx % ns], 16 * (idx // ns + 1))

        def body(e):
            def run(engine):
                waited = {}
                for i in per_eng[e]:
                    _, fn, rd, wr, dma = ops[i]
                    tg = [target(p) for p in deps[i]]
                    if dma:
                        _, q, idx = sig[i]
                        ns = len(dsem[q])
                        if idx >= ns:
                            tg.append((("d", q, idx % ns), dsem[q][idx % ns], 16 * (idx // ns)))
                    best = {}
                    for key, sem, val in tg:
                        if val > waited.get(key, 0) and val > best.get(key, (None, 0))[1]:
                            best[key] = (sem, val)
                    for key, (sem, val) in best.items():
                        engine.wait_ge(sem, val)
                        waited[key] = val
                    ins = fn(engine)
                    if i in sig:
                        key, sem, val = target(i)
                        ins.then_inc(sem, 16 if dma else 1)
                if e == "sp":
                    for q in self.ENGS:
                        if dcnt[q]:
                            ns = len(dsem[q])
                            for s_ in range(ns):
                                if dcnt[q] > s_:
                                    engine.wait_ge(dsem[q][s_], 16 * ((dcnt[q] - 1 - s_) // ns + 1))
            return run

        with stack:
            with nc.Block() as block:
                for e in self.ENGS:
                    if per_eng[e] or e == "sp":
                        getattr(block, engobj[e])(body(e))
        return self


def make_consts():
    bf = ml_dtypes.bfloat16
    c = {}
    c["ident_f"] = np.eye(128, dtype=np.float32)
    c["ident_b"] = np.eye(128, dtype=np.float32).astype(bf)
    c["ones_b"] = np.ones((128, 128), np.float32).astype(bf)
    bd = np.zeros((128, 128), np.float32)
    bd[:64, :64] = 1.0; bd[64:, 64:] = 1.0
    c["bd_b"] = bd.astype(bf)
    c["bd_f"] = bd
    j = np.arange(128)[:, None] % 64
    i = np.arange(128)[None, :] % 64
    c["cm_f"] = (j <= i).astype(np.float32)
    c["cm4_f"] = np.tile(c["cm_f"][:, 0:64], (1, 4)).astype(np.float32)
    c["bd4_f"] = np.tile(bd, (1, 4)).astype(np.float32)
    ma = np.zeros((128, 640), np.float32)
    kh = (np.arange(128) >= 64)[:, None]
    qh = (np.arange(128) >= 64)[None, :]
    ma[:, 0:128][(~kh) & qh] = NEG
    ma[:, 512:640][kh & (~qh)] = NEG
    c["maskadd"] = ma
    bm = np.ones((128, 128), np.float32)
    bm[64:, :64] = 0.0
    c["bmask_b"] = bm.astype(bf)
    sm = np.ones((128, 512), np.float32)
    sm[:, ::64] = 0.0
    c["scanmask"] = sm
    invf = np.zeros((32, 1), np.float32)
    fr = (10000.0 ** (-np.arange(0, 32, 2, dtype=np.float32) / 32.0)).astype(np.float32)
    invf[0:16, 0] = fr; invf[16:32, 0] = fr
    c["invf"] = invf
    ika = np.zeros((32, 96), np.float32)
    ikb = np.zeros((32, 96), np.float32)
    for t in range(32):
        ika[t, 64 + t] = 1.0
    for t in range(16):
        ikb[16 + t, 64 + t] = -1.0
        ikb[t, 80 + t] = 1.0
    c["ika_b"] = ika.astype(bf)
    c["ikb_b"] = ikb.astype(bf)
    c["ones_col_b"] = np.ones((128, 2), np.float32).astype(bf)
    return c


CONST_SPECS = {
    "ident_f": ([128, 128], F32), "ident_b": ([128, 128], BF16), "ones_b": ([128, 128], BF16),
    "bd_b": ([128, 128], BF16), "bd_f": ([128, 128], F32), "cm_f": ([128, 128], F32),
    "cm4_f": ([128, 256], F32), "bd4_f": ([128, 512], F32),
    "maskadd": ([128, 640], F32), "bmask_b": ([128, 128], BF16), "scanmask": ([128, 512], F32),
    "invf": ([32, 1], F32), "ika_b": ([32, 96], BF16), "ikb_b": ([32, 96], BF16),
    "ones_col_b": ([128, 2], BF16),
}

WEIGHT_SPECS = {
    "norm_mix_g": [DEPTH, 1024], "w_in": [DEPTH, 1024, IN_COLS], "rel_bias": [DEPTH, 8, 257],
    "hgrn_lb_logits": [DEPTH, 512], "hgrn_norm_g": [DEPTH, 64], "mla_q_norm_g": [DEPTH, 256],
    "mla_kv_norm_g": [DEPTH, 128], "mla_w_uq": [DEPTH, 256, 768], "mla_w_ukv": [DEPTH, 128, 1024],
    "w_branch": [DEPTH, 3, 512, 1024], "w_out": [DEPTH, 1024, 1024], "norm_ffn_g": [DEPTH, 1024],
    "w_ff1": [DEPTH, 1024, 4096], "w_ff2": [DEPTH, 4096, 1024], "final_norm_g": [1024],
}


def build(S, depth=DEPTH, phases=None, dbg=False):
    assert S % 512 == 0
    NG = S // 512
    NT = S // 128
    nc = bass.Bass("TRN2", target_bir_lowering=False)
    x_in = nc.dram_tensor("x", [S, D], F32, kind="ExternalInput").ap()
    pos_in = nc.dram_tensor("positions", [1, S], I32, kind="ExternalInput").ap()
    Wd = {k: nc.dram_tensor(k, shp, F32, kind="ExternalInput").ap() for k, shp in WEIGHT_SPECS.items()}
    Cd = {k: nc.dram_tensor(k, shp, dt, kind="ExternalInput").ap() for k, (shp, dt) in CONST_SPECS.items()}
    out_d = nc.dram_tensor("out", [S, D], F32, kind="ExternalOutput").ap()
    dk = {"kind": "ExternalOutput"} if dbg else {}
    xT_d = nc.dram_tensor("xT_d", [D, S], F32, **dk).ap()
    hT_d = nc.dram_tensor("hT_d", [D, S], BF16, **dk).ap()
    YT_d = nc.dram_tensor("YT_d", [1536, S], BF16, **dk).ap()
    qTc_d = nc.dram_tensor("qTc_d", [8, 96, S], BF16, **dk).ap()
    kTc_d = nc.dram_tensor("kTc_d", [8, 96, S], BF16, **dk).ap()
    Vc_d = nc.dram_tensor("Vc_d", [S, 1024], BF16, **dk).ap()
    cos_d = nc.dram_tensor("cos_d", [96, S], F32, **dk).ap()
    sin_d = nc.dram_tensor("sin_d", [96, S], F32, **dk).ap()
    tt_d = nc.dram_tensor("tt_d", [8, 768], F32).ap()
    DD_d = nc.dram_tensor("DD_d", [8, 128, 768], F32).ap()

    P = Prog(nc)
    rr = {"i": 0}

    def ev_eng():
        rr["i"] += 1
        return "dve" if rr["i"] % 2 else "act"

    def copy_op(eng, out, in_, r, w):
        if eng == "act":
            P.act(lambda e, o=out, i=in_: e.activation(o, i, AF.Copy), r, w)
        elif eng == "dve":
            P.dve(lambda e, o=out, i=in_: e.tensor_copy(o, i), r, w)
        else:
            P.pool(lambda e, o=out, i=in_: e.tensor_copy(o, i), r, w)

    def mm(out, lhsT, rhs, start, stop, r, w):
        P.pe(lambda e, o=out, l=lhsT, x=rhs, s=start, t=stop: e.matmul(o, l, x, start=s, stop=t), r, w)

    def tt(eng, out, a, b, op, r, w):
        f = lambda e, o=out, a_=a, b_=b, op_=op: e.tensor_tensor(o, a_, b_, op_)
        P.op(eng, f, r, w)

    def ts(eng, out, a, s1, s2, op0, op1, r, w):
        if op1 is None:
            f = lambda e, o=out, a_=a, s1_=s1, op0_=op0: e.tensor_scalar(o, a_, s1_, None, op0_)
        else:
            f = lambda e, o=out, a_=a, s1_=s1, s2_=s2, op0_=op0, op1_=op1: e.tensor_scalar(o, a_, s1_, s2_, op0_, op1_)
        P.op(eng, f, r, w)

    def stt(out, a, sc, b, op0, op1, r, w):
        P.dve(lambda e, o=out, a_=a, s_=sc, b_=b, o0=op0, o1=op1: e.scalar_tensor_tensor(o, a_, s_, b_, o0, o1), r, w)

    def actf(out, in_, func, r, w, scale=None, bias=None):
        kw = {}
        if scale is not None: kw["scale"] = scale
        if bias is not None: kw["bias"] = bias
        P.act(lambda e, o=out, i=in_, f=func, kw=kw: e.activation(o, i, f, **kw), r, w)

    def recip(out, in_, r, w):
        P.dve(lambda e, o=out, i=in_: e.reciprocal(o, i), r, w)

    def dma(out, in_, r, w, q="sp", nc_ok=False):
        if nc_ok:
            P.dma(lambda e, o=out, i=in_: e.dma_start(out=o, in_=i, allow_slow_non_contiguous=True), r, w, q)
        else:
            P.dma(lambda e, o=out, i=in_: e.dma_start(out=o, in_=i), r, w, q)

    glob = contextlib.ExitStack()
    with glob:
        uid = {"i": 0}

        def SB(stk, name, shape, dt=F32):
            uid["i"] += 1
            return stk.enter_context(nc.sbuf_tensor(f"{name}_{uid['i']}", shape, dt))

        def PS(stk, name, shape=(128, 512), dt=F32):
            uid["i"] += 1
            return stk.enter_context(nc.psum_tensor(f"{name}_{uid['i']}", list(shape), dt))

        C = {}
        for k, (shp, dt) in CONST_SPECS.items():
            C[k] = SB(glob, "c_" + k, shp, dt)
            dma(C[k][:], Cd[k][:, :] if len(shp) == 2 else Cd[k][:], [], ["c_" + k])
        CK = ["c_" + k for k in CONST_SPECS]
        epsc = SB(glob, "epsc", [128, 1])
        P.pool(lambda e: e.memset(epsc[:], EPS), [], ["epsc"])
        halfpi = SB(glob, "halfpi", [128, 1])
        P.pool(lambda e: e.memset(halfpi[:], math.pi / 2), [], ["halfpi"])

        def rstd_from(ps_ap, out_ap, tmp_ap, inv_n, np_, r, w, tmpk):
            actf(tmp_ap, ps_ap, AF.Ln, r + ["epsc"], [tmpk], scale=inv_n, bias=epsc[0:np_, :])
            actf(out_ap, tmp_ap, AF.Exp, [tmpk], w, scale=-0.5)

        stage_rr = {"i": 0}

        def phase0():
            st = contextlib.ExitStack()
            with st:
                xin = [SB(st, f"p0_xin{i}", [128, 4, D]) for i in range(2)]
                xo = [SB(st, f"p0_xo{i}", [128, 8, 512]) for i in range(2)]
                pb = [PS(st, f"p0_ps{i}") for i in range(4)]
                import os as _os
                k = 0
                for g in range(NG if _os.environ.get("P0MODE", "all") in ("all", "tr") else 0):
                    xi = xin[g % 2]; xoo = xo[g % 2]
                    dma(xi[:], x_in[g * 512:(g + 1) * 512, :].rearrange("(t p) d -> p t d", p=128),
                        [], [f"xin{g % 2}"])
                    for c in range(8):
                        b = pb[k % 4]; bk = f"p0ps{k % 4}"; k += 1
                        for t in range(4):
                            P.pe(lambda e, o=b[:, t * 128:(t + 1) * 128], i=xi[:, t, c * 128:(c + 1) * 128]:
                                 e.transpose(o, i, C["ident_f"][:]),
                                 [f"xin{g % 2}", "c_ident_f"], [bk])
                        copy_op(ev_eng(), xoo[:, c, :], b[:, :], [bk], [f"xo{g % 2}_{c}"])
                    dma(xT_d.rearrange("(c p) s -> p c s", p=128)[:, :, g * 512:(g + 1) * 512], xoo[:],
                        [f"xo{g % 2}_{c}" for c in range(8)], [f"xT{g}"], q="pool")
                CH = min(S, 2048)
                pi_ = SB(st, "p0_pi", [32, CH], I32)
                a = SB(st, "p0_a", [32, CH]); kf = SB(st, "p0_kf", [32, CH]); ki = SB(st, "p0_ki", [32, CH], I32)
                r_ = SB(st, "p0_r", [32, CH]); t_ = SB(st, "p0_t", [32, CH])
                co = SB(st, "p0_co", [32, CH]); si = SB(st, "p0_si", [32, CH])
                lo1 = SB(st, "p0_lo1", [64, CH]); lo0 = SB(st, "p0_lo0", [64, CH])
                P.pool(lambda e: e.memset(lo1[:], 1.0), [], ["lo1"])
                P.pool(lambda e: e.memset(lo0[:], 0.0), [], ["lo0"])
                C1 = 6.28125
                C2 = 2 * math.pi - 6.28125
                cut = int(_os.environ.get("P0CUT", "99"))
                for ch in range(S // CH if _os.environ.get("P0MODE", "all") in ("all", "cs") else 0):
                    src = bass.AP(tensor=pos_in.tensor, offset=ch * CH, ap=[[0, 32], [1, CH]])
                    dma(pi_[:], src, [], ["pi"])
                    seq = []
                    seq.append(lambda: P.dve(lambda e: e.tensor_copy(a[:], pi_[:]), ["pi"], ["a"]))
                    seq.append(lambda: ts("dve", a[:], a[:], C["invf"][:], None, ALU.mult, None, ["a", "c_invf"], ["a"]))
                    seq.append(lambda: ts("dve", kf[:], a[:], 1.0 / (2 * math.pi), None, ALU.mult, None, ["a"], ["kf"]))
                    seq.append(lambda: P.dve(lambda e: e.tensor_copy(ki[:], kf[:]), ["kf"], ["ki"]))
                    seq.append(lambda: P.dve(lambda e: e.tensor_copy(kf[:], ki[:]), ["ki"], ["kf"]))
                    seq.append(lambda: stt(r_[:], kf[:], -C1, a[:], ALU.mult, ALU.add, ["kf", "a"], ["r"]))
                    seq.append(lambda: stt(r_[:], kf[:], -C2, r_[:], ALU.mult, ALU.add, ["kf", "r"], ["r"]))
                    seq.append(lambda: ts("dve", t_[:], r_[:], -math.pi, 1e30, ALU.add, ALU.mult, ["r"], ["t"]))
                    seq.append(lambda: ts("dve", t_[:], t_[:], 0.0, 1.0, ALU.max, ALU.min, ["t"], ["t"]))
                    seq.append(lambda: stt(r_[:], t_[:], -2 * math.pi, r_[:], ALU.mult, ALU.add, ["t", "r"], ["r"]))
                    seq.append(lambda: ts("dve", t_[:], r_[:], math.pi, -1e30, ALU.add, ALU.mult, ["r"], ["t"]))
                    seq.append(lambda: ts("dve", t_[:], t_[:], 0.0, 1.0, ALU.max, ALU.min, ["t"], ["t"]))
                    seq.append(lambda: stt(r_[:], t_[:], 2 * math.pi, r_[:], ALU.mult, ALU.add, ["t", "r"], ["r"]))
                    seq.append(lambda: ts("dve", r_[:], r_[:], math.pi, -math.pi, ALU.min, ALU.max, ["r"], ["r"]))
                    seq.append(lambda: actf(si[:], r_[:], AF.Sin, ["r"], ["si"]))
                    seq.append(lambda: stt(t_[:], r_[:], -1.0, r_[:], ALU.mult, ALU.max, ["r"], ["t"]))
                    seq.append(lambda: actf(co[:], t_[:], AF.Sin, ["t", "halfpi"], ["co"], scale=-1.0, bias=halfpi[0:32, :]))
                    for f_ in seq[:cut]:
                        f_()
                    dma(cos_d[64:96, ch * CH:(ch + 1) * CH], co[:, :], ["co", "r", "a", "kf", "t"], [f"cos{ch}"], q="pool")
                    dma(sin_d[64:96, ch * CH:(ch + 1) * CH], si[:, :], ["si", "r", "a", "kf", "t"], [f"sin{ch}"], q="pool")
                    dma(cos_d[0:64, ch * CH:(ch + 1) * CH], lo1[:, :], ["lo1"], [f"cos{ch}"], q="pool")
                    dma(sin_d[0:64, ch * CH:(ch + 1) * CH], lo0[:, :], ["lo0"], [f"sin{ch}"], q="pool")
            P.barrier()

        def load_cast(stg, dst_ap, src_ap, ncols, rows=128, scale_ap=None, dstkey=None, neg=False):
            i = stage_rr["i"]; stage_rr["i"] += 1
            s = stg[i % len(stg)]; sk = f"stg{i % len(stg)}"
            if len(src_ap.shape) == 3:
                dst_stage = s[0:rows, 0:ncols].rearrange("p (a b) -> p a b", a=src_ap.shape[1])
            else:
                dst_stage = s[0:rows, 0:ncols]
            dma(dst_stage, src_ap, [], [sk])
            eng = ("dve", "act")[i % 2]
            if scale_ap is not None:
                ts("dve", dst_ap, s[0:rows, 0:ncols], scale_ap, None, ALU.mult, None, [sk, "wprep"], [dstkey])
            elif neg:
                ts("dve", dst_ap, s[0:rows, 0:ncols], -1.0, None, ALU.mult, None, [sk], [dstkey])
            else:
                copy_op(eng, dst_ap, s[0:rows, 0:ncols], [sk], [dstkey])

        def phase1a(l):
            st = contextlib.ExitStack()
            with st:
                stg = [SB(st, f"a_stg{i}", [128, 1024]) for i in range(2)]
                Wa = SB(st, "a_Wa", [128, 8, 1952], BF16)
                g1 = SB(st, "a_g1", [128, 8])
                gq = SB(st, "a_gq", [128, 2]); gkv = SB(st, "a_gkv", [128, 1])
                Wqa = SB(st, "a_Wqa", [128, 2, 768], BF16); Wqb = SB(st, "a_Wqb", [128, 2, 768], BF16)
                Wka = SB(st, "a_Wka", [128, 8, 96], BF16); Wv = SB(st, "a_Wv", [128, 512], BF16)
                BT = SB(st, "a_BT", [128, 8, 640], BF16)
                xg = SB(st, "a_xg", [128, 8, 512]); sq = SB(st, "a_sq", [128, 8, 512], BF16)
                xgf = xg[:].rearrange("p c s -> p (c s)")
                braw = xgf[:, 0:640]
                rbs = xgf[0:8, 1024:1281]; ttsb = xgf[0:8, 2048:2816]
                hTs = [SB(st, f"a_hT{i}", [128, 8, 512], BF16) for i in range(2)]
                rstd = SB(st, "a_rstd", [128, 512]); lnt = SB(st, "a_lnt", [128, 512])
                qTa = SB(st, "a_qTa", [128, 4, 512], BF16)
                kTa = [SB(st, f"a_kTa{i}", [128, 4, 512], BF16) for i in range(2)]
                Va = [SB(st, f"a_Va{i}", [128, 4, 8, 128], BF16) for i in range(2)]
                PT = [SB(st, f"a_PT{i}", [128, 640], BF16) for i in range(2)]
                recA = SB(st, "a_recA", [128, 512])
                yAT = SB(st, "a_yAT", [128, 4, 512], BF16)
                cqT = SB(st, "a_cqT", [128, 2, 512], BF16); ckvT = SB(st, "a_ckvT", [128, 512], BF16)
                ckrT = SB(st, "a_ckrT", [32, 512], BF16)
                sqq = SB(st, "a_sqq", [128, 2, 512], BF16); sqkv = SB(st, "a_sqkv", [128, 512], BF16)
                rq = SB(st, "a_rq", [128, 512]); rkv = SB(st, "a_rkv", [128, 512]); rkt = SB(st, "a_rkt", [128, 4])
                lnt2 = SB(st, "a_lnt2", [128, 512]); lnt3 = SB(st, "a_lnt3", [128, 4])
                cosg = SB(st, "a_cosg", [96, 512]); sing = SB(st, "a_sing", [96, 512])
                cxq = SB(st, "a_cxq", [96, 512]); sxq = SB(st, "a_sxq", [96, 512]); cxk = SB(st, "a_cxk", [96, 512])
                t1 = [SB(st, f"a_t1{i}", [96, 512]) for i in range(2)]
                t2 = [SB(st, f"a_t2{i}", [96, 512]) for i in range(2)]
                t2k = SB(st, "a_t2k", [96, 512])
                qst = SB(st, "a_qst", [96, 8, 512], BF16); kst = SB(st, "a_kst", [96, 8, 512], BF16)
                vst = SB(st, "a_vst", [128, 4, 8, 128], BF16)
                pp = [PS(st, f"a_pp{i}") for i in range(2)]
                pss = PS(st, "a_pss")
                psAs = [PS(st, f"a_psA{i}", (128, 1024)) for i in range(2)]
                pacc = PS(st, "a_pacc")
                pc = pp

                win = Wd["w_in"][l].rearrange("(c p) n -> p c n", p=128)
                for kc in range(8):
                    load_cast(stg, Wa[:, kc, 0:1024], win[:, kc, 0:1024], 1024, dstkey="Wa")
                    load_cast(stg, Wa[:, kc, 1024:1536], win[:, kc, 1024:1536], 512, dstkey="Wa")
                    load_cast(stg, Wa[:, kc, 1536:1952], win[:, kc, 3584:4000], 416, dstkey="Wa")
                dma(g1[:], Wd["norm_mix_g"][l].rearrange("(c p) -> p c", p=128), [], ["wprep"], nc_ok=True)
                dma(gq[:], Wd["mla_q_norm_g"][l].rearrange("(c p) -> p c", p=128), [], ["wprep"], nc_ok=True)
                dma(gkv[:], Wd["mla_kv_norm_g"][l].rearrange("(c p) -> p c", p=128), [], ["wprep"], nc_ok=True)
                wuq = Wd["mla_w_uq"][l].rearrange("(c p) n -> p c n", p=128)
                P.pool(lambda e: e.memset(Wqb[:], 0.0), [], ["Wqb"])
                P.pool(lambda e: e.memset(Wka[:], 0.0), [], ["Wka"])
                for c2 in range(2):
                    load_cast(stg, Wqa[:, c2, :], wuq[:, c2, :], 768, scale_ap=gq[:, c2:c2 + 1], dstkey="Wqa")
                    wa3 = Wqa[:, c2, :].rearrange("p (h d) -> p h d", d=96)
                    wb3 = Wqb[:, c2, :].rearrange("p (h d) -> p h d", d=96)
                    ts("dve", wb3[:, :, 64:80], wa3[:, :, 80:96], -1.0, None, ALU.mult, None, ["Wqa"], ["Wqb"])
                    P.dve(lambda e, o=wb3[:, :, 80:96], i=wa3[:, :, 64:80]: e.tensor_copy(o, i), ["Wqa"], ["Wqb"])
                wukv = Wd["mla_w_ukv"][l]
                i_ = stage_rr["i"]; stage_rr["i"] += 1
                s_ = stg[i_ % 2]; sk_ = f"stg{i_ % 2}"
                dma(s_[:, 0:1024], wukv[:, :], [], [sk_])
                s3 = s_[:, 0:1024].rearrange("p (h d) -> p h d", d=128)
                ts("dve", Wka[:, :, 0:64], s3[:, :, 0:64], gkv[:, 0:1], None, ALU.mult, None, [sk_, "wprep", "Wka"], ["Wka"])
                ts("dve", Wv[:, :].rearrange("p (h d) -> p h d", d=64), s3[:, :, 64:128], gkv[:, 0:1], None,
                   ALU.mult, None, [sk_, "wprep"], ["Wv"])
                rb = Wd["rel_bias"][l]
                dma(rbs[:], rb[:, :], [], ["rbs"])
                P.pool(lambda e: e.memset(ttsb[:], 0.0), [], ["ttsb"])
                P.dve(lambda e: e.tensor_copy(ttsb[:, 0:256], rbs[:, 1:257]), ["rbs", "ttsb"], ["ttsb"])
                ts("dve", ttsb[:, 256:767], ttsb[:, 256:767], rbs[:, 256:257], None, ALU.add, None, ["rbs", "ttsb"], ["ttsb"])
                dma(tt_d[:, :], ttsb[:], ["ttsb"], ["tt"])
                for h in range(8):
                    src = bass.AP(tensor=tt_d.tensor, offset=tt_d[h:h + 1, 0:1].offset, ap=[[0, 128], [1, 767]])
                    dma(DD_d[h, :, 0:767], src, ["tt"], ["DD"])
                for h in range(8):
                    src = bass.AP(tensor=DD_d.tensor, offset=DD_d[h, 0:1, 127:128].offset, ap=[[767, 128], [1, 640]])
                    dma(braw[:], src, ["DD"], ["braw"])
                    for j in range(5):
                        tt("dve", BT[:, h, j * 128:(j + 1) * 128], braw[:, (4 - j) * 128:(5 - j) * 128],
                           C["maskadd"][:, j * 128:(j + 1) * 128], ALU.add, ["braw", "c_maskadd"], ["BT"])
                for i in range(2):
                    P.pool(lambda e, v=Va[i]: e.memset(v[:], 1.0), [], [f"Va{i}"])
                P.pool(lambda e: e.memset(vst[:], 1.0), [], ["vst"])

                scale_c = 96.0 ** -0.5
                k = 0
                pi = 0
                def norm(g):
                    par = g % 2
                    hT = hTs[par]
                    tok = slice(g * 512, (g + 1) * 512)
                    dma(xg[:], xT_d.rearrange("(c p) s -> p c s", p=128)[:, :, tok], [f"xT{g}"], ["xg", "braw", "rbs", "ttsb"])
                    actf(sq[:].rearrange("p c s -> p (c s)"), xg[:].rearrange("p c s -> p (c s)"), AF.Square, ["xg"], ["sq"])
                    for c in range(8):
                        mm(pss[:, :], C["ones_b"][:], sq[:, c, :], c == 0, c == 7, ["sq", "c_ones_b"], ["pss"])
                    rstd_from(pss[:, :], rstd[:], lnt[:], 1.0 / D, 128, ["pss"], ["rstd"], "lnt")
                    for c in range(8):
                        stt(hT[:, c, :], xg[:, c, :], g1[:, c:c + 1], rstd[:], ALU.mult, ALU.mult,
                            ["xg", "wprep", "rstd"], [f"hT{par}_{c}"])
                    hk = [f"hT{par}_{c}" for c in range(8)]
                    dma(hT_d.rearrange("(c p) s -> p c s", p=128)[:, :, tok], hT[:], hk, [f"hTd{g}"], q="pool")
                norm(0)
                for g in range(NG):
                    slot = g % 2
                    par = g % 2
                    hT = hTs[par]
                    tok = slice(g * 512, (g + 1) * 512)
                    dma(cosg[:], cos_d[:, tok], [f"cos{(g * 512) // min(S, 2048)}"], ["cosg"])
                    dma(sing[:], sin_d[:, tok], [f"sin{(g * 512) // min(S, 2048)}"], ["sing"])

                    def proj_fm(col0, ncol, M=128):
                        nonlocal k
                        b = pp[k % 2]; bk = f"pp{k % 2}"; k += 1
                        for kc in range(8):
                            mm(b[0:M, :], Wa[:, kc, col0:col0 + ncol], hT[:, kc, :], kc == 0, kc == 7, ["Wa", f"hT{par}_{kc}"], [bk])
                        return b, bk
                    for c in range(4):
                        b, bk = proj_fm(c * 128, 128)
                        ts("dve", qTa[:, c, :], b[:, :], 0.125, None, ALU.mult, None, [bk], ["qTa"])
                        b, bk = proj_fm(512 + c * 128, 128)
                        copy_op("act", kTa[slot][:, c, :], b[:, :], [bk], [f"kTa{slot}"])
                    for t in range(4):
                        b = pp[k % 2]; bk = f"pp{k % 2}"; k += 1
                        for kc in range(8):
                            mm(b[:, :], hT[:, kc, t * 128:(t + 1) * 128], Wa[:, kc, 1024:1536], kc == 0, kc == 7,
                               ["Wa", f"hT{par}_{kc}"], [bk])
                        copy_op(ev_eng(), Va[slot][:, t, :, 0:64], b[:, :].rearrange("p (h d) -> p h d", d=64), [bk], [f"Va{slot}"])
                    if g + 1 < NG:
                        norm(g + 1)
                    pendA = []
                    import os as _os2
                    for ml in range(0 if _os2.environ.get("SKIP_ATT") else 4):
                        m = 4 * g + ml
                        j0 = max(0, 4 - m)
                        for h in range(8):
                            c = h // 2; hp = h % 2
                            ps_ = slice(hp * 64, hp * 64 + 64)
                            pA = psAs[pi % 2]; pAk = f"psA{pi % 2}"
                            pt = PT[pi % 2]; ptk = f"PT{pi % 2}"; pi += 1
                            if j0 < 4:
                                mm(pA[:, j0 * 128:512], C["ident_b"][:], BT[:, h, j0 * 128:512], True, False, ["BT", "c_ident_b"], [pAk])
                            mm(pA[:, 512:640], C["ident_b"][:], BT[:, h, 512:640], True, False, ["BT", "c_ident_b"], [pAk])
                            for j in range(j0, 5):
                                kt = m - 4 + j
                                ks = (kt // 4) % 2; ktl = kt % 4
                                mm(pA[:, j * 128:(j + 1) * 128], kTa[ks][ps_, c, ktl * 128:(ktl + 1) * 128],
                                   qTa[ps_, c, ml * 128:(ml + 1) * 128], False, True,
                                   [f"kTa{ks}", "qTa"], [pAk])
                            actf(pt[:, j0 * 128:640], pA[:, j0 * 128:640], AF.Exp, [pAk], [ptk])

                            def pvA(m=m, ml=ml, h=h, j0=j0, pt=pt, ptk=ptk):
                                hc = h % 4
                                for j in range(j0, 5):
                                    kt = m - 4 + j
                                    ks = (kt // 4) % 2; ktl = kt % 4
                                    mm(pacc[:, hc * 128:(hc + 1) * 128], Va[ks][:, ktl, h, :], pt[:, j * 128:(j + 1) * 128],
                                       j == j0, j == 4, [f"Va{ks}", ptk], ["pacc"])
                                if hc == 3:
                                    recip(recA[64:128, :], pacc[64:128, :], ["pacc"], ["recA"])
                                    for h2 in range(h - 3, h + 1):
                                        c2 = h2 // 2; hp2 = h2 % 2; hc2 = h2 % 4
                                        tt("dve", yAT[hp2 * 64:hp2 * 64 + 64, c2, ml * 128:(ml + 1) * 128],
                                           pacc[0:64, hc2 * 128:(hc2 + 1) * 128], recA[64:128, hc2 * 128:(hc2 + 1) * 128],
                                           ALU.mult, ["pacc", "recA"], ["yAT"])
                            pendA.append(pvA)
                            while len(pendA) > 1:
                                pendA.pop(0)()
                    while pendA:
                        pendA.pop(0)()
                    dma(YT_d[0:512, tok].rearrange("(c p) s -> p c s", p=128), yAT[:], ["yAT"], [f"YTa{g}"], q="pool")

                    if _os2.environ.get("SKIP_C"):
                        continue
                    for c2 in range(2):
                        b, bk = proj_fm(1536 + c2 * 128, 128)
                        copy_op("act", cqT[:, c2, :], b[:, :], [bk], ["cqT"])
                        actf(sqq[:, c2, :], b[:, :], AF.Square, [bk], ["sqq"])
                    b, bk = proj_fm(1792, 128)
                    copy_op("act", ckvT[:, :], b[:, :], [bk], ["ckvT"])
                    actf(sqkv[:, :], b[:, :], AF.Square, [bk], ["sqkv"])
                    b, bk = proj_fm(1920, 32, M=32)
                    copy_op("act", ckrT[:, :], b[0:32, :], [bk], ["ckrT"])
                    for c2 in range(2):
                        mm(pss[:, :], C["ones_b"][:], sqq[:, c2, :], c2 == 0, c2 == 1, ["sqq", "c_ones_b"], ["pss"])
                    rstd_from(pss[:, :], rq[:], lnt2[:], 1.0 / 256, 128, ["pss"], ["rq"], "lnt2")
                    mm(pss[:, :], C["ones_b"][:], sqkv[:, :], True, True, ["sqkv", "c_ones_b"], ["pss"])
                    rstd_from(pss[:, :], rkv[:], lnt2[:], 1.0 / 128, 128, ["pss"], ["rkv"], "lnt2")
                    for t in range(4):
                        mm(pss[:, 2 * t:2 * t + 2], sqkv[:, t * 128:(t + 1) * 128], C["ones_col_b"][:], True, True,
                           ["sqkv", "c_ones_col_b"], ["pss"])
                    rstd_from(pss[:, 0:8].rearrange("p (t two) -> p t two", two=2)[:, :, 0:1].rearrange("p t o -> p (t o)"),
                              rkt[:], lnt3[:], 1.0 / 128, 128, ["pss"], ["rkt"], "lnt3")
                    stt(cxq[:], cosg[:], scale_c, rq[0:96, :], ALU.mult, ALU.mult, ["cosg", "rq"], ["cxq"])
                    stt(sxq[:], sing[:], scale_c, rq[0:96, :], ALU.mult, ALU.mult, ["sing", "rq"], ["sxq"])
                    copy_op("dve", cxk[0:64, :], rkv[0:64, :], ["rkv"], ["cxk_lo"])
                    dma(cxk[64:96, :], cos_d[64:96, tok], [f"cos{(g * 512) // min(S, 2048)}"], ["cxk_hi"])
                    cxkk = ["cxk_lo", "cxk_hi"]
                    bb = pc[0]
                    mm(bb[0:96, :], C["ikb_b"][:], ckrT[:, :], True, True, ["ckrT", "c_ikb_b"], ["pp0"])
                    tt("dve", t2k[:], bb[0:96, :], sing[:], ALU.mult, ["pp0", "sing"], ["t2k"])
                    for h in range(8):
                        ba = pc[0]; bb = pc[1]
                        for c2 in range(2):
                            mm(ba[0:96, :], Wqa[:, c2, h * 96:(h + 1) * 96], cqT[:, c2, :], c2 == 0, c2 == 1, ["Wqa", "cqT"], ["pp0"])
                        for c2 in range(2):
                            mm(bb[0:96, :], Wqb[:, c2, h * 96:(h + 1) * 96], cqT[:, c2, :], c2 == 0, c2 == 1, ["Wqb", "cqT"], ["pp1"])
                        ta = t1[h % 2]; tb = t2[h % 2]
                        tt("dve", ta[:], ba[0:96, :], cxq[:], ALU.mult, ["pp0", "cxq"], [f"t1{h % 2}"])
                        tt("dve", tb[:], bb[0:96, :], sxq[:], ALU.mult, ["pp1", "sxq"], [f"t2{h % 2}"])
                        tt("dve", qst[:, h, :], ta[:], tb[:], ALU.add, [f"t1{h % 2}", f"t2{h % 2}"], ["qst"])
                        b2 = pacc; b2k = "pacc"
                        mm(b2[0:96, :], Wka[:, h, :], ckvT[:, :], True, False, ["Wka", "ckvT"], [b2k])
                        mm(b2[0:96, :], C["ika_b"][:], ckrT[:, :], False, True, ["ckrT", "c_ika_b"], [b2k])
                        tt("dve", ta[:], b2[0:96, :], cxk[:], ALU.mult, [b2k] + cxkk, [f"t1{h % 2}"])
                        tt("dve", kst[:, h, :], ta[:], t2k[:], ALU.add, [f"t1{h % 2}", "t2k"], ["kst"])
                    for t in range(4):
                        b2 = pp[k % 2]; b2k = f"pp{k % 2}"; k += 1
                        mm(b2[:, :], ckvT[:, t * 128:(t + 1) * 128], Wv[:, :], True, True, ["ckvT", "Wv"], [b2k])
                        ts("dve", vst[:, t, :, 0:64], b2[:, :].rearrange("p (h d) -> p h d", d=64), rkt[:, t:t + 1], None,
                           ALU.mult, None, [b2k, "rkt"], ["vst"])
                    dma(qTc_d[:, :, tok].rearrange("h p s -> p h s"), qst[:], ["qst"], [f"qTc{g}"], q="pool")
                    dma(kTc_d[:, :, tok].rearrange("h p s -> p h s"), kst[:], ["kst"], [f"kTc{g}"], q="pool")
                    dma(Vc_d[tok, :].rearrange("(t p) f -> p t f", p=128), vst[:].rearrange("p t h d -> p t (h d)"),
                        ["vst"], [f"Vc{g}"], q="pool")
            P.barrier()

        def phase1b(l):
            import os as _os
            BCUT = int(_os.environ.get("BCUT", "9"))
            st = contextlib.ExitStack()
            with st:
                stg = [SB(st, f"b_stg{i}", [128, 2048]) for i in range(3)]
                Wb = SB(st, "b_Wb", [128, 8, 2048], BF16)
                lbl = SB(st, "b_lbl", [128, 2, 4]); lb = SB(st, "b_lb", [128, 4]); oml = SB(st, "b_oml", [128, 4]); noml = SB(st, "b_noml", [128, 4])
                gn = SB(st, "b_gn", [128, 1])
                hT = SB(st, "b_hT", [128, 8, 512], BF16)
                e1s = [SB(st, f"b_e1{i}", [128, 512]) for i in range(2)]; fs = [SB(st, f"b_f{i}", [128, 512]) for i in range(2)]
                kks = [SB(st, f"b_kk{i}", [128, 512]) for i in range(2)]
                lfs = [SB(st, f"b_lf{i}", [128, 512]) for i in range(2)]; cums = [SB(st, f"b_cum{i}", [128, 512]) for i in range(2)]
                E2s = [SB(st, f"b_E2{i}", [128, 512]) for i in range(2)]; tq2s = [SB(st, f"b_tq2{i}", [128, 512]) for i in range(2)]
                khTfs = [SB(st, f"b_khTf{i}", [128, 512]) for i in range(2)]
                E1 = [SB(st, f"b_E1{c}", [128, 512]) for c in range(4)]
                khT = [SB(st, f"b_khT{c}", [128, 512], BF16) for c in range(4)]
                qhT = [SB(st, f"b_qhT{c}", [128, 512], BF16) for c in range(4)]
                sgT = [SB(st, f"b_sgT{c}", [128, 512]) for c in range(4)]
                kh = SB(st, "b_kh", [128, 4, 512], BF16)
                vB = SB(st, "b_vB", [128, 4, 512], BF16)
                sall = SB(st, "b_sall", [128, 2, 256], BF16)
                tmall = SB(st, "b_tmall", [128, 512])
                stf = SB(st, "b_stf", [128, 4, 128]); stb = SB(st, "b_stb", [128, 4, 128], BF16)
                osqs = [SB(st, f"b_osq{i}", [128, 512], BF16) for i in range(2)]; ros = [SB(st, f"b_ro{i}", [128, 512]) for i in range(2)]
                lnts = [SB(st, f"b_lnt{i}", [128, 512]) for i in range(2)]; yts = [SB(st, f"b_yt{i}", [128, 512]) for i in range(2)]
                yBT = SB(st, "b_yBT", [128, 4, 512], BF16)
                pp = [PS(st, "b_pp0")]
                pp = [pp[0], pp[0]]
                pkv = PS(st, "b_pkv")
                po = [PS(st, f"b_po{i}") for i in range(4)]
                psc = PS(st, "b_psc")
                ptr = PS(st, "b_ptr")

                win = Wd["w_in"][l].rearrange("(c p) n -> p c n", p=128)
                for kc in range(8):
                    load_cast(stg, Wb[:, kc, :], win[:, kc, 1536:3584], 2048, dstkey="Wb")
                if l == 0:
                    P.pool(lambda e: e.memset(lb[:], 0.0), [], ["lb"])
                    P.pool(lambda e: e.memset(oml[:], 1.0), [], ["oml"])
                else:
                    dma(lbl[:, 0, :], Wd["hgrn_lb_logits"][0].rearrange("(c p) -> p c", p=128), [], ["lbl"], nc_ok=True)
                    dma(lbl[:, 1, :], Wd["hgrn_lb_logits"][1].rearrange("(c p) -> p c", p=128), [], ["lbl"], nc_ok=True)
                    tt("dve", lb[:], lbl[:, 0, :], lbl[:, 1, :], ALU.subtract, ["lbl"], ["lb"])
                    actf(lb[:], lb[:], AF.Exp, ["lb"], ["lb"])
                    ts("dve", lb[:], lb[:], 1.0, None, ALU.add, None, ["lb"], ["lb"])
                    recip(lb[:], lb[:], ["lb"], ["lb"])
                    ts("dve", oml[:], lb[:], -1.0, 1.0, ALU.mult, ALU.add, ["lb"], ["oml"])
                ts("dve", noml[:], oml[:], -1.0, None, ALU.mult, None, ["oml"], ["oml"])
                gsrc = Wd["hgrn_norm_g"][l:l + 1, :].rearrange("o d -> d o")
                dma(gn[0:64, :], gsrc, [], ["gn"], nc_ok=True)
                dma(gn[64:128, :], gsrc, [], ["gn"], nc_ok=True)
                P.pool(lambda e: e.memset(stf[:], 0.0), [], ["stf"])
                P.pool(lambda e: e.memset(stb[:], 0.0), [], ["stb"])
                k = 0
                for g in range(NG):
                    tok = slice(g * 512, (g + 1) * 512)
                    dma(hT[:], hT_d.rearrange("(c p) s -> p c s", p=128)[:, :, tok], [f"hTd{g}"], ["hT"])

                    def proj_fm(col0):
                        nonlocal k
                        b = pp[k % 2]; bk = "pp0"; k += 1
                        for kc in range(8):
                            mm(b[:, :], Wb[:, kc, col0:col0 + 128], hT[:, kc, :], kc == 0, kc == 7, ["Wb", "hT"], [bk])
                        return b, bk
                    for t in range(4):
                        b = pp[k % 2]; bk = "pp0"; k += 1
                        for kc in range(8):
                            mm(b[:, :], hT[:, kc, t * 128:(t + 1) * 128], Wb[:, kc, 1024:1536], kc == 0, kc == 7, ["Wb", "hT"], [bk])
                        copy_op(ev_eng(), vB[:, t, :], b[:, :], [bk], ["vB"])
                    for c in range(4):
                        p2 = c % 2
                        e1 = e1s[p2]; f_ = fs[p2]; kk = kks[p2]; lf = lfs[p2]; cum = cums[p2]; E2 = E2s[p2]; tq2 = tq2s[p2]; khTf = khTfs[p2]
                        b, bk = proj_fm(512 + c * 128)
                        actf(e1[:], b[:, :], AF.Sigmoid, [bk], [f"e1{p2}"])
                        bq, bqk = proj_fm(c * 128)
                        actf(tq2[:], bq[:, :], AF.Silu, [bqk], [f"tq2{p2}"])
                        bg_, bgk_ = proj_fm(1536 + c * 128)
                        actf(sgT[c][:], bg_[:, :], AF.Silu, [bgk_], [f"sgT{c}"])
                        ts("dve", f_[:], e1[:], oml[:, c:c + 1], lb[:, c:c + 1], ALU.mult, ALU.add, [f"e1{p2}", "oml", "lb"], [f"f{p2}"])
                        ts("dve", kk[:], e1[:], noml[:, c:c + 1], oml[:, c:c + 1], ALU.mult, ALU.add, [f"e1{p2}", "oml"], [f"kk{p2}"])
                        actf(lf[:], f_[:], AF.Ln, [f"f{p2}"], [f"lf{p2}"])
                        P.dve(lambda e, cum=cum, lf=lf: e.tensor_tensor_scan(cum[:], C["scanmask"][:], lf[:], 0.0, ALU.mult, ALU.add),
                              [f"lf{p2}", "c_scanmask"], [f"cum{p2}"])
                        ts("dve", cum[:], cum[:], -80.0, None, ALU.max, None, [f"cum{p2}"], [f"cum{p2}"])
                        actf(E1[c][:], cum[:], AF.Exp, [f"cum{p2}"], [f"E1{c}"])
                        actf(E2[:], cum[:], AF.Exp, [f"cum{p2}"], [f"E2{p2}"], scale=-1.0)
                        tt("dve", khTf[:], kk[:], E2[:], ALU.mult, [f"kk{p2}", f"E2{p2}"], [f"khTf{p2}"])
                        copy_op("act", khT[c][:], khTf[:], [f"khTf{p2}"], [f"khT{c}"])
                        tt("dve", qhT[c][:], tq2[:], E1[c][:], ALU.mult, [f"tq2{p2}", f"E1{c}"], [f"qhT{c}"])
                        for t in range(4 if BCUT >= 2 else 0):
                            P.pe(lambda e, o=ptr[:, t * 128:(t + 1) * 128], i=khTf[:, t * 128:(t + 1) * 128]:
                                 e.transpose(o, i, C["ident_f"][:]), [f"khTf{p2}", "c_ident_f"], ["ptr"])
                        if BCUT >= 2:
                            copy_op(ev_eng(), kh[:, :, c * 128:(c + 1) * 128],
                                    ptr[:, 0:512].rearrange("p (t d) -> p t d", d=128), ["ptr"], ["kh"])
                    for ch in range(8):
                        t = ch // 2; hf = ch % 2
                        rows = slice(hf * 64, hf * 64 + 64)
                        cols = slice(ch * 64, ch * 64 + 64)
                        for c in range(4):
                            for hp in range(2):
                                hs = slice(hp * 64, hp * 64 + 64)
                                mm((psc if hp == 0 else ptr)[rows, c * 64:(c + 1) * 64],
                                   khT[c][hs, cols], qhT[c][hs, cols], True, True, [f"khT{c}", f"qhT{c}"], ["psc" if hp == 0 else "ptr"])
                        for c in range(4):
                            mm(pkv[:, c * 128:(c + 1) * 128], kh[rows, t, c * 128:(c + 1) * 128], vB[rows, t, c * 128:(c + 1) * 128],
                               True, True, ["kh", "vB"], ["pkv"])
                        tt("dve", sall[rows, 0, :], psc[rows, 0:256], C["cm4_f"][rows, :], ALU.mult, ["psc", "c_cm4_f"], ["sall"])
                        tt("dve", sall[rows, 1, :], ptr[rows, 0:256], C["cm4_f"][rows, :], ALU.mult, ["ptr", "c_cm4_f"], ["sall"])
                        for c in range(4):
                            mm(po[c][:, cols], stb[:, c, :], qhT[c][:, cols], True, False, ["stb", f"qhT{c}"], [f"po{c}"])
                            for hp in range(2):
                                hs = slice(hp * 64, hp * 64 + 64)
                                hcol = slice((2 * c + hp) * 64, (2 * c + hp) * 64 + 64)
                                mm(po[c][hs, cols], vB[rows, t, hcol], sall[rows, hp, c * 64:(c + 1) * 64], False, hp == 1,
                                   ["vB", "sall"], [f"po{c}"])
                        tt("dve", tmall[:], pkv[:, :], C["bd4_f"][:], ALU.mult, ["pkv", "c_bd4_f"], ["tmall"])
                        tt("dve", tmall[:], tmall[:], stf[:].rearrange("p c d -> p (c d)"), ALU.add, ["tmall", "stf"], ["tmall"])
                        for c in range(4):
                            ts("dve", stf[:, c, :], tmall[:, c * 128:(c + 1) * 128], E1[c][:, ch * 64 + 63:ch * 64 + 64], None, ALU.mult, None,
                               ["tmall", f"E1{c}"], ["stf"])
                        copy_op("act", stb[:].rearrange("p c d -> p (c d)"), stf[:].rearrange("p c d -> p (c d)"), ["stf"], ["stb"])
                    for c in range(4 if BCUT >= 4 else 0):
                        p2 = c % 2
                        osq = osqs[p2]; ro = ros[p2]; lnt = lnts[p2]; yt = yts[p2]
                        actf(osq[:], po[c][:, :], AF.Square, [f"po{c}"], [f"osq{p2}"])
                        b = pp[k % 2]; bk = "pp0"; k += 1
                        mm(b[:, :], C["bd_b"][:], osq[:], True, True, [f"osq{p2}", "c_bd_b"], [bk])
                        rstd_from(b[:, :], ro[:], lnt[:], 1.0 / 64, 128, [bk], [f"ro{p2}"], f"lnt{p2}")
                        stt(yt[:], po[c][:, :], gn[:, 0:1], ro[:], ALU.mult, ALU.mult, [f"po{c}", "gn", f"ro{p2}"], [f"yt{p2}"])
                        tt("dve", yBT[:, c, :], yt[:], sgT[c][:], ALU.mult, [f"yt{p2}", f"sgT{c}"], ["yBT"])
                    dma(YT_d[512:1024, tok].rearrange("(c p) s -> p c s", p=128), yBT[:], ["yBT"], [f"YTb{g}"], q="pool")
            P.barrier()

        def phase4(l):
            st = contextlib.ExitStack()
            NB = 4
            LA = 2
            with st:
                kT4 = SB(st, "m_kT4", [96, 4, S], BF16)
                V4 = SB(st, "m_V4", [128, NT, 512], BF16)
                q4 = [SB(st, f"m_q4{i}", [96, 4, 512], BF16) for i in range(2)]
                PTm = [SB(st, f"m_PT{i}", [128, 512], BF16) for i in range(NB)]
                rec = [SB(st, f"m_rec{i}", [128, 512]) for i in range(2)]
                ycT = [SB(st, f"m_ycT{i}", [128, 2, 512], BF16) for i in range(2)]
                pS = [PS(st, f"m_pS{i}") for i in range(NB)]
                pO = [PS(st, f"m_pO{i}") for i in range(2)]
                ks_ = 0; os_ = 0; qi = 0
                pend = []

                def flush(n_keep):
                    while len(pend) > n_keep:
                        pend.pop(0)()
                for hg in range(2):
                    flush(0)
                    dma(kT4[:], kTc_d[hg * 4:(hg + 1) * 4, :, :].rearrange("h p s -> p h s"),
                        [f"kTc{g}" for g in range(NG)], ["kT4"])
                    for part in range(0, NT, 16):
                        pe_ = min(NT, part + 16)
                        dma(V4[:, part:pe_, :], Vc_d[part * 128:pe_ * 128, hg * 512:(hg + 1) * 512].rearrange("(t p) f -> p t f", p=128),
                            [f"Vc{g}" for g in range(NG)], ["V4"])
                    for G in range(NG):
                        q_ = q4[qi % 2]; qk = f"q4{qi % 2}"
                        yc = ycT[qi % 2]; yk = f"ycT{qi % 2}"; qi += 1
                        dma(q_[:], qTc_d[hg * 4:(hg + 1) * 4, :, G * 512:(G + 1) * 512].rearrange("h p s -> p h s"),
                            [f"qTc{G}"], [qk])
                        for hl in range(4):
                            o_ = pO[os_ % 2]; ok = f"pO{os_ % 2}"
                            rc = rec[os_ % 2]; rk = f"rec{os_ % 2}"; os_ += 1
                            last = 4 * G + 3
                            for kt in range(last + 1):
                                j = kt - 4 * G
                                q0 = max(0, j) * 128
                                s_ = pS[ks_ % NB]; sk = f"pS{ks_ % NB}"
                                pt = PTm[ks_ % NB]; pk = f"PTm{ks_ % NB}"; ks_ += 1
                                mm(s_[:, q0:512], kT4[:, hl, kt * 128:(kt + 1) * 128], q_[:, hl, q0:512], True, True, ["kT4", qk], [sk])
                                actf(pt[:, q0:512], s_[:, q0:512], AF.Exp, [sk], [pk])
                                if j >= 0:
                                    tt("dve", pt[:, q0:q0 + 128], pt[:, q0:q0 + 128], C["bmask_b"][:], ALU.mult, [pk, "c_bmask_b"], [pk])

                                def pv(o_=o_, ok=ok, pt=pt, pk=pk, kt=kt, hl=hl, q0=q0, last=last, rc=rc, rk=rk, yc=yc, yk=yk):
                                    mm(o_[:, q0:512], V4[:, kt, hl * 128:(hl + 1) * 128], pt[:, q0:512], kt == 0, kt == last, ["V4", pk], [ok])
                                    if kt == last:
                                        recip(rc[64:128, :], o_[64:128, :], [ok], [rk])
                                        tt("dve", yc[(hl % 2) * 64:(hl % 2) * 64 + 64, hl // 2, :], o_[0:64, :], rc[64:128, :], ALU.mult,
                                           [ok, rk], [yk])
                                pend.append(pv)
                                flush(LA)
                        flush(0)
                        r0 = 1024 + hg * 256
                        dma(YT_d[r0:r0 + 256, G * 512:(G + 1) * 512].rearrange("(c p) s -> p c s", p=128), yc[:], [yk],
                            [f"YTc{hg}_{G}"], q="pool")
            P.barrier()

        def phase5a(l):
            st = contextlib.ExitStack()
            with st:
                stg = [SB(st, f"o_stg{i}", [128, 2048]) for i in range(4)]
                Wg = SB(st, "o_Wg", [128, 8, 3072], BF16)
                Wbr = SB(st, "o_Wbr", [128, 12, 1024], BF16)
                Wo = SB(st, "o_Wo", [128, 8, 1024], BF16)
                hT = SB(st, "o_hT", [128, 8, 512], BF16)
                YT = SB(st, "o_YT", [128, 12, 512], BF16)
                xg = SB(st, "o_xg", [128, 8, 512])
                eg = [SB(st, f"o_eg{i}", [128, 512]) for i in range(3)]
                mtb = [SB(st, f"o_mtb{i}", [128, 512]) for i in range(2)]
                maccs = [SB(st, f"o_macc{i}", [128, 512]) for i in range(2)]
                mT = SB(st, "o_mT", [128, 8, 512], BF16)
                pg = [PS(st, f"o_pg{i}") for i in range(3)]
                pu = [PS(st, f"o_pu{i}") for i in range(3)]
                pd = [PS(st, f"o_pd{i}") for i in range(2)]
                win = Wd["w_in"][l].rearrange("(c p) n -> p c n", p=128)
                for kc in range(8):
                    load_cast(stg, Wg[:, kc, 0:2048], win[:, kc, 4000:6048], 2048, dstkey="Wg")
                    load_cast(stg, Wg[:, kc, 2048:3072], win[:, kc, 6048:7072], 1024, dstkey="Wg")
                wbr = Wd["w_branch"][l].rearrange("n (c p) d -> p (n c) d", p=128)
                for c in range(12):
                    load_cast(stg, Wbr[:, c, :], wbr[:, c, :], 1024, dstkey="Wbr")
                wo = Wd["w_out"][l].rearrange("(c p) n -> p c n", p=128)
                for kc in range(8):
                    load_cast(stg, Wo[:, kc, :], wo[:, kc, :], 1024, dstkey="Wo")
                kg = 0
                for g in range(NG):
                    tok = slice(g * 512, (g + 1) * 512)
                    dma(hT[:], hT_d.rearrange("(c p) s -> p c s", p=128)[:, :, tok], [f"hTd{g}"], ["hT"])
                    dma(YT[:], YT_d[:, tok].rearrange("(c p) s -> p c s", p=128),
                        [f"YTa{g}", f"YTb{g}", f"YTc0_{g}", f"YTc1_{g}"], ["YT"])
                    dma(xg[:], xT_d.rearrange("(c p) s -> p c s", p=128)[:, :, tok], [f"xT{g}"], ["xg"])
                    for dc in range(8):
                        macc = maccs[dc % 2]; mck = f"macc{dc % 2}"
                        for n_ in range(3):
                            bg = pg[kg % 3]; bgk = f"pg{kg % 3}"
                            bu = pu[kg % 3]; buk = f"pu{kg % 3}"
                            e_ = eg[kg % 3]; ek = f"eg{kg % 3}"
                            mt_ = mtb[kg % 2]; mk = f"mtb{kg % 2}"; kg += 1
                            col = n_ * 1024 + dc * 128
                            for kc in range(8):
                                mm(bg[:, :], Wg[:, kc, col:col + 128], hT[:, kc, :], kc == 0, kc == 7, ["Wg", "hT"], [bgk])
                            for c4 in range(4):
                                mm(bu[:, :], Wbr[:, n_ * 4 + c4, dc * 128:(dc + 1) * 128], YT[:, n_ * 4 + c4, :], c4 == 0, c4 == 3,
                                   ["Wbr", "YT"], [buk])
                            actf(e_[:], bg[:, :], AF.Sigmoid, [bgk], [ek])
                            if n_ == 0:
                                tt("dve", macc[:], bu[:, :], e_[:], ALU.mult, [buk, ek], [mck])
                            else:
                                tt("dve", mt_[:], bu[:, :], e_[:], ALU.mult, [buk, ek], [mk])
                                if n_ == 1:
                                    tt("dve", macc[:], macc[:], mt_[:], ALU.add, [mck, mk], [mck])
                                else:
                                    tt("dve", mT[:, dc, :], macc[:], mt_[:], ALU.add, [mck, mk], [f"mT{dc}"])
                    for dc in range(8):
                        b = pd[dc % 2]; bk = f"pd{dc % 2}"
                        for kc in range(8):
                            mm(b[:, :], Wo[:, kc, dc * 128:(dc + 1) * 128], mT[:, kc, :], kc == 0, kc == 7, ["Wo", f"mT{kc}"], [bk])
                        tt("dve", xg[:, dc, :], b[:, :], xg[:, dc, :], ALU.add, [bk, "xg"], ["xg"])
                    dma(xT_d.rearrange("(c p) s -> p c s", p=128)[:, :, tok], xg[:], ["xg"], [f"xT{g}"], q="pool")
            P.barrier()

        def phase5b(l, final):
            st = contextlib.ExitStack()
            GT = 512
            with st:
                xg = SB(st, "f_xg", [128, 8, GT])
                xflat = xg[:].rearrange("p c s -> p (c s)")
                stg = [xflat[:, 0:2048], xflat[:, 2048:4096]]
                XK = ["stg0", "stg1"]
                W1 = SB(st, "f_W1", [128, 8, 4096], BF16)
                W2 = SB(st, "f_W2", [128, 32, 1024], BF16)
                g2 = SB(st, "f_g2", [128, 8]); gf = SB(st, "f_gf", [128, 8])
                hT = SB(st, "f_hT", [128, 8, GT], BF16)
                rstd = SB(st, "f_rstd", [128, GT]); lnt = SB(st, "f_lnt", [128, GT])
                rl = [SB(st, f"f_rl{i}", [128, GT]) for i in range(2)]
                aT = SB(st, "f_aT", [128, 32, GT], BF16)
                sq = aT[:, 0:8, :]
                SQK = [f"aT{i}" for i in range(8)]
                aflat = aT[:].rearrange("p c s -> p (c s)").bitcast(F32)
                yo = aflat[:, 0:4096].rearrange("p (c s) -> p c s", s=GT)
                ot = aflat[:, 4096:8192].rearrange("p (t d) -> p t d", d=D)
                YOK = [f"aT{i}" for i in range(16)]; OTK = [f"aT{i}" for i in range(16, 32)]
                pss = PS(st, "f_pss")
                ph = [PS(st, f"f_ph{i}") for i in range(2)]
                po = [PS(st, f"f_po{i}") for i in range(2)]
                ptr = [PS(st, f"f_ptr{i}") for i in range(2)]
                w1 = Wd["w_ff1"][l].rearrange("(c p) n -> p c n", p=128)
                for kc in range(8):
                    for hh in range(2):
                        load_cast(stg, W1[:, kc, hh * 2048:(hh + 1) * 2048], w1[:, kc, hh * 2048:(hh + 1) * 2048], 2048, dstkey="W1")
                w2 = Wd["w_ff2"][l].rearrange("(c p) n -> p c n", p=128)
                for fc in range(0, 32, 2):
                    load_cast(stg, W2[:, fc:fc + 2, :].rearrange("p a b -> p (a b)"), w2[:, fc:fc + 2, :], 2048, dstkey="W2")
                dma(g2[:], Wd["norm_ffn_g"][l].rearrange("(c p) -> p c", p=128), [], ["g2"], nc_ok=True)
                dma(gf[:], Wd["final_norm_g"].rearrange("(c p) -> p c", p=128), [], ["gf"], nc_ok=True)
                kk_ = 0; tk = 0
                for sg in range(S // GT):
                    g = sg
                    tok = slice(sg * GT, (sg + 1) * GT)
                    dma(xg[:], xT_d.rearrange("(c p) s -> p c s", p=128)[:, :, tok], [f"xT{g}"] + XK, XK)
                    actf(sq.rearrange("p c s -> p (c s)") if False else sq, xg[:], AF.Square, XK, SQK)
                    for c in range(8):
                        mm(pss[:, 0:GT], C["ones_b"][:], sq[:, c, :], c == 0, c == 7, SQK + ["c_ones_b"], ["pss"])
                    rstd_from(pss[:, 0:GT], rstd[:], lnt[:], 1.0 / D, 128, ["pss"], ["rstd"], "lnt")
                    for c in range(8):
                        stt(hT[:, c, :], xg[:, c, :], g2[:, c:c + 1], rstd[:], ALU.mult, ALU.mult, XK + ["g2", "rstd"], ["hT"])
                    for fc in range(32):
                        b = ph[kk_ % 2]; bk = f"ph{kk_ % 2}"
                        r_ = rl[kk_ % 2]; rk = f"rl{kk_ % 2}"; kk_ += 1
                        for kc in range(8):
                            mm(b[:, 0:GT], W1[:, kc, fc * 128:(fc + 1) * 128], hT[:, kc, :], kc == 0, kc == 7, ["W1", "hT"], [bk])
                        actf(r_[:], b[:, 0:GT], AF.Relu, [bk], [rk])
                        tt("dve", aT[:, fc, :], r_[:], r_[:], ALU.mult, [rk], [f"aT{fc}"])
                    for dc in range(8):
                        b = po[dc % 2]; bk = f"po{dc % 2}"
                        for fc in range(32):
                            mm(b[:, 0:GT], W2[:, fc, dc * 128:(dc + 1) * 128], aT[:, fc, :], fc == 0, fc == 31, ["W2", f"aT{fc}"], [bk])
                        tt("dve", xg[:, dc, :], b[:, 0:GT], xg[:, dc, :], ALU.add, [bk] + XK, XK)
                    if not final:
                        dma(xT_d.rearrange("(c p) s -> p c s", p=128)[:, :, tok], xg[:], XK, [f"xT{g}"], q="pool")
                    else:
                        actf(sq, xg[:], AF.Square, XK, SQK)
                        for c in range(8):
                            mm(pss[:, 0:GT], C["ones_b"][:], sq[:, c, :], c == 0, c == 7, SQK + ["c_ones_b"], ["pss"])
                        rstd_from(pss[:, 0:GT], rstd[:], lnt[:], 1.0 / D, 128, ["pss"], ["rstd"], "lnt")
                        for c in range(8):
                            stt(yo[:, c, :], xg[:, c, :], gf[:, c:c + 1], rstd[:], ALU.mult, ALU.mult, XK + ["gf", "rstd"] + SQK, YOK)
                        for t in range(GT // 128):
                            for half in range(2):
                                b = ptr[tk % 2]; bk = f"ptr{tk % 2}"; tk += 1
                                for c4 in range(4):
                                    c = half * 4 + c4
                                    P.pe(lambda e, o=b[:, c4 * 128:(c4 + 1) * 128], i=yo[:, c, t * 128:(t + 1) * 128]:
                                         e.transpose(o, i, C["ident_f"][:]), YOK + ["c_ident_f"], [bk])
                                copy_op(ev_eng(), ot[:, t, half * 512:(half + 1) * 512], b[:, :], [bk], OTK)
                        dma(out_d[tok, :].rearrange("(t p) d -> p t d", p=128), ot, OTK, [f"out{sg}"], q="pool")
            P.barrier()

        ph_all = phases or ["0", "1a", "1b", "4", "5a", "5b"]
        if "0" in ph_all:
            phase0()
        for l in range(depth):
            if "1a" in ph_all: phase1a(l)
            if "1b" in ph_all: phase1b(l)
            if "4" in ph_all: phase4(l)
            if "5a" in ph_all: phase5a(l)
            if "5b" in ph_all: phase5b(l, final=(l == depth - 1))
        P.emit()
    return nc, P


_CACHE = {}


def _get_prog(S):
    if S not in _CACHE:
        _CACHE[S] = build(S)
    return _CACHE[S]


def run_cores(xs, poss, weights, S):
    nc, _ = _get_prog(S)
    consts = make_consts()
    in_maps = []
    for x, p in zip(xs, poss):
        m = {"x": np.ascontiguousarray(x, dtype=np.float32), "positions": np.ascontiguousarray(p, dtype=np.int32)}
        for k in WEIGHT_SPECS:
            m[k] = np.ascontiguousarray(weights[k], dtype=np.float32)
        m.update(consts)
        in_maps.append(m)
    res = run_bass_kernel_spmd(nc, in_maps, core_ids=list(range(len(xs))))
    return [r["out"] for r in res.results]


def kernel(**inputs):
    x = np.asarray(inputs["x"])
    pos = np.asarray(inputs["positions"])
    B, S, _ = x.shape
    weights = {k: np.asarray(inputs[k]) for k in WEIGHT_SPECS}
    ncore = 8
    xs = [x[i % B] for i in range(ncore)]
    ps = [pos[i % B][None, :] for i in range(ncore)]
    outs = run_cores(xs, ps, weights, S)
    return np.stack([outs[b] for b in range(B)], axis=0).astype(np.float32)
```

```python
import contextlib
import math
import numpy as np
import ml_dtypes
import concourse.bass as bass
import concourse.mybir as mybir
from concourse.bass_utils import run_bass_kernel_spmd

F32 = mybir.dt.float32
BF16 = mybir.dt.bfloat16
I32 = mybir.dt.int32
AF = mybir.ActivationFunctionType
ALU = mybir.AluOpType

D = 1024
KC = 8
DEPTH = 2
IN_COLS = 7072
EPS = 1e-6
SEM_CHUNK = 30000
NDMA_SEM = 12
NEG = -30000.0


class Prog:
    ENGS = ("pe", "act", "dve", "pool", "sp")

    def __init__(self, nc):
        self.nc = nc
        self.ops = []
        self.sync_same = {"act", "dve", "pool"}
        self.barriers = []

    def op(self, eng, fn, reads=(), writes=(), dma=False):
        self.ops.append((eng, fn, tuple(reads), tuple(writes), dma))

    def pe(self, fn, r=(), w=()): self.op("pe", fn, r, w)
    def act(self, fn, r=(), w=()): self.op("act", fn, r, w)
    def dve(self, fn, r=(), w=()): self.op("dve", fn, r, w)
    def pool(self, fn, r=(), w=()): self.op("pool", fn, r, w)
    def dma(self, fn, r=(), w=(), q="sp"): self.op(q, fn, r, w, True)

    def barrier(self):
        self.barriers.append(len(self.ops))

    def emit(self):
        nc = self.nc
        ops = self.ops
        n = len(ops)
        last_w = {}
        readers = {}
        deps = [None] * n
        bset = sorted(set(self.barriers))
        bi = 0
        pending_bar = {}
        last_op = {}
        recent_dma = {e: [] for e in self.ENGS}
        for i, (eng, fn, rd, wr, dma) in enumerate(ops):
            while bi < len(bset) and bset[bi] <= i:
                s = set(last_op.values())
                for q in self.ENGS:
                    s |= set(recent_dma[q][-NDMA_SEM:])
                for e in self.ENGS:
                    pending_bar[e] = set(pending_bar.get(e, set())) | s
                last_w.clear(); readers.clear()
                bi += 1
            d = set()
            for k in rd:
                if k in last_w: d.add(last_w[k])
            for k in wr:
                if k in last_w: d.add(last_w[k])
                for r in readers.get(k, ()): d.add(r)
            if eng in pending_bar:
                d |= pending_bar.pop(eng)
            d.discard(i)
            deps[i] = d
            for k in rd:
                lst = readers.setdefault(k, [])
                if not dma:
                    lst[:] = [r_ for r_ in lst if ops[r_][4] or ops[r_][0] != eng]
                lst.append(i)
            for k in wr:
                last_w[k] = i
                readers[k] = []
            if dma:
                recent_dma[eng].append(i)
            else:
                last_op[eng] = i
        needed = set()
        for i in range(n):
            eng = ops[i][0]
            nd = set()
            for p in deps[i]:
                peng, _, _, _, pdma = ops[p]
                if (not pdma) and peng == eng and eng not in self.sync_same and not ops[i][4]:
                    continue
                nd.add(p)
            deps[i] = nd
            needed |= nd
        cnt = {e: 0 for e in self.ENGS}
        dcnt = {e: 0 for e in self.ENGS}
        sig = {}
        for i, (eng, fn, rd, wr, dma) in enumerate(ops):
            if dma:
                sig[i] = ("d", eng, dcnt[eng]); dcnt[eng] += 1
            elif i in needed:
                cnt[eng] += 1; sig[i] = ("c", eng, cnt[eng])
        stack = contextlib.ExitStack()
        csem = {}
        for e in self.ENGS:
            k = (cnt[e] + SEM_CHUNK - 1) // SEM_CHUNK
            csem[e] = [stack.enter_context(nc.semaphore(f"c_{e}_{j}")) for j in range(k)]
        dsem = {}
        for e in self.ENGS:
            if dcnt[e]:
                dsem[e] = [stack.enter_context(nc.semaphore(f"d_{e}_{j}"))
                           for j in range(min(NDMA_SEM, dcnt[e]))]
        per_eng = {e: [i for i in range(n) if ops[i][0] == e] for e in self.ENGS}
        engobj = {"pe": "tensor", "act": "scalar", "dve": "vector", "pool": "gpsimd", "sp": "sync"}
        self.stats = {"ops": {e: len(per_eng[e]) for e in self.ENGS}, "sig": cnt, "dma": dcnt}

        def target(p):
            s = sig[p]
            if s[0] == "c":
                _, e, v = s
                j = (v - 1) // SEM_CHUNK
                return (("c", e, j), csem[e][j], v - j * SEM_CHUNK)
            _, q, idx = s
            ns = len(dsem[q])
            return (("d", q, idx % ns), dsem[q][idx % ns], 16 * (idx // ns + 1))

        def body(e):
            def run(engine):
                waited = {}
                for i in per_eng[e]:
                    _, fn, rd, wr, dma = ops[i]
                    tg = [target(p) for p in deps[i]]
                    if dma:
                        _, q, idx = sig[i]
                        ns = len(dsem[q])
                        if idx >= ns:
                            tg.append((("d", q, idx % ns), dsem[q][idx % ns], 16 * (idx // ns)))
                    best = {}
                    for key, sem, val in tg:
                        if val > waited.get(key, 0) and val > best.get(key, (None, 0))[1]:
                            best[key] = (sem, val)
                    for key, (sem, val) in best.items():
                        engine.wait_ge(sem, val)
                        waited[key] = val
                    ins = fn(engine)
                    if i in sig:
                        key, sem, val = target(i)
                        ins.then_inc(sem, 16 if dma else 1)
                if e == "sp":
                    for q in self.ENGS:
                        if dcnt[q]:
                            ns = len(dsem[q])
                            for s_ in range(ns):
                                if dcnt[q] > s_:
                                    engine.wait_ge(dsem[q][s_], 16 * ((dcnt[q] - 1 - s_) // ns + 1))
            return run

        with stack:
            with nc.Block() as block:
                for e in self.ENGS:
                    if per_eng[e] or e == "sp":
                        getattr(block, engobj[e])(body(e))
        return self


def make_consts():
    bf = ml_dtypes.bfloat16
    c = {}
    c["ident_f"] = np.eye(128, dtype=np.float32)
    c["ident_b"] = np.eye(128, dtype=np.float32).astype(bf)
    c["ones_b"] = np.ones((128, 128), np.float32).astype(bf)
    bd = np.zeros((128, 128), np.float32)
    bd[:64, :64] = 1.0; bd[64:, 64:] = 1.0
    c["bd_b"] = bd.astype(bf)
    c["bd_f"] = bd
    j = np.arange(128)[:, None] % 64
    i = np.arange(128)[None, :] % 64
    c["cm_f"] = (j <= i).astype(np.float32)
    c["cm4_f"] = np.tile(c["cm_f"][:, 0:64], (1, 4)).astype(np.float32)
    c["bd4_f"] = np.tile(bd, (1, 4)).astype(np.float32)
    ma = np.zeros((128, 640), np.float32)
    kh = (np.arange(128) >= 64)[:, None]
    qh = (np.arange(128) >= 64)[None, :]
    ma[:, 0:128][(~kh) & qh] = NEG
    ma[:, 512:640][kh & (~qh)] = NEG
    c["maskadd"] = ma
    bm = np.ones((128, 128), np.float32)
    bm[64:, :64] = 0.0
    c["bmask_b"] = bm.astype(bf)
    sm = np.ones((128, 512), np.float32)
    sm[:, ::64] = 0.0
    c["scanmask"] = sm
    invf = np.zeros((32, 1), np.float32)
    fr = (10000.0 ** (-np.arange(0, 32, 2, dtype=np.float32) / 32.0)).astype(np.float32)
    invf[0:16, 0] = fr; invf[16:32, 0] = fr
    c["invf"] = invf
    ika = np.zeros((32, 96), np.float32)
    ikb = np.zeros((32, 96), np.float32)
    for t in range(32):
        ika[t, 64 + t] = 1.0
    for t in range(16):
        ikb[16 + t, 64 + t] = -1.0
        ikb[t, 80 + t] = 1.0
    c["ika_b"] = ika.astype(bf)
    c["ikb_b"] = ikb.astype(bf)
    c["ones_col_b"] = np.ones((128, 2), np.float32).astype(bf)
    return c


CONST_SPECS = {
    "ident_f": ([128, 128], F32), "ident_b": ([128, 128], BF16), "ones_b": ([128, 128], BF16),
    "bd_b": ([128, 128], BF16), "bd_f": ([128, 128], F32), "cm_f": ([128, 128], F32),
    "cm4_f": ([128, 256], F32), "bd4_f": ([128, 512], F32),
    "maskadd": ([128, 640], F32), "bmask_b": ([128, 128], BF16), "scanmask": ([128, 512], F32),
    "invf": ([32, 1], F32), "ika_b": ([32, 96], BF16), "ikb_b": ([32, 96], BF16),
    "ones_col_b": ([128, 2], BF16),
}

WEIGHT_SPECS = {
    "norm_mix_g": [DEPTH, 1024], "w_in": [DEPTH, 1024, IN_COLS], "rel_bias": [DEPTH, 8, 257],
    "hgrn_lb_logits": [DEPTH, 512], "hgrn_norm_g": [DEPTH, 64], "mla_q_norm_g": [DEPTH, 256],
    "mla_kv_norm_g": [DEPTH, 128], "mla_w_uq": [DEPTH, 256, 768], "mla_w_ukv": [DEPTH, 128, 1024],
    "w_branch": [DEPTH, 3, 512, 1024], "w_out": [DEPTH, 1024, 1024], "norm_ffn_g": [DEPTH, 1024],
    "w_ff1": [DEPTH, 1024, 4096], "w_ff2": [DEPTH, 4096, 1024], "final_norm_g": [1024],
}


def build(S, depth=DEPTH, phases=None, dbg=False):
    assert S % 512 == 0
    NG = S // 512
    NT = S // 128
    nc = bass.Bass("TRN2", target_bir_lowering=False)
    x_in = nc.dram_tensor("x", [S, D], F32, kind="ExternalInput").ap()
    pos_in = nc.dram_tensor("positions", [1, S], I32, kind="ExternalInput").ap()
    Wd = {k: nc.dram_tensor(k, shp, F32, kind="ExternalInput").ap() for k, shp in WEIGHT_SPECS.items()}
    Cd = {k: nc.dram_tensor(k, shp, dt, kind="ExternalInput").ap() for k, (shp, dt) in CONST_SPECS.items()}
    out_d = nc.dram_tensor("out", [S, D], F32, kind="ExternalOutput").ap()
    dk = {"kind": "ExternalOutput"} if dbg else {}
    xT_d = nc.dram_tensor("xT_d", [D, S], F32, **dk).ap()
    hT_d = nc.dram_tensor("hT_d", [D, S], BF16, **dk).ap()
    YT_d = nc.dram_tensor("YT_d", [1536, S], BF16, **dk).ap()
    qTc_d = nc.dram_tensor("qTc_d", [8, 96, S], BF16, **dk).ap()
    kTc_d = nc.dram_tensor("kTc_d", [8, 96, S], BF16, **dk).ap()
    Vc_d = nc.dram_tensor("Vc_d", [S, 1024], BF16, **dk).ap()
    cos_d = nc.dram_tensor("cos_d", [96, S], F32, **dk).ap()
    sin_d = nc.dram_tensor("sin_d", [96, S], F32, **dk).ap()
    tt_d = nc.dram_tensor("tt_d", [8, 768], F32).ap()
    DD_d = nc.dram_tensor("DD_d", [8, 128, 768], F32).ap()

    P = Prog(nc)
    rr = {"i": 0}

    def ev_eng():
        rr["i"] += 1
        return "dve" if rr["i"] % 2 else "act"

    def copy_op(eng, out, in_, r, w):
        if eng == "act":
            P.act(lambda e, o=out, i=in_: e.activation(o, i, AF.Copy), r, w)
        elif eng == "dve":
            P.dve(lambda e, o=out, i=in_: e.tensor_copy(o, i), r, w)
        else:
            P.pool(lambda e, o=out, i=in_: e.tensor_copy(o, i), r, w)

    def mm(out, lhsT, rhs, start, stop, r, w):
        P.pe(lambda e, o=out, l=lhsT, x=rhs, s=start, t=stop: e.matmul(o, l, x, start=s, stop=t), r, w)

    def tt(eng, out, a, b, op, r, w):
        f = lambda e, o=out, a_=a, b_=b, op_=op: e.tensor_tensor(o, a_, b_, op_)
        P.op(eng, f, r, w)

    def ts(eng, out, a, s1, s2, op0, op1, r, w):
        if op1 is None:
            f = lambda e, o=out, a_=a, s1_=s1, op0_=op0: e.tensor_scalar(o, a_, s1_, None, op0_)
        else:
            f = lambda e, o=out, a_=a, s1_=s1, s2_=s2, op0_=op0, op1_=op1: e.tensor_scalar(o, a_, s1_, s2_, op0_, op1_)
        P.op(eng, f, r, w)

    def stt(out, a, sc, b, op0, op1, r, w):
        P.dve(lambda e, o=out, a_=a, s_=sc, b_=b, o0=op0, o1=op1: e.scalar_tensor_tensor(o, a_, s_, b_, o0, o1), r, w)

    def actf(out, in_, func, r, w, scale=None, bias=None):
        kw = {}
        if scale is not None: kw["scale"] = scale
        if bias is not None: kw["bias"] = bias
        P.act(lambda e, o=out, i=in_, f=func, kw=kw: e.activation(o, i, f, **kw), r, w)

    def recip(out, in_, r, w):
        P.dve(lambda e, o=out, i=in_: e.reciprocal(o, i), r, w)

    def dma(out, in_, r, w, q="sp", nc_ok=False):
        if nc_ok:
            P.dma(lambda e, o=out, i=in_: e.dma_start(out=o, in_=i, allow_slow_non_contiguous=True), r, w, q)
        else:
            P.dma(lambda e, o=out, i=in_: e.dma_start(out=o, in_=i), r, w, q)

    glob = contextlib.ExitStack()
    with glob:
        uid = {"i": 0}

        def SB(stk, name, shape, dt=F32):
            uid["i"] += 1
            return stk.enter_context(nc.sbuf_tensor(f"{name}_{uid['i']}", shape, dt))

        def PS(stk, name, shape=(128, 512), dt=F32):
            uid["i"] += 1
            return stk.enter_context(nc.psum_tensor(f"{name}_{uid['i']}", list(shape), dt))

        C = {}
        for k, (shp, dt) in CONST_SPECS.items():
            C[k] = SB(glob, "c_" + k, shp, dt)
            dma(C[k][:], Cd[k][:, :] if len(shp) == 2 else Cd[k][:], [], ["c_" + k])
        CK = ["c_" + k for k in CONST_SPECS]
        epsc = SB(glob, "epsc", [128, 1])
        P.pool(lambda e: e.memset(epsc[:], EPS), [], ["epsc"])
        halfpi = SB(glob, "halfpi", [128, 1])
        P.pool(lambda e: e.memset(halfpi[:], math.pi / 2), [], ["halfpi"])

        def rstd_from(ps_ap, out_ap, tmp_ap, inv_n, np_, r, w, tmpk):
            actf(tmp_ap, ps_ap, AF.Ln, r + ["epsc"], [tmpk], scale=inv_n, bias=epsc[0:np_, :])
            actf(out_ap, tmp_ap, AF.Exp, [tmpk], w, scale=-0.5)

        stage_rr = {"i": 0}

        def phase0():
            st = contextlib.ExitStack()
            with st:
                xin = [SB(st, f"p0_xin{i}", [128, 4, D]) for i in range(2)]
                xo = [SB(st, f"p0_xo{i}", [128, 8, 512]) for i in range(2)]
                pb = [PS(st, f"p0_ps{i}") for i in range(4)]
                import os as _os
                k = 0
                for g in range(NG if _os.environ.get("P0MODE", "all") in ("all", "tr") else 0):
                    xi = xin[g % 2]; xoo = xo[g % 2]
                    dma(xi[:], x_in[g * 512:(g + 1) * 512, :].rearrange("(t p) d -> p t d", p=128),
                        [], [f"xin{g % 2}"])
                    for c in range(8):
                        b = pb[k % 4]; bk = f"p0ps{k % 4}"; k += 1
                        for t in range(4):
                            P.pe(lambda e, o=b[:, t * 128:(t + 1) * 128], i=xi[:, t, c * 128:(c + 1) * 128]:
                                 e.transpose(o, i, C["ident_f"][:]),
                                 [f"xin{g % 2}", "c_ident_f"], [bk])
                        copy_op(ev_eng(), xoo[:, c, :], b[:, :], [bk], [f"xo{g % 2}_{c}"])
                    dma(xT_d.rearrange("(c p) s -> p c s", p=128)[:, :, g * 512:(g + 1) * 512], xoo[:],
                        [f"xo{g % 2}_{c}" for c in range(8)], [f"xT{g}"], q="pool")
                CH = min(S, 2048)
                pi_ = SB(st, "p0_pi", [32, CH], I32)
                a = SB(st, "p0_a", [32, CH]); kf = SB(st, "p0_kf", [32, CH]); ki = SB(st, "p0_ki", [32, CH], I32)
                r_ = SB(st, "p0_r", [32, CH]); t_ = SB(st, "p0_t", [32, CH])
                co = SB(st, "p0_co", [32, CH]); si = SB(st, "p0_si", [32, CH])
                lo1 = SB(st, "p0_lo1", [64, CH]); lo0 = SB(st, "p0_lo0", [64, CH])
                P.pool(lambda e: e.memset(lo1[:], 1.0), [], ["lo1"])
                P.pool(lambda e: e.memset(lo0[:], 0.0), [], ["lo0"])
                C1 = 6.28125
                C2 = 2 * math.pi - 6.28125
                cut = int(_os.environ.get("P0CUT", "99"))
                for ch in range(S // CH if _os.environ.get("P0MODE", "all") in ("all", "cs") else 0):
                    src = bass.AP(tensor=pos_in.tensor, offset=ch * CH, ap=[[0, 32], [1, CH]])
                    dma(pi_[:], src, [], ["pi"])
                    seq = []
                    seq.append(lambda: P.dve(lambda e: e.tensor_copy(a[:], pi_[:]), ["pi"], ["a"]))
                    seq.append(lambda: ts("dve", a[:], a[:], C["invf"][:], None, ALU.mult, None, ["a", "c_invf"], ["a"]))
                    seq.append(lambda: ts("dve", kf[:], a[:], 1.0 / (2 * math.pi), None, ALU.mult, None, ["a"], ["kf"]))
                    seq.append(lambda: P.dve(lambda e: e.tensor_copy(ki[:], kf[:]), ["kf"], ["ki"]))
                    seq.append(lambda: P.dve(lambda e: e.tensor_copy(kf[:], ki[:]), ["ki"], ["kf"]))
                    seq.append(lambda: stt(r_[:], kf[:], -C1, a[:], ALU.mult, ALU.add, ["kf", "a"], ["r"]))
                    seq.append(lambda: stt(r_[:], kf[:], -C2, r_[:], ALU.mult, ALU.add, ["kf", "r"], ["r"]))
                    seq.append(lambda: ts("dve", t_[:], r_[:], -math.pi, 1e30, ALU.add, ALU.mult, ["r"], ["t"]))
                    seq.append(lambda: ts("dve", t_[:], t_[:], 0.0, 1.0, ALU.max, ALU.min, ["t"], ["t"]))
                    seq.append(lambda: stt(r_[:], t_[:], -2 * math.pi, r_[:], ALU.mult, ALU.add, ["t", "r"], ["r"]))
                    seq.append(lambda: ts("dve", t_[:], r_[:], math.pi, -1e30, ALU.add, ALU.mult, ["r"], ["t"]))
                    seq.append(lambda: ts("dve", t_[:], t_[:], 0.0, 1.0, ALU.max, ALU.min, ["t"], ["t"]))
                    seq.append(lambda: stt(r_[:], t_[:], 2 * math.pi, r_[:], ALU.mult, ALU.add, ["t", "r"], ["r"]))
                    seq.append(lambda: ts("dve", r_[:], r_[:], math.pi, -math.pi, ALU.min, ALU.max, ["r"], ["r"]))
                    seq.append(lambda: actf(si[:], r_[:], AF.Sin, ["r"], ["si"]))
                    seq.append(lambda: stt(t_[:], r_[:], -1.0, r_[:], ALU.mult, ALU.max, ["r"], ["t"]))
                    seq.append(lambda: actf(co[:], t_[:], AF.Sin, ["t", "halfpi"], ["co"], scale=-1.0, bias=halfpi[0:32, :]))
                    for f_ in seq[:cut]:
                        f_()
                    dma(cos_d[64:96, ch * CH:(ch + 1) * CH], co[:, :], ["co", "r", "a", "kf", "t"], [f"cos{ch}"], q="pool")
                    dma(sin_d[64:96, ch * CH:(ch + 1) * CH], si[:, :], ["si", "r", "a", "kf", "t"], [f"sin{ch}"], q="pool")
                    dma(cos_d[0:64, ch * CH:(ch + 1) * CH], lo1[:, :], ["lo1"], [f"cos{ch}"], q="pool")
                    dma(sin_d[0:64, ch * CH:(ch + 1) * CH], lo0[:, :], ["lo0"], [f"sin{ch}"], q="pool")
            P.barrier()

        def load_cast(stg, dst_ap, src_ap, ncols, rows=128, scale_ap=None, dstkey=None, neg=False):
            i = stage_rr["i"]; stage_rr["i"] += 1
            s = stg[i % len(stg)]; sk = f"stg{i % len(stg)}"
            if len(src_ap.shape) == 3:
                dst_stage = s[0:rows, 0:ncols].rearrange("p (a b) -> p a b", a=src_ap.shape[1])
            else:
                dst_stage = s[0:rows, 0:ncols]
            dma(dst_stage, src_ap, [], [sk])
            eng = ("dve", "act")[i % 2]
            if scale_ap is not None:
                ts("dve", dst_ap, s[0:rows, 0:ncols], scale_ap, None, ALU.mult, None, [sk, "wprep"], [dstkey])
            elif neg:
                ts("dve", dst_ap, s[0:rows, 0:ncols], -1.0, None, ALU.mult, None, [sk], [dstkey])
            else:
                copy_op(eng, dst_ap, s[0:rows, 0:ncols], [sk], [dstkey])

        def phase1a(l):
            st = contextlib.ExitStack()
            with st:
                stg = [SB(st, f"a_stg{i}", [128, 1536]) for i in range(2)]
                Wa = SB(st, "a_Wa", [128, 8, 1952], BF16)
                g1 = SB(st, "a_g1", [128, 8])
                gq = SB(st, "a_gq", [128, 2]); gkv = SB(st, "a_gkv", [128, 1])
                Wqa = SB(st, "a_Wqa", [128, 2, 768], BF16); Wqb = SB(st, "a_Wqb", [128, 2, 768], BF16)
                Wka = SB(st, "a_Wka", [128, 8, 96], BF16); Wv = SB(st, "a_Wv", [128, 512], BF16)
                BT = SB(st, "a_BT", [128, 8, 640], BF16)
                braw = SB(st, "a_braw", [128, 640])
                xg = SB(st, "a_xg", [128, 8, 512]); sq = SB(st, "a_sq", [128, 8, 512], BF16)
                hT = SB(st, "a_hT", [128, 8, 512], BF16)
                rstd = SB(st, "a_rstd", [128, 512]); lnt = SB(st, "a_lnt", [128, 512])
                qTa = SB(st, "a_qTa", [128, 4, 512], BF16)
                kTa = [SB(st, f"a_kTa{i}", [128, 4, 512], BF16) for i in range(2)]
                Va = [SB(st, f"a_Va{i}", [128, 4, 8, 128], BF16) for i in range(2)]
                PT = [SB(st, f"a_PT{i}", [128, 640], BF16) for i in range(2)]
                recA = SB(st, "a_recA", [128, 512])
                yAT = SB(st, "a_yAT", [128, 4, 512], BF16)
                cqT = SB(st, "a_cqT", [128, 2, 512], BF16); ckvT = SB(st, "a_ckvT", [128, 512], BF16)
                ckrT = SB(st, "a_ckrT", [32, 512], BF16)
                sqq = SB(st, "a_sqq", [128, 2, 512], BF16); sqkv = SB(st, "a_sqkv", [128, 512], BF16)
                rq = SB(st, "a_rq", [128, 512]); rkv = SB(st, "a_rkv", [128, 512]); rkt = SB(st, "a_rkt", [128, 4])
                lnt2 = SB(st, "a_lnt2", [128, 512]); lnt3 = SB(st, "a_lnt3", [128, 4])
                cosg = SB(st, "a_cosg", [96, 512]); sing = SB(st, "a_sing", [96, 512])
                cxq = SB(st, "a_cxq", [96, 512]); sxq = SB(st, "a_sxq", [96, 512]); cxk = SB(st, "a_cxk", [96, 512])
                t1 = [SB(st, f"a_t1{i}", [96, 512]) for i in range(2)]
                t2 = [SB(st, f"a_t2{i}", [96, 512]) for i in range(2)]
                t2k = SB(st, "a_t2k", [96, 512])
                qst = SB(st, "a_qst", [96, 8, 512], BF16); kst = SB(st, "a_kst", [96, 8, 512], BF16)
                vst = SB(st, "a_vst", [128, 4, 8, 128], BF16)
                pp = [PS(st, f"a_pp{i}") for i in range(2)]
                pss = PS(st, "a_pss")
                psAs = [PS(st, f"a_psA{i}", (128, 1024)) for i in range(2)]
                pacc = PS(st, "a_pacc")
                pc = pp

                win = Wd["w_in"][l].rearrange("(c p) n -> p c n", p=128)
                for kc in range(8):
                    load_cast(stg, Wa[:, kc, 0:1536], win[:, kc, 0:1536], 1536, dstkey="Wa")
                    load_cast(stg, Wa[:, kc, 1536:1952], win[:, kc, 3584:4000], 416, dstkey="Wa")
                dma(g1[:], Wd["norm_mix_g"][l].rearrange("(c p) -> p c", p=128), [], ["wprep"], nc_ok=True)
                dma(gq[:], Wd["mla_q_norm_g"][l].rearrange("(c p) -> p c", p=128), [], ["wprep"], nc_ok=True)
                dma(gkv[:], Wd["mla_kv_norm_g"][l].rearrange("(c p) -> p c", p=128), [], ["wprep"], nc_ok=True)
                wuq = Wd["mla_w_uq"][l].rearrange("(c p) n -> p c n", p=128)
                P.pool(lambda e: e.memset(Wqb[:], 0.0), [], ["Wqb"])
                P.pool(lambda e: e.memset(Wka[:], 0.0), [], ["Wka"])
                for c2 in range(2):
                    load_cast(stg, Wqa[:, c2, :], wuq[:, c2, :], 768, scale_ap=gq[:, c2:c2 + 1], dstkey="Wqa")
                    wa3 = Wqa[:, c2, :].rearrange("p (h d) -> p h d", d=96)
                    wb3 = Wqb[:, c2, :].rearrange("p (h d) -> p h d", d=96)
                    ts("dve", wb3[:, :, 64:80], wa3[:, :, 80:96], -1.0, None, ALU.mult, None, ["Wqa"], ["Wqb"])
                    P.dve(lambda e, o=wb3[:, :, 80:96], i=wa3[:, :, 64:80]: e.tensor_copy(o, i), ["Wqa"], ["Wqb"])
                wukv = Wd["mla_w_ukv"][l]
                i_ = stage_rr["i"]; stage_rr["i"] += 1
                s_ = stg[i_ % 2]; sk_ = f"stg{i_ % 2}"
                dma(s_[:, 0:1024], wukv[:, :], [], [sk_])
                s3 = s_[:, 0:1024].rearrange("p (h d) -> p h d", d=128)
                ts("dve", Wka[:, :, 0:64], s3[:, :, 0:64], gkv[:, 0:1], None, ALU.mult, None, [sk_, "wprep", "Wka"], ["Wka"])
                ts("dve", Wv[:, :].rearrange("p (h d) -> p h d", d=64), s3[:, :, 64:128], gkv[:, 0:1], None,
                   ALU.mult, None, [sk_, "wprep"], ["Wv"])
                rb = Wd["rel_bias"][l]
                rbs = SB(st, "a_rbs", [8, 257]); ttsb = SB(st, "a_ttsb", [8, 768])
                dma(rbs[:], rb[:, :], [], ["rbs"])
                P.pool(lambda e: e.memset(ttsb[:], 0.0), [], ["ttsb"])
                P.dve(lambda e: e.tensor_copy(ttsb[:, 0:256], rbs[:, 1:257]), ["rbs", "ttsb"], ["ttsb"])
                ts("dve", ttsb[:, 256:767], ttsb[:, 256:767], rbs[:, 256:257], None, ALU.add, None, ["rbs", "ttsb"], ["ttsb"])
                dma(tt_d[:, :], ttsb[:], ["ttsb"], ["tt"])
                for h in range(8):
                    src = bass.AP(tensor=tt_d.tensor, offset=tt_d[h:h + 1, 0:1].offset, ap=[[0, 128], [1, 767]])
                    dma(DD_d[h, :, 0:767], src, ["tt"], ["DD"])
                for h in range(8):
                    src = bass.AP(tensor=DD_d.tensor, offset=DD_d[h, 0:1, 127:128].offset, ap=[[767, 128], [1, 640]])
                    dma(braw[:], src, ["DD"], ["braw"])
                    for j in range(5):
                        tt("dve", BT[:, h, j * 128:(j + 1) * 128], braw[:, (4 - j) * 128:(5 - j) * 128],
                           C["maskadd"][:, j * 128:(j + 1) * 128], ALU.add, ["braw", "c_maskadd"], ["BT"])
                for i in range(2):
                    P.pool(lambda e, v=Va[i]: e.memset(v[:], 1.0), [], [f"Va{i}"])
                P.pool(lambda e: e.memset(vst[:], 1.0), [], ["vst"])

                scale_c = 96.0 ** -0.5
                k = 0
                pi = 0
                for g in range(NG):
                    slot = g % 2
                    tok = slice(g * 512, (g + 1) * 512)
                    dma(xg[:], xT_d.rearrange("(c p) s -> p c s", p=128)[:, :, tok], [f"xT{g}"], ["xg"])
                    dma(cosg[:], cos_d[:, tok], [f"cos{(g * 512) // min(S, 2048)}"], ["cosg"])
                    dma(sing[:], sin_d[:, tok], [f"sin{(g * 512) // min(S, 2048)}"], ["sing"])
                    actf(sq[:].rearrange("p c s -> p (c s)"), xg[:].rearrange("p c s -> p (c s)"), AF.Square, ["xg"], ["sq"])
                    for c in range(8):
                        mm(pss[:, :], C["ones_b"][:], sq[:, c, :], c == 0, c == 7, ["sq", "c_ones_b"], ["pss"])
                    rstd_from(pss[:, :], rstd[:], lnt[:], 1.0 / D, 128, ["pss"], ["rstd"], "lnt")
                    for c in range(8):
                        stt(hT[:, c, :], xg[:, c, :], g1[:, c:c + 1], rstd[:], ALU.mult, ALU.mult,
                            ["xg", "wprep", "rstd"], [f"hT{c}"])
                    hk = [f"hT{c}" for c in range(8)]
                    dma(hT_d.rearrange("(c p) s -> p c s", p=128)[:, :, tok], hT[:], hk, [f"hTd{g}"], q="pool")

                    def proj_fm(col0, ncol, M=128):
                        nonlocal k
                        b = pp[k % 2]; bk = f"pp{k % 2}"; k += 1
                        for kc in range(8):
                            mm(b[0:M, :], Wa[:, kc, col0:col0 + ncol], hT[:, kc, :], kc == 0, kc == 7, ["Wa", f"hT{kc}"], [bk])
                        return b, bk
                    for c in range(4):
                        b, bk = proj_fm(c * 128, 128)
                        ts("dve", qTa[:, c, :], b[:, :], 0.125, None, ALU.mult, None, [bk], ["qTa"])
                        b, bk = proj_fm(512 + c * 128, 128)
                        copy_op("act", kTa[slot][:, c, :], b[:, :], [bk], [f"kTa{slot}"])
                    for t in range(4):
                        b = pp[k % 2]; bk = f"pp{k % 2}"; k += 1
                        for kc in range(8):
                            mm(b[:, :], hT[:, kc, t * 128:(t + 1) * 128], Wa[:, kc, 1024:1536], kc == 0, kc == 7,
                               ["Wa", f"hT{kc}"], [bk])
                        copy_op(ev_eng(), Va[slot][:, t, :, 0:64], b[:, :].rearrange("p (h d) -> p h d", d=64), [bk], [f"Va{slot}"])
                    pendA = []
                    import os as _os2
                    for ml in range(0 if _os2.environ.get("SKIP_ATT") else 4):
                        m = 4 * g + ml
                        j0 = max(0, 4 - m)
                        for h in range(8):
                            c = h // 2; hp = h % 2
                            ps_ = slice(hp * 64, hp * 64 + 64)
                            pA = psAs[pi % 2]; pAk = f"psA{pi % 2}"
                            pt = PT[pi % 2]; ptk = f"PT{pi % 2}"; pi += 1
                            if j0 < 4:
                                mm(pA[:, j0 * 128:512], C["ident_b"][:], BT[:, h, j0 * 128:512], True, False, ["BT", "c_ident_b"], [pAk])
                            mm(pA[:, 512:640], C["ident_b"][:], BT[:, h, 512:640], True, False, ["BT", "c_ident_b"], [pAk])
                            for j in range(j0, 5):
                                kt = m - 4 + j
                                ks = (kt // 4) % 2; ktl = kt % 4
                                mm(pA[:, j * 128:(j + 1) * 128], kTa[ks][ps_, c, ktl * 128:(ktl + 1) * 128],
                                   qTa[ps_, c, ml * 128:(ml + 1) * 128], False, True,
                                   [f"kTa{ks}", "qTa"], [pAk])
                            actf(pt[:, j0 * 128:640], pA[:, j0 * 128:640], AF.Exp, [pAk], [ptk])

                            def pvA(m=m, ml=ml, h=h, j0=j0, pt=pt, ptk=ptk):
                                hc = h % 4
                                for j in range(j0, 5):
                                    kt = m - 4 + j
                                    ks = (kt // 4) % 2; ktl = kt % 4
                                    mm(pacc[:, hc * 128:(hc + 1) * 128], Va[ks][:, ktl, h, :], pt[:, j * 128:(j + 1) * 128],
                                       j == j0, j == 4, [f"Va{ks}", ptk], ["pacc"])
                                if hc == 3:
                                    recip(recA[64:128, :], pacc[64:128, :], ["pacc"], ["recA"])
                                    for h2 in range(h - 3, h + 1):
                                        c2 = h2 // 2; hp2 = h2 % 2; hc2 = h2 % 4
                                        tt("dve", yAT[hp2 * 64:hp2 * 64 + 64, c2, ml * 128:(ml + 1) * 128],
                                           pacc[0:64, hc2 * 128:(hc2 + 1) * 128], recA[64:128, hc2 * 128:(hc2 + 1) * 128],
                                           ALU.mult, ["pacc", "recA"], ["yAT"])
                            pendA.append(pvA)
                            while len(pendA) > 1:
                                pendA.pop(0)()
                    while pendA:
                        pendA.pop(0)()
                    dma(YT_d[0:512, tok].rearrange("(c p) s -> p c s", p=128), yAT[:], ["yAT"], [f"YTa{g}"], q="pool")

                    if _os2.environ.get("SKIP_C"):
                        continue
                    for c2 in range(2):
                        b, bk = proj_fm(1536 + c2 * 128, 128)
                        copy_op("act", cqT[:, c2, :], b[:, :], [bk], ["cqT"])
                        actf(sqq[:, c2, :], b[:, :], AF.Square, [bk], ["sqq"])
                    b, bk = proj_fm(1792, 128)
                    copy_op("act", ckvT[:, :], b[:, :], [bk], ["ckvT"])
                    actf(sqkv[:, :], b[:, :], AF.Square, [bk], ["sqkv"])
                    b, bk = proj_fm(1920, 32, M=32)
                    copy_op("act", ckrT[:, :], b[0:32, :], [bk], ["ckrT"])
                    for c2 in range(2):
                        mm(pss[:, :], C["ones_b"][:], sqq[:, c2, :], c2 == 0, c2 == 1, ["sqq", "c_ones_b"], ["pss"])
                    rstd_from(pss[:, :], rq[:], lnt2[:], 1.0 / 256, 128, ["pss"], ["rq"], "lnt2")
                    mm(pss[:, :], C["ones_b"][:], sqkv[:, :], True, True, ["sqkv", "c_ones_b"], ["pss"])
                    rstd_from(pss[:, :], rkv[:], lnt2[:], 1.0 / 128, 128, ["pss"], ["rkv"], "lnt2")
                    for t in range(4):
                        mm(pss[:, 2 * t:2 * t + 2], sqkv[:, t * 128:(t + 1) * 128], C["ones_col_b"][:], True, True,
                           ["sqkv", "c_ones_col_b"], ["pss"])
                    rstd_from(pss[:, 0:8].rearrange("p (t two) -> p t two", two=2)[:, :, 0:1].rearrange("p t o -> p (t o)"),
                              rkt[:], lnt3[:], 1.0 / 128, 128, ["pss"], ["rkt"], "lnt3")
                    stt(cxq[:], cosg[:], scale_c, rq[0:96, :], ALU.mult, ALU.mult, ["cosg", "rq"], ["cxq"])
                    stt(sxq[:], sing[:], scale_c, rq[0:96, :], ALU.mult, ALU.mult, ["sing", "rq"], ["sxq"])
                    copy_op("dve", cxk[0:64, :], rkv[0:64, :], ["rkv"], ["cxk_lo"])
                    dma(cxk[64:96, :], cos_d[64:96, tok], [f"cos{(g * 512) // min(S, 2048)}"], ["cxk_hi"])
                    cxkk = ["cxk_lo", "cxk_hi"]
                    bb = pc[0]
                    mm(bb[0:96, :], C["ikb_b"][:], ckrT[:, :], True, True, ["ckrT", "c_ikb_b"], ["pp0"])
                    tt("dve", t2k[:], bb[0:96, :], sing[:], ALU.mult, ["pp0", "sing"], ["t2k"])
                    for h in range(8):
                        ba = pc[0]; bb = pc[1]
                        for c2 in range(2):
                            mm(ba[0:96, :], Wqa[:, c2, h * 96:(h + 1) * 96], cqT[:, c2, :], c2 == 0, c2 == 1, ["Wqa", "cqT"], ["pp0"])
                        for c2 in range(2):
                            mm(bb[0:96, :], Wqb[:, c2, h * 96:(h + 1) * 96], cqT[:, c2, :], c2 == 0, c2 == 1, ["Wqb", "cqT"], ["pp1"])
                        ta = t1[h % 2]; tb = t2[h % 2]
                        tt("dve", ta[:], ba[0:96, :], cxq[:], ALU.mult, ["pp0", "cxq"], [f"t1{h % 2}"])
                        tt("dve", tb[:], bb[0:96, :], sxq[:], ALU.mult, ["pp1", "sxq"], [f"t2{h % 2}"])
                        tt("dve", qst[:, h, :], ta[:], tb[:], ALU.add, [f"t1{h % 2}", f"t2{h % 2}"], ["qst"])
                        b2 = pacc; b2k = "pacc"
                        mm(b2[0:96, :], Wka[:, h, :], ckvT[:, :], True, False, ["Wka", "ckvT"], [b2k])
                        mm(b2[0:96, :], C["ika_b"][:], ckrT[:, :], False, True, ["ckrT", "c_ika_b"], [b2k])
                        tt("dve", ta[:], b2[0:96, :], cxk[:], ALU.mult, [b2k] + cxkk, [f"t1{h % 2}"])
                        tt("dve", kst[:, h, :], ta[:], t2k[:], ALU.add, [f"t1{h % 2}", "t2k"], ["kst"])
                    for t in range(4):
                        b2 = pp[k % 2]; b2k = f"pp{k % 2}"; k += 1
                        mm(b2[:, :], ckvT[:, t * 128:(t + 1) * 128], Wv[:, :], True, True, ["ckvT", "Wv"], [b2k])
                        ts("dve", vst[:, t, :, 0:64], b2[:, :].rearrange("p (h d) -> p h d", d=64), rkt[:, t:t + 1], None,
                           ALU.mult, None, [b2k, "rkt"], ["vst"])
                    dma(qTc_d[:, :, tok].rearrange("h p s -> p h s"), qst[:], ["qst"], [f"qTc{g}"], q="pool")
                    dma(kTc_d[:, :, tok].rearrange("h p s -> p h s"), kst[:], ["kst"], [f"kTc{g}"], q="pool")
                    dma(Vc_d[tok, :].rearrange("(t p) f -> p t f", p=128), vst[:].rearrange("p t h d -> p t (h d)"),
                        ["vst"], [f"Vc{g}"], q="pool")
            P.barrier()

        def phase1b(l):
            import os as _os
            BCUT = int(_os.environ.get("BCUT", "9"))
            st = contextlib.ExitStack()
            with st:
                stg = [SB(st, f"b_stg{i}", [128, 2048]) for i in range(3)]
                Wb = SB(st, "b_Wb", [128, 8, 2048], BF16)
                lbl = SB(st, "b_lbl", [128, 2, 4]); lb = SB(st, "b_lb", [128, 4]); oml = SB(st, "b_oml", [128, 4]); noml = SB(st, "b_noml", [128, 4])
                gn = SB(st, "b_gn", [128, 1])
                hT = SB(st, "b_hT", [128, 8, 512], BF16)
                e1s = [SB(st, f"b_e1{i}", [128, 512]) for i in range(2)]; fs = [SB(st, f"b_f{i}", [128, 512]) for i in range(2)]
                kks = [SB(st, f"b_kk{i}", [128, 512]) for i in range(2)]
                lfs = [SB(st, f"b_lf{i}", [128, 512]) for i in range(2)]; cums = [SB(st, f"b_cum{i}", [128, 512]) for i in range(2)]
                E2s = [SB(st, f"b_E2{i}", [128, 512]) for i in range(2)]; tq2s = [SB(st, f"b_tq2{i}", [128, 512]) for i in range(2)]
                khTfs = [SB(st, f"b_khTf{i}", [128, 512]) for i in range(2)]
                E1 = [SB(st, f"b_E1{c}", [128, 512]) for c in range(4)]
                khT = [SB(st, f"b_khT{c}", [128, 512], BF16) for c in range(4)]
                qhT = [SB(st, f"b_qhT{c}", [128, 512], BF16) for c in range(4)]
                sgT = [SB(st, f"b_sgT{c}", [128, 512]) for c in range(4)]
                kh = SB(st, "b_kh", [128, 4, 512], BF16)
                vB = SB(st, "b_vB", [128, 4, 512], BF16)
                sall = SB(st, "b_sall", [128, 2, 256], BF16)
                tmall = SB(st, "b_tmall", [128, 512])
                stf = SB(st, "b_stf", [128, 4, 128]); stb = SB(st, "b_stb", [128, 4, 128], BF16)
                osqs = [SB(st, f"b_osq{i}", [128, 512], BF16) for i in range(2)]; ros = [SB(st, f"b_ro{i}", [128, 512]) for i in range(2)]
                lnts = [SB(st, f"b_lnt{i}", [128, 512]) for i in range(2)]; yts = [SB(st, f"b_yt{i}", [128, 512]) for i in range(2)]
                yBT = SB(st, "b_yBT", [128, 4, 512], BF16)
                pp = [PS(st, "b_pp0")]
                pp = [pp[0], pp[0]]
                pkv = PS(st, "b_pkv")
                po = [PS(st, f"b_po{i}") for i in range(4)]
                psc = PS(st, "b_psc")
                ptr = PS(st, "b_ptr")

                win = Wd["w_in"][l].rearrange("(c p) n -> p c n", p=128)
                for kc in range(8):
                    load_cast(stg, Wb[:, kc, :], win[:, kc, 1536:3584], 2048, dstkey="Wb")
                if l == 0:
                    P.pool(lambda e: e.memset(lb[:], 0.0), [], ["lb"])
                    P.pool(lambda e: e.memset(oml[:], 1.0), [], ["oml"])
                else:
                    dma(lbl[:, 0, :], Wd["hgrn_lb_logits"][0].rearrange("(c p) -> p c", p=128), [], ["lbl"], nc_ok=True)
                    dma(lbl[:, 1, :], Wd["hgrn_lb_logits"][1].rearrange("(c p) -> p c", p=128), [], ["lbl"], nc_ok=True)
                    tt("dve", lb[:], lbl[:, 0, :], lbl[:, 1, :], ALU.subtract, ["lbl"], ["lb"])
                    actf(lb[:], lb[:], AF.Exp, ["lb"], ["lb"])
                    ts("dve", lb[:], lb[:], 1.0, None, ALU.add, None, ["lb"], ["lb"])
                    recip(lb[:], lb[:], ["lb"], ["lb"])
                    ts("dve", oml[:], lb[:], -1.0, 1.0, ALU.mult, ALU.add, ["lb"], ["oml"])
                ts("dve", noml[:], oml[:], -1.0, None, ALU.mult, None, ["oml"], ["oml"])
                gsrc = Wd["hgrn_norm_g"][l:l + 1, :].rearrange("o d -> d o")
                dma(gn[0:64, :], gsrc, [], ["gn"], nc_ok=True)
                dma(gn[64:128, :], gsrc, [], ["gn"], nc_ok=True)
                P.pool(lambda e: e.memset(stf[:], 0.0), [], ["stf"])
                P.pool(lambda e: e.memset(stb[:], 0.0), [], ["stb"])
                k = 0
                for g in range(NG):
                    tok = slice(g * 512, (g + 1) * 512)
                    dma(hT[:], hT_d.rearrange("(c p) s -> p c s", p=128)[:, :, tok], [f"hTd{g}"], ["hT"])

                    def proj_fm(col0):
                        nonlocal k
                        b = pp[k % 2]; bk = "pp0"; k += 1
                        for kc in range(8):
                            mm(b[:, :], Wb[:, kc, col0:col0 + 128], hT[:, kc, :], kc == 0, kc == 7, ["Wb", "hT"], [bk])
                        return b, bk
                    for t in range(4):
                        b = pp[k % 2]; bk = "pp0"; k += 1
                        for kc in range(8):
                            mm(b[:, :], hT[:, kc, t * 128:(t + 1) * 128], Wb[:, kc, 1024:1536], kc == 0, kc == 7, ["Wb", "hT"], [bk])
                        copy_op(ev_eng(), vB[:, t, :], b[:, :], [bk], ["vB"])
                    for c in range(4):
                        p2 = c % 2
                        e1 = e1s[p2]; f_ = fs[p2]; kk = kks[p2]; lf = lfs[p2]; cum = cums[p2]; E2 = E2s[p2]; tq2 = tq2s[p2]; khTf = khTfs[p2]
                        b, bk = proj_fm(512 + c * 128)
                        actf(e1[:], b[:, :], AF.Sigmoid, [bk], [f"e1{p2}"])
                        bq, bqk = proj_fm(c * 128)
                        actf(tq2[:], bq[:, :], AF.Silu, [bqk], [f"tq2{p2}"])
                        bg_, bgk_ = proj_fm(1536 + c * 128)
                        actf(sgT[c][:], bg_[:, :], AF.Silu, [bgk_], [f"sgT{c}"])
                        ts("dve", f_[:], e1[:], oml[:, c:c + 1], lb[:, c:c + 1], ALU.mult, ALU.add, [f"e1{p2}", "oml", "lb"], [f"f{p2}"])
                        ts("dve", kk[:], e1[:], noml[:, c:c + 1], oml[:, c:c + 1], ALU.mult, ALU.add, [f"e1{p2}", "oml"], [f"kk{p2}"])
                        actf(lf[:], f_[:], AF.Ln, [f"f{p2}"], [f"lf{p2}"])
                        P.dve(lambda e, cum=cum, lf=lf: e.tensor_tensor_scan(cum[:], C["scanmask"][:], lf[:], 0.0, ALU.mult, ALU.add),
                              [f"lf{p2}", "c_scanmask"], [f"cum{p2}"])
                        ts("dve", cum[:], cum[:], -80.0, None, ALU.max, None, [f"cum{p2}"], [f"cum{p2}"])
                        actf(E1[c][:], cum[:], AF.Exp, [f"cum{p2}"], [f"E1{c}"])
                        actf(E2[:], cum[:], AF.Exp, [f"cum{p2}"], [f"E2{p2}"], scale=-1.0)
                        tt("dve", khTf[:], kk[:], E2[:], ALU.mult, [f"kk{p2}", f"E2{p2}"], [f"khTf{p2}"])
                        copy_op("act", khT[c][:], khTf[:], [f"khTf{p2}"], [f"khT{c}"])
                        tt("dve", qhT[c][:], tq2[:], E1[c][:], ALU.mult, [f"tq2{p2}", f"E1{c}"], [f"qhT{c}"])
                        for t in range(4 if BCUT >= 2 else 0):
                            P.pe(lambda e, o=ptr[:, t * 128:(t + 1) * 128], i=khTf[:, t * 128:(t + 1) * 128]:
                                 e.transpose(o, i, C["ident_f"][:]), [f"khTf{p2}", "c_ident_f"], ["ptr"])
                        if BCUT >= 2:
                            copy_op(ev_eng(), kh[:, :, c * 128:(c + 1) * 128],
                                    ptr[:, 0:512].rearrange("p (t d) -> p t d", d=128), ["ptr"], ["kh"])
                    for ch in range(8):
                        t = ch // 2; hf = ch % 2
                        rows = slice(hf * 64, hf * 64 + 64)
                        cols = slice(ch * 64, ch * 64 + 64)
                        for c in range(4):
                            for hp in range(2):
                                hs = slice(hp * 64, hp * 64 + 64)
                                mm((psc if hp == 0 else ptr)[rows, c * 64:(c + 1) * 64],
                                   khT[c][hs, cols], qhT[c][hs, cols], True, True, [f"khT{c}", f"qhT{c}"], ["psc" if hp == 0 else "ptr"])
                        for c in range(4):
                            mm(pkv[:, c * 128:(c + 1) * 128], kh[rows, t, c * 128:(c + 1) * 128], vB[rows, t, c * 128:(c + 1) * 128],
                               True, True, ["kh", "vB"], ["pkv"])
                        tt("dve", sall[rows, 0, :], psc[rows, 0:256], C["cm4_f"][rows, :], ALU.mult, ["psc", "c_cm4_f"], ["sall"])
                        tt("dve", sall[rows, 1, :], ptr[rows, 0:256], C["cm4_f"][rows, :], ALU.mult, ["ptr", "c_cm4_f"], ["sall"])
                        for c in range(4):
                            mm(po[c][:, cols], stb[:, c, :], qhT[c][:, cols], True, False, ["stb", f"qhT{c}"], [f"po{c}"])
                            for hp in range(2):
                                hs = slice(hp * 64, hp * 64 + 64)
                                hcol = slice((2 * c + hp) * 64, (2 * c + hp) * 64 + 64)
                                mm(po[c][hs, cols], vB[rows, t, hcol], sall[rows, hp, c * 64:(c + 1) * 64], False, hp == 1,
                                   ["vB", "sall"], [f"po{c}"])
                        tt("dve", tmall[:], pkv[:, :], C["bd4_f"][:], ALU.mult, ["pkv", "c_bd4_f"], ["tmall"])
                        tt("dve", tmall[:], tmall[:], stf[:].rearrange("p c d -> p (c d)"), ALU.add, ["tmall", "stf"], ["tmall"])
                        for c in range(4):
                            ts("dve", stf[:, c, :], tmall[:, c * 128:(c + 1) * 128], E1[c][:, ch * 64 + 63:ch * 64 + 64], None, ALU.mult, None,
                               ["tmall", f"E1{c}"], ["stf"])
                        copy_op("act", stb[:].rearrange("p c d -> p (c d)"), stf[:].rearrange("p c d -> p (c d)"), ["stf"], ["stb"])
                    for c in range(4 if BCUT >= 4 else 0):
                        p2 = c % 2
                        osq = osqs[p2]; ro = ros[p2]; lnt = lnts[p2]; yt = yts[p2]
                        actf(osq[:], po[c][:, :], AF.Square, [f"po{c}"], [f"osq{p2}"])
                        b = pp[k % 2]; bk = "pp0"; k += 1
                        mm(b[:, :], C["bd_b"][:], osq[:], True, True, [f"osq{p2}", "c_bd_b"], [bk])
                        rstd_from(b[:, :], ro[:], lnt[:], 1.0 / 64, 128, [bk], [f"ro{p2}"], f"lnt{p2}")
                        stt(yt[:], po[c][:, :], gn[:, 0:1], ro[:], ALU.mult, ALU.mult, [f"po{c}", "gn", f"ro{p2}"], [f"yt{p2}"])
                        tt("dve", yBT[:, c, :], yt[:], sgT[c][:], ALU.mult, [f"yt{p2}", f"sgT{c}"], ["yBT"])
                    dma(YT_d[512:1024, tok].rearrange("(c p) s -> p c s", p=128), yBT[:], ["yBT"], [f"YTb{g}"], q="pool")
            P.barrier()

        def phase4(l):
            st = contextlib.ExitStack()
            NB = 5
            LA = 3
            with st:
                kT4 = SB(st, "m_kT4", [96, 4, S], BF16)
                V4 = SB(st, "m_V4", [128, NT, 512], BF16)
                q4 = [SB(st, f"m_q4{i}", [96, 4, 512], BF16) for i in range(2)]
                PTm = [SB(st, f"m_PT{i}", [128, 512], BF16) for i in range(NB)]
                rec = [SB(st, f"m_rec{i}", [128, 512]) for i in range(2)]
                ycT = [SB(st, f"m_ycT{i}", [128, 2, 512], BF16) for i in range(2)]
                pS = [PS(st, f"m_pS{i}") for i in range(NB)]
                pO = [PS(st, f"m_pO{i}") for i in range(2)]
                ks_ = 0; os_ = 0; qi = 0
                pend = []

                def flush(n_keep):
                    while len(pend) > n_keep:
                        pend.pop(0)()
                for hg in range(2):
                    flush(0)
                    dma(kT4[:], kTc_d[hg * 4:(hg + 1) * 4, :, :].rearrange("h p s -> p h s"),
                        [f"kTc{g}" for g in range(NG)], ["kT4"])
                    for part in range(0, NT, 16):
                        pe_ = min(NT, part + 16)
                        dma(V4[:, part:pe_, :], Vc_d[part * 128:pe_ * 128, hg * 512:(hg + 1) * 512].rearrange("(t p) f -> p t f", p=128),
                            [f"Vc{g}" for g in range(NG)], ["V4"])
                    for G in range(NG):
                        q_ = q4[qi % 2]; qk = f"q4{qi % 2}"
                        yc = ycT[qi % 2]; yk = f"ycT{qi % 2}"; qi += 1
                        dma(q_[:], qTc_d[hg * 4:(hg + 1) * 4, :, G * 512:(G + 1) * 512].rearrange("h p s -> p h s"),
                            [f"qTc{G}"], [qk])
                        for hl in range(4):
                            o_ = pO[os_ % 2]; ok = f"pO{os_ % 2}"
                            rc = rec[os_ % 2]; rk = f"rec{os_ % 2}"; os_ += 1
                            last = 4 * G + 3
                            for kt in range(last + 1):
                                j = kt - 4 * G
                                q0 = max(0, j) * 128
                                s_ = pS[ks_ % NB]; sk = f"pS{ks_ % NB}"
                                pt = PTm[ks_ % NB]; pk = f"PTm{ks_ % NB}"; ks_ += 1
                                mm(s_[:, q0:512], kT4[:, hl, kt * 128:(kt + 1) * 128], q_[:, hl, q0:512], True, True, ["kT4", qk], [sk])
                                actf(pt[:, q0:512], s_[:, q0:512], AF.Exp, [sk], [pk])
                                if j >= 0:
                                    tt("dve", pt[:, q0:q0 + 128], pt[:, q0:q0 + 128], C["bmask_b"][:], ALU.mult, [pk, "c_bmask_b"], [pk])

                                def pv(o_=o_, ok=ok, pt=pt, pk=pk, kt=kt, hl=hl, q0=q0, last=last, rc=rc, rk=rk, yc=yc, yk=yk):
                                    mm(o_[:, q0:512], V4[:, kt, hl * 128:(hl + 1) * 128], pt[:, q0:512], kt == 0, kt == last, ["V4", pk], [ok])
                                    if kt == last:
                                        recip(rc[64:128, :], o_[64:128, :], [ok], [rk])
                                        tt("dve", yc[(hl % 2) * 64:(hl % 2) * 64 + 64, hl // 2, :], o_[0:64, :], rc[64:128, :], ALU.mult,
                                           [ok, rk], [yk])
                                pend.append(pv)
                                flush(LA)
                        flush(0)
                        r0 = 1024 + hg * 256
                        dma(YT_d[r0:r0 + 256, G * 512:(G + 1) * 512].rearrange("(c p) s -> p c s", p=128), yc[:], [yk],
                            [f"YTc{hg}_{G}"], q="pool")
            P.barrier()

        def phase5a(l):
            st = contextlib.ExitStack()
            with st:
                stg = [SB(st, f"o_stg{i}", [128, 2048]) for i in range(4)]
                Wg = SB(st, "o_Wg", [128, 8, 3072], BF16)
                Wbr = SB(st, "o_Wbr", [128, 12, 1024], BF16)
                Wo = SB(st, "o_Wo", [128, 8, 1024], BF16)
                hT = SB(st, "o_hT", [128, 8, 512], BF16)
                YT = SB(st, "o_YT", [128, 12, 512], BF16)
                xg = SB(st, "o_xg", [128, 8, 512])
                eg = [SB(st, f"o_eg{i}", [128, 512]) for i in range(3)]
                mtb = [SB(st, f"o_mtb{i}", [128, 512]) for i in range(2)]
                maccs = [SB(st, f"o_macc{i}", [128, 512]) for i in range(2)]
                mT = SB(st, "o_mT", [128, 8, 512], BF16)
                pg = [PS(st, f"o_pg{i}") for i in range(3)]
                pu = [PS(st, f"o_pu{i}") for i in range(3)]
                pd = [PS(st, f"o_pd{i}") for i in range(2)]
                win = Wd["w_in"][l].rearrange("(c p) n -> p c n", p=128)
                for kc in range(8):
                    load_cast(stg, Wg[:, kc, 0:2048], win[:, kc, 4000:6048], 2048, dstkey="Wg")
                    load_cast(stg, Wg[:, kc, 2048:3072], win[:, kc, 6048:7072], 1024, dstkey="Wg")
                wbr = Wd["w_branch"][l].rearrange("n (c p) d -> p (n c) d", p=128)
                for c in range(12):
                    load_cast(stg, Wbr[:, c, :], wbr[:, c, :], 1024, dstkey="Wbr")
                wo = Wd["w_out"][l].rearrange("(c p) n -> p c n", p=128)
                for kc in range(8):
                    load_cast(stg, Wo[:, kc, :], wo[:, kc, :], 1024, dstkey="Wo")
                kg = 0
                for g in range(NG):
                    tok = slice(g * 512, (g + 1) * 512)
                    dma(hT[:], hT_d.rearrange("(c p) s -> p c s", p=128)[:, :, tok], [f"hTd{g}"], ["hT"])
                    dma(YT[:], YT_d[:, tok].rearrange("(c p) s -> p c s", p=128),
                        [f"YTa{g}", f"YTb{g}", f"YTc0_{g}", f"YTc1_{g}"], ["YT"])
                    dma(xg[:], xT_d.rearrange("(c p) s -> p c s", p=128)[:, :, tok], [f"xT{g}"], ["xg"])
                    for dc in range(8):
                        macc = maccs[dc % 2]; mck = f"macc{dc % 2}"
                        for n_ in range(3):
                            bg = pg[kg % 3]; bgk = f"pg{kg % 3}"
                            bu = pu[kg % 3]; buk = f"pu{kg % 3}"
                            e_ = eg[kg % 3]; ek = f"eg{kg % 3}"
                            mt_ = mtb[kg % 2]; mk = f"mtb{kg % 2}"; kg += 1
                            col = n_ * 1024 + dc * 128
                            for kc in range(8):
                                mm(bg[:, :], Wg[:, kc, col:col + 128], hT[:, kc, :], kc == 0, kc == 7, ["Wg", "hT"], [bgk])
                            for c4 in range(4):
                                mm(bu[:, :], Wbr[:, n_ * 4 + c4, dc * 128:(dc + 1) * 128], YT[:, n_ * 4 + c4, :], c4 == 0, c4 == 3,
                                   ["Wbr", "YT"], [buk])
                            actf(e_[:], bg[:, :], AF.Sigmoid, [bgk], [ek])
                            if n_ == 0:
                                tt("dve", macc[:], bu[:, :], e_[:], ALU.mult, [buk, ek], [mck])
                            else:
                                tt("dve", mt_[:], bu[:, :], e_[:], ALU.mult, [buk, ek], [mk])
                                if n_ == 1:
                                    tt("dve", macc[:], macc[:], mt_[:], ALU.add, [mck, mk], [mck])
                                else:
                                    tt("dve", mT[:, dc, :], macc[:], mt_[:], ALU.add, [mck, mk], [f"mT{dc}"])
                    for dc in range(8):
                        b = pd[dc % 2]; bk = f"pd{dc % 2}"
                        for kc in range(8):
                            mm(b[:, :], Wo[:, kc, dc * 128:(dc + 1) * 128], mT[:, kc, :], kc == 0, kc == 7, ["Wo", f"mT{kc}"], [bk])
                        tt("dve", xg[:, dc, :], b[:, :], xg[:, dc, :], ALU.add, [bk, "xg"], ["xg"])
                    dma(xT_d.rearrange("(c p) s -> p c s", p=128)[:, :, tok], xg[:], ["xg"], [f"xT{g}"], q="pool")
            P.barrier()

        def phase5b(l, final):
            st = contextlib.ExitStack()
            GT = 512
            with st:
                xg = SB(st, "f_xg", [128, 8, GT])
                xflat = xg[:].rearrange("p c s -> p (c s)")
                stg = [xflat[:, 0:2048], xflat[:, 2048:4096]]
                XK = ["stg0", "stg1"]
                W1 = SB(st, "f_W1", [128, 8, 4096], BF16)
                W2 = SB(st, "f_W2", [128, 32, 1024], BF16)
                g2 = SB(st, "f_g2", [128, 8]); gf = SB(st, "f_gf", [128, 8])
                hT = SB(st, "f_hT", [128, 8, GT], BF16)
                rstd = SB(st, "f_rstd", [128, GT]); lnt = SB(st, "f_lnt", [128, GT])
                rl = [SB(st, f"f_rl{i}", [128, GT]) for i in range(2)]
                aT = SB(st, "f_aT", [128, 32, GT], BF16)
                sq = aT[:, 0:8, :]
                SQK = [f"aT{i}" for i in range(8)]
                aflat = aT[:].rearrange("p c s -> p (c s)").bitcast(F32)
                yo = aflat[:, 0:4096].rearrange("p (c s) -> p c s", s=GT)
                ot = aflat[:, 4096:8192].rearrange("p (t d) -> p t d", d=D)
                YOK = [f"aT{i}" for i in range(16)]; OTK = [f"aT{i}" for i in range(16, 32)]
                pss = PS(st, "f_pss")
                ph = [PS(st, f"f_ph{i}") for i in range(2)]
                po = [PS(st, f"f_po{i}") for i in range(2)]
                ptr = [PS(st, f"f_ptr{i}") for i in range(2)]
                w1 = Wd["w_ff1"][l].rearrange("(c p) n -> p c n", p=128)
                for kc in range(8):
                    for hh in range(2):
                        load_cast(stg, W1[:, kc, hh * 2048:(hh + 1) * 2048], w1[:, kc, hh * 2048:(hh + 1) * 2048], 2048, dstkey="W1")
                w2 = Wd["w_ff2"][l].rearrange("(c p) n -> p c n", p=128)
                for fc in range(0, 32, 2):
                    load_cast(stg, W2[:, fc:fc + 2, :].rearrange("p a b -> p (a b)"), w2[:, fc:fc + 2, :], 2048, dstkey="W2")
                dma(g2[:], Wd["norm_ffn_g"][l].rearrange("(c p) -> p c", p=128), [], ["g2"], nc_ok=True)
                dma(gf[:], Wd["final_norm_g"].rearrange("(c p) -> p c", p=128), [], ["gf"], nc_ok=True)
                kk_ = 0; tk = 0
                for sg in range(S // GT):
                    g = sg
                    tok = slice(sg * GT, (sg + 1) * GT)
                    dma(xg[:], xT_d.rearrange("(c p) s -> p c s", p=128)[:, :, tok], [f"xT{g}"] + XK, XK)
                    actf(sq.rearrange("p c s -> p (c s)") if False else sq, xg[:], AF.Square, XK, SQK)
                    for c in range(8):
                        mm(pss[:, 0:GT], C["ones_b"][:], sq[:, c, :], c == 0, c == 7, SQK + ["c_ones_b"], ["pss"])
                    rstd_from(pss[:, 0:GT], rstd[:], lnt[:], 1.0 / D, 128, ["pss"], ["rstd"], "lnt")
                    for c in range(8):
                        stt(hT[:, c, :], xg[:, c, :], g2[:, c:c + 1], rstd[:], ALU.mult, ALU.mult, XK + ["g2", "rstd"], ["hT"])
                    for fc in range(32):
                        b = ph[kk_ % 2]; bk = f"ph{kk_ % 2}"
                        r_ = rl[kk_ % 2]; rk = f"rl{kk_ % 2}"; kk_ += 1
                        for kc in range(8):
                            mm(b[:, 0:GT], W1[:, kc, fc * 128:(fc + 1) * 128], hT[:, kc, :], kc == 0, kc == 7, ["W1", "hT"], [bk])
                        actf(r_[:], b[:, 0:GT], AF.Relu, [bk], [rk])
                        tt("dve", aT[:, fc, :], r_[:], r_[:], ALU.mult, [rk], [f"aT{fc}"])
                    for dc in range(8):
                        b = po[dc % 2]; bk = f"po{dc % 2}"
                        for fc in range(32):
                            mm(b[:, 0:GT], W2[:, fc, dc * 128:(dc + 1) * 128], aT[:, fc, :], fc == 0, fc == 31, ["W2", f"aT{fc}"], [bk])
                        tt("dve", xg[:, dc, :], b[:, 0:GT], xg[:, dc, :], ALU.add, [bk] + XK, XK)
                    if not final:
                        dma(xT_d.rearrange("(c p) s -> p c s", p=128)[:, :, tok], xg[:], XK, [f"xT{g}"], q="pool")
                    else:
                        actf(sq, xg[:], AF.Square, XK, SQK)
                        for c in range(8):
                            mm(pss[:, 0:GT], C["ones_b"][:], sq[:, c, :], c == 0, c == 7, SQK + ["c_ones_b"], ["pss"])
                        rstd_from(pss[:, 0:GT], rstd[:], lnt[:], 1.0 / D, 128, ["pss"], ["rstd"], "lnt")
                        for c in range(8):
                            stt(yo[:, c, :], xg[:, c, :], gf[:, c:c + 1], rstd[:], ALU.mult, ALU.mult, XK + ["gf", "rstd"] + SQK, YOK)
                        for t in range(GT // 128):
                            for half in range(2):
                                b = ptr[tk % 2]; bk = f"ptr{tk % 2}"; tk += 1
                                for c4 in range(4):
                                    c = half * 4 + c4
                                    P.pe(lambda e, o=b[:, c4 * 128:(c4 + 1) * 128], i=yo[:, c, t * 128:(t + 1) * 128]:
                                         e.transpose(o, i, C["ident_f"][:]), YOK + ["c_ident_f"], [bk])
                                copy_op(ev_eng(), ot[:, t, half * 512:(half + 1) * 512], b[:, :], [bk], OTK)
                        dma(out_d[tok, :].rearrange("(t p) d -> p t d", p=128), ot, OTK, [f"out{sg}"], q="pool")
            P.barrier()

        ph_all = phases or ["0", "1a", "1b", "4", "5a", "5b"]
        if "0" in ph_all:
            phase0()
        for l in range(depth):
            if "1a" in ph_all: phase1a(l)
            if "1b" in ph_all: phase1b(l)
            if "4" in ph_all: phase4(l)
            if "5a" in ph_all: phase5a(l)
            if "5b" in ph_all: phase5b(l, final=(l == depth - 1))
        P.emit()
    return nc, P


_CACHE = {}


def _get_prog(S):
    if S not in _CACHE:
        _CACHE[S] = build(S)
    return _CACHE[S]


def run_cores(xs, poss, weights, S):
    nc, _ = _get_prog(S)
    consts = make_consts()
    in_maps = []
    for x, p in zip(xs, poss):
        m = {"x": np.ascontiguousarray(x, dtype=np.float32), "positions": np.ascontiguousarray(p, dtype=np.int32)}
        for k in WEIGHT_SPECS:
            m[k] = np.ascontiguousarray(weights[k], dtype=np.float32)
        m.update(consts)
        in_maps.append(m)
    res = run_bass_kernel_spmd(nc, in_maps, core_ids=list(range(len(xs))))
    return [r["out"] for r in res.results]


def kernel(**inputs):
    x = np.asarray(inputs["x"])
    pos = np.asarray(inputs["positions"])
    B, S, _ = x.shape
    weights = {k: np.asarray(inputs[k]) for k in WEIGHT_SPECS}
    ncore = 8
    xs = [x[i % B] for i in range(ncore)]
    ps = [pos[i % B][None, :] for i in range(ncore)]
    outs = run_cores(xs, ps, weights, S)
    return np.stack([outs[b] for b in range(B)], axis=0).astype(np.float32)
```

```python
import contextlib
import math
import numpy as np
import ml_dtypes
import concourse.bass as bass
import concourse.mybir as mybir
from concourse.bass_utils import run_bass_kernel_spmd

F32 = mybir.dt.float32
BF16 = mybir.dt.bfloat16
I32 = mybir.dt.int32
AF = mybir.ActivationFunctionType
ALU = mybir.AluOpType

D = 1024
KC = 8
DEPTH = 2
IN_COLS = 7072
EPS = 1e-6
SEM_CHUNK = 30000
NDMA_SEM = 12
NEG = -30000.0


class Prog:
    ENGS = ("pe", "act", "dve", "pool", "sp")

    def __init__(self, nc):
        self.nc = nc
        self.ops = []
        self.sync_same = {"act", "dve", "pool"}
        self.barriers = []

    def op(self, eng, fn, reads=(), writes=(), dma=False):
        self.ops.append((eng, fn, tuple(reads), tuple(writes), dma))

    def pe(self, fn, r=(), w=()): self.op("pe", fn, r, w)
    def act(self, fn, r=(), w=()): self.op("act", fn, r, w)
    def dve(self, fn, r=(), w=()): self.op("dve", fn, r, w)
    def pool(self, fn, r=(), w=()): self.op("pool", fn, r, w)
    def dma(self, fn, r=(), w=(), q="sp"): self.op(q, fn, r, w, True)

    def barrier(self):
        self.barriers.append(len(self.ops))

    def emit(self):
        nc = self.nc
        ops = self.ops
        n = len(ops)
        last_w = {}
        readers = {}
        deps = [None] * n
        bset = sorted(set(self.barriers))
        bi = 0
        pending_bar = {}
        last_op = {}
        recent_dma = {e: [] for e in self.ENGS}
        for i, (eng, fn, rd, wr, dma) in enumerate(ops):
            while bi < len(bset) and bset[bi] <= i:
                s = set(last_op.values())
                for q in self.ENGS:
                    s |= set(recent_dma[q][-NDMA_SEM:])
                for e in self.ENGS:
                    pending_bar[e] = set(pending_bar.get(e, set())) | s
                last_w.clear(); readers.clear()
                bi += 1
            d = set()
            for k in rd:
                if k in last_w: d.add(last_w[k])
            for k in wr:
                if k in last_w: d.add(last_w[k])
                for r in readers.get(k, ()): d.add(r)
            if eng in pending_bar:
                d |= pending_bar.pop(eng)
            d.discard(i)
            deps[i] = d
            for k in rd:
                lst = readers.setdefault(k, [])
                if not dma:
                    lst[:] = [r_ for r_ in lst if ops[r_][4] or ops[r_][0] != eng]
                lst.append(i)
            for k in wr:
                last_w[k] = i
                readers[k] = []
            if dma:
                recent_dma[eng].append(i)
            else:
                last_op[eng] = i
        needed = set()
        for i in range(n):
            eng = ops[i][0]
            nd = set()
            for p in deps[i]:
                peng, _, _, _, pdma = ops[p]
                if (not pdma) and peng == eng and eng not in self.sync_same and not ops[i][4]:
                    continue
                nd.add(p)
            deps[i] = nd
            needed |= nd
        cnt = {e: 0 for e in self.ENGS}
        dcnt = {e: 0 for e in self.ENGS}
        sig = {}
        for i, (eng, fn, rd, wr, dma) in enumerate(ops):
            if dma:
                sig[i] = ("d", eng, dcnt[eng]); dcnt[eng] += 1
            elif i in needed:
                cnt[eng] += 1; sig[i] = ("c", eng, cnt[eng])
        stack = contextlib.ExitStack()
        csem = {}
        for e in self.ENGS:
            k = (cnt[e] + SEM_CHUNK - 1) // SEM_CHUNK
            csem[e] = [stack.enter_context(nc.semaphore(f"c_{e}_{j}")) for j in range(k)]
        dsem = {}
        for e in self.ENGS:
            if dcnt[e]:
                dsem[e] = [stack.enter_context(nc.semaphore(f"d_{e}_{j}"))
                           for j in range(min(NDMA_SEM, dcnt[e]))]
        per_eng = {e: [i for i in range(n) if ops[i][0] == e] for e in self.ENGS}
        engobj = {"pe": "tensor", "act": "scalar", "dve": "vector", "pool": "gpsimd", "sp": "sync"}
        self.stats = {"ops": {e: len(per_eng[e]) for e in self.ENGS}, "sig": cnt, "dma": dcnt}

        def target(p):
            s = sig[p]
            if s[0] == "c":
                _, e, v = s
                j = (v - 1) // SEM_CHUNK
                return (("c", e, j), csem[e][j], v - j * SEM_CHUNK)
            _, q, idx = s
            ns = len(dsem[q])
            return (("d", q, idx % ns), dsem[q][idx % ns], 16 * (idx // ns + 1))

        def body(e):
            def run(engine):
                waited = {}
                for i in per_eng[e]:
                    _, fn, rd, wr, dma = ops[i]
                    tg = [target(p) for p in deps[i]]
                    if dma:
                        _, q, idx = sig[i]
                        ns = len(dsem[q])
                        if idx >= ns:
                            tg.append((("d", q, idx % ns), dsem[q][idx % ns], 16 * (idx // ns)))
                    best = {}
                    for key, sem, val in tg:
                        if val > waited.get(key, 0) and val > best.get(key, (None, 0))[1]:
                            best[key] = (sem, val)
                    for key, (sem, val) in best.items():
                        engine.wait_ge(sem, val)
                        waited[key] = val
                    ins = fn(engine)
                    if i in sig:
                        key, sem, val = target(i)
                        ins.then_inc(sem, 16 if dma else 1)
                if e == "sp":
                    for q in self.ENGS:
                        if dcnt[q]:
                            ns = len(dsem[q])
                            for s_ in range(ns):
                                if dcnt[q] > s_:
                                    engine.wait_ge(dsem[q][s_], 16 * ((dcnt[q] - 1 - s_) // ns + 1))
            return run

        with stack:
            with nc.Block() as block:
                for e in self.ENGS:
                    if per_eng[e] or e == "sp":
                        getattr(block, engobj[e])(body(e))
        return self


def make_consts():
    bf = ml_dtypes.bfloat16
    c = {}
    c["ident_f"] = np.eye(128, dtype=np.float32)
    c["ident_b"] = np.eye(128, dtype=np.float32).astype(bf)
    c["ones_b"] = np.ones((128, 128), np.float32).astype(bf)
    bd = np.zeros((128, 128), np.float32)
    bd[:64, :64] = 1.0; bd[64:, 64:] = 1.0
    c["bd_b"] = bd.astype(bf)
    c["bd_f"] = bd
    j = np.arange(128)[:, None] % 64
    i = np.arange(128)[None, :] % 64
    c["cm_f"] = (j <= i).astype(np.float32)
    c["cm4_f"] = np.tile(c["cm_f"][:, 0:64], (1, 4)).astype(np.float32)
    c["bd4_f"] = np.tile(bd, (1, 4)).astype(np.float32)
    ma = np.zeros((128, 640), np.float32)
    kh = (np.arange(128) >= 64)[:, None]
    qh = (np.arange(128) >= 64)[None, :]
    ma[:, 0:128][(~kh) & qh] = NEG
    ma[:, 512:640][kh & (~qh)] = NEG
    c["maskadd"] = ma
    bm = np.ones((128, 128), np.float32)
    bm[64:, :64] = 0.0
    c["bmask_b"] = bm.astype(bf)
    sm = np.ones((128, 512), np.float32)
    sm[:, ::64] = 0.0
    c["scanmask"] = sm
    invf = np.zeros((32, 1), np.float32)
    fr = (10000.0 ** (-np.arange(0, 32, 2, dtype=np.float32) / 32.0)).astype(np.float32)
    invf[0:16, 0] = fr; invf[16:32, 0] = fr
    c["invf"] = invf
    ika = np.zeros((32, 96), np.float32)
    ikb = np.zeros((32, 96), np.float32)
    for t in range(32):
        ika[t, 64 + t] = 1.0
    for t in range(16):
        ikb[16 + t, 64 + t] = -1.0
        ikb[t, 80 + t] = 1.0
    c["ika_b"] = ika.astype(bf)
    c["ikb_b"] = ikb.astype(bf)
    c["ones_col_b"] = np.ones((128, 2), np.float32).astype(bf)
    return c


CONST_SPECS = {
    "ident_f": ([128, 128], F32), "ident_b": ([128, 128], BF16), "ones_b": ([128, 128], BF16),
    "bd_b": ([128, 128], BF16), "bd_f": ([128, 128], F32), "cm_f": ([128, 128], F32),
    "cm4_f": ([128, 256], F32), "bd4_f": ([128, 512], F32),
    "maskadd": ([128, 640], F32), "bmask_b": ([128, 128], BF16), "scanmask": ([128, 512], F32),
    "invf": ([32, 1], F32), "ika_b": ([32, 96], BF16), "ikb_b": ([32, 96], BF16),
    "ones_col_b": ([128, 2], BF16),
}

WEIGHT_SPECS = {
    "norm_mix_g": [DEPTH, 1024], "w_in": [DEPTH, 1024, IN_COLS], "rel_bias": [DEPTH, 8, 257],
    "hgrn_lb_logits": [DEPTH, 512], "hgrn_norm_g": [DEPTH, 64], "mla_q_norm_g": [DEPTH, 256],
    "mla_kv_norm_g": [DEPTH, 128], "mla_w_uq": [DEPTH, 256, 768], "mla_w_ukv": [DEPTH, 128, 1024],
    "w_branch": [DEPTH, 3, 512, 1024], "w_out": [DEPTH, 1024, 1024], "norm_ffn_g": [DEPTH, 1024],
    "w_ff1": [DEPTH, 1024, 4096], "w_ff2": [DEPTH, 4096, 1024], "final_norm_g": [1024],
}


def build(S, depth=DEPTH, phases=None, dbg=False):
    assert S % 512 == 0
    NG = S // 512
    NT = S // 128
    nc = bass.Bass("TRN2", target_bir_lowering=False)
    x_in = nc.dram_tensor("x", [S, D], F32, kind="ExternalInput").ap()
    pos_in = nc.dram_tensor("positions", [1, S], I32, kind="ExternalInput").ap()
    Wd = {k: nc.dram_tensor(k, shp, F32, kind="ExternalInput").ap() for k, shp in WEIGHT_SPECS.items()}
    Cd = {k: nc.dram_tensor(k, shp, dt, kind="ExternalInput").ap() for k, (shp, dt) in CONST_SPECS.items()}
    out_d = nc.dram_tensor("out", [S, D], F32, kind="ExternalOutput").ap()
    dk = {"kind": "ExternalOutput"} if dbg else {}
    xT_d = nc.dram_tensor("xT_d", [D, S], F32, **dk).ap()
    hT_d = nc.dram_tensor("hT_d", [D, S], BF16, **dk).ap()
    YT_d = nc.dram_tensor("YT_d", [1536, S], BF16, **dk).ap()
    qTc_d = nc.dram_tensor("qTc_d", [8, 96, S], BF16, **dk).ap()
    kTc_d = nc.dram_tensor("kTc_d", [8, 96, S], BF16, **dk).ap()
    Vc_d = nc.dram_tensor("Vc_d", [S, 1024], BF16, **dk).ap()
    cos_d = nc.dram_tensor("cos_d", [96, S], F32, **dk).ap()
    sin_d = nc.dram_tensor("sin_d", [96, S], F32, **dk).ap()
    tt_d = nc.dram_tensor("tt_d", [8, 768], F32).ap()
    DD_d = nc.dram_tensor("DD_d", [8, 128, 768], F32).ap()

    P = Prog(nc)
    rr = {"i": 0}

    def ev_eng():
        rr["i"] += 1
        return "dve" if rr["i"] % 2 else "act"

    def copy_op(eng, out, in_, r, w):
        if eng == "act":
            P.act(lambda e, o=out, i=in_: e.activation(o, i, AF.Copy), r, w)
        elif eng == "dve":
            P.dve(lambda e, o=out, i=in_: e.tensor_copy(o, i), r, w)
        else:
            P.pool(lambda e, o=out, i=in_: e.tensor_copy(o, i), r, w)

    def mm(out, lhsT, rhs, start, stop, r, w):
        P.pe(lambda e, o=out, l=lhsT, x=rhs, s=start, t=stop: e.matmul(o, l, x, start=s, stop=t), r, w)

    def tt(eng, out, a, b, op, r, w):
        f = lambda e, o=out, a_=a, b_=b, op_=op: e.tensor_tensor(o, a_, b_, op_)
        P.op(eng, f, r, w)

    def ts(eng, out, a, s1, s2, op0, op1, r, w):
        if op1 is None:
            f = lambda e, o=out, a_=a, s1_=s1, op0_=op0: e.tensor_scalar(o, a_, s1_, None, op0_)
        else:
            f = lambda e, o=out, a_=a, s1_=s1, s2_=s2, op0_=op0, op1_=op1: e.tensor_scalar(o, a_, s1_, s2_, op0_, op1_)
        P.op(eng, f, r, w)

    def stt(out, a, sc, b, op0, op1, r, w):
        P.dve(lambda e, o=out, a_=a, s_=sc, b_=b, o0=op0, o1=op1: e.scalar_tensor_tensor(o, a_, s_, b_, o0, o1), r, w)

    def actf(out, in_, func, r, w, scale=None, bias=None):
        kw = {}
        if scale is not None: kw["scale"] = scale
        if bias is not None: kw["bias"] = bias
        P.act(lambda e, o=out, i=in_, f=func, kw=kw: e.activation(o, i, f, **kw), r, w)

    def recip(out, in_, r, w):
        P.dve(lambda e, o=out, i=in_: e.reciprocal(o, i), r, w)

    def dma(out, in_, r, w, q="sp", nc_ok=False):
        if nc_ok:
            P.dma(lambda e, o=out, i=in_: e.dma_start(out=o, in_=i, allow_slow_non_contiguous=True), r, w, q)
        else:
            P.dma(lambda e, o=out, i=in_: e.dma_start(out=o, in_=i), r, w, q)

    glob = contextlib.ExitStack()
    with glob:
        uid = {"i": 0}

        def SB(stk, name, shape, dt=F32):
            uid["i"] += 1
            return stk.enter_context(nc.sbuf_tensor(f"{name}_{uid['i']}", shape, dt))

        def PS(stk, name, shape=(128, 512), dt=F32):
            uid["i"] += 1
            return stk.enter_context(nc.psum_tensor(f"{name}_{uid['i']}", list(shape), dt))

        C = {}
        for k, (shp, dt) in CONST_SPECS.items():
            C[k] = SB(glob, "c_" + k, shp, dt)
            dma(C[k][:], Cd[k][:, :] if len(shp) == 2 else Cd[k][:], [], ["c_" + k])
        CK = ["c_" + k for k in CONST_SPECS]
        epsc = SB(glob, "epsc", [128, 1])
        P.pool(lambda e: e.memset(epsc[:], EPS), [], ["epsc"])
        halfpi = SB(glob, "halfpi", [128, 1])
        P.pool(lambda e: e.memset(halfpi[:], math.pi / 2), [], ["halfpi"])

        def rstd_from(ps_ap, out_ap, tmp_ap, inv_n, np_, r, w, tmpk):
            actf(tmp_ap, ps_ap, AF.Ln, r + ["epsc"], [tmpk], scale=inv_n, bias=epsc[0:np_, :])
            actf(out_ap, tmp_ap, AF.Exp, [tmpk], w, scale=-0.5)

        stage_rr = {"i": 0}

        def phase0():
            st = contextlib.ExitStack()
            with st:
                xin = [SB(st, f"p0_xin{i}", [128, 4, D]) for i in range(2)]
                xo = [SB(st, f"p0_xo{i}", [128, 8, 512]) for i in range(2)]
                pb = [PS(st, f"p0_ps{i}") for i in range(4)]
                import os as _os
                k = 0
                for g in range(NG if _os.environ.get("P0MODE", "all") in ("all", "tr") else 0):
                    xi = xin[g % 2]; xoo = xo[g % 2]
                    dma(xi[:], x_in[g * 512:(g + 1) * 512, :].rearrange("(t p) d -> p t d", p=128),
                        [], [f"xin{g % 2}"])
                    for c in range(8):
                        b = pb[k % 4]; bk = f"p0ps{k % 4}"; k += 1
                        for t in range(4):
                            P.pe(lambda e, o=b[:, t * 128:(t + 1) * 128], i=xi[:, t, c * 128:(c + 1) * 128]:
                                 e.transpose(o, i, C["ident_f"][:]),
                                 [f"xin{g % 2}", "c_ident_f"], [bk])
                        copy_op(ev_eng(), xoo[:, c, :], b[:, :], [bk], [f"xo{g % 2}_{c}"])
                    dma(xT_d.rearrange("(c p) s -> p c s", p=128)[:, :, g * 512:(g + 1) * 512], xoo[:],
                        [f"xo{g % 2}_{c}" for c in range(8)], [f"xT{g}"], q="pool")
                CH = min(S, 2048)
                pi_ = SB(st, "p0_pi", [32, CH], I32)
                a = SB(st, "p0_a", [32, CH]); kf = SB(st, "p0_kf", [32, CH]); ki = SB(st, "p0_ki", [32, CH], I32)
                r_ = SB(st, "p0_r", [32, CH]); t_ = SB(st, "p0_t", [32, CH])
                co = SB(st, "p0_co", [32, CH]); si = SB(st, "p0_si", [32, CH])
                lo1 = SB(st, "p0_lo1", [64, CH]); lo0 = SB(st, "p0_lo0", [64, CH])
                P.pool(lambda e: e.memset(lo1[:], 1.0), [], ["lo1"])
                P.pool(lambda e: e.memset(lo0[:], 0.0), [], ["lo0"])
                C1 = 6.28125
                C2 = 2 * math.pi - 6.28125
                cut = int(_os.environ.get("P0CUT", "99"))
                for ch in range(S // CH if _os.environ.get("P0MODE", "all") in ("all", "cs") else 0):
                    src = bass.AP(tensor=pos_in.tensor, offset=ch * CH, ap=[[0, 32], [1, CH]])
                    dma(pi_[:], src, [], ["pi"])
                    seq = []
                    seq.append(lambda: P.dve(lambda e: e.tensor_copy(a[:], pi_[:]), ["pi"], ["a"]))
                    seq.append(lambda: ts("dve", a[:], a[:], C["invf"][:], None, ALU.mult, None, ["a", "c_invf"], ["a"]))
                    seq.append(lambda: ts("dve", kf[:], a[:], 1.0 / (2 * math.pi), None, ALU.mult, None, ["a"], ["kf"]))
                    seq.append(lambda: P.dve(lambda e: e.tensor_copy(ki[:], kf[:]), ["kf"], ["ki"]))
                    seq.append(lambda: P.dve(lambda e: e.tensor_copy(kf[:], ki[:]), ["ki"], ["kf"]))
                    seq.append(lambda: stt(r_[:], kf[:], -C1, a[:], ALU.mult, ALU.add, ["kf", "a"], ["r"]))
                    seq.append(lambda: stt(r_[:], kf[:], -C2, r_[:], ALU.mult, ALU.add, ["kf", "r"], ["r"]))
                    seq.append(lambda: ts("dve", t_[:], r_[:], -math.pi, 1e30, ALU.add, ALU.mult, ["r"], ["t"]))
                    seq.append(lambda: ts("dve", t_[:], t_[:], 0.0, 1.0, ALU.max, ALU.min, ["t"], ["t"]))
                    seq.append(lambda: stt(r_[:], t_[:], -2 * math.pi, r_[:], ALU.mult, ALU.add, ["t", "r"], ["r"]))
                    seq.append(lambda: ts("dve", t_[:], r_[:], math.pi, -1e30, ALU.add, ALU.mult, ["r"], ["t"]))
                    seq.append(lambda: ts("dve", t_[:], t_[:], 0.0, 1.0, ALU.max, ALU.min, ["t"], ["t"]))
                    seq.append(lambda: stt(r_[:], t_[:], 2 * math.pi, r_[:], ALU.mult, ALU.add, ["t", "r"], ["r"]))
                    seq.append(lambda: ts("dve", r_[:], r_[:], math.pi, -math.pi, ALU.min, ALU.max, ["r"], ["r"]))
                    seq.append(lambda: actf(si[:], r_[:], AF.Sin, ["r"], ["si"]))
                    seq.append(lambda: stt(t_[:], r_[:], -1.0, r_[:], ALU.mult, ALU.max, ["r"], ["t"]))
                    seq.append(lambda: actf(co[:], t_[:], AF.Sin, ["t", "halfpi"], ["co"], scale=-1.0, bias=halfpi[0:32, :]))
                    for f_ in seq[:cut]:
                        f_()
                    dma(cos_d[64:96, ch * CH:(ch + 1) * CH], co[:, :], ["co", "r", "a", "kf", "t"], [f"cos{ch}"], q="pool")
                    dma(sin_d[64:96, ch * CH:(ch + 1) * CH], si[:, :], ["si", "r", "a", "kf", "t"], [f"sin{ch}"], q="pool")
                    dma(cos_d[0:64, ch * CH:(ch + 1) * CH], lo1[:, :], ["lo1"], [f"cos{ch}"], q="pool")
                    dma(sin_d[0:64, ch * CH:(ch + 1) * CH], lo0[:, :], ["lo0"], [f"sin{ch}"], q="pool")
            P.barrier()

        def load_cast(stg, dst_ap, src_ap, ncols, rows=128, scale_ap=None, dstkey=None, neg=False):
            i = stage_rr["i"]; stage_rr["i"] += 1
            s = stg[i % len(stg)]; sk = f"stg{i % len(stg)}"
            if len(src_ap.shape) == 3:
                dst_stage = s[0:rows, 0:ncols].rearrange("p (a b) -> p a b", a=src_ap.shape[1])
            else:
                dst_stage = s[0:rows, 0:ncols]
            dma(dst_stage, src_ap, [], [sk])
            eng = ("dve", "act")[i % 2]
            if scale_ap is not None:
                ts("dve", dst_ap, s[0:rows, 0:ncols], scale_ap, None, ALU.mult, None, [sk, "wprep"], [dstkey])
            elif neg:
                ts("dve", dst_ap, s[0:rows, 0:ncols], -1.0, None, ALU.mult, None, [sk], [dstkey])
            else:
                copy_op(eng, dst_ap, s[0:rows, 0:ncols], [sk], [dstkey])

        def phase1a(l):
            st = contextlib.ExitStack()
            with st:
                stg = [SB(st, f"a_stg{i}", [128, 1536]) for i in range(2)]
                Wa = SB(st, "a_Wa", [128, 8, 1952], BF16)
                g1 = SB(st, "a_g1", [128, 8])
                gq = SB(st, "a_gq", [128, 2]); gkv = SB(st, "a_gkv", [128, 1])
                Wqa = SB(st, "a_Wqa", [128, 2, 768], BF16); Wqb = SB(st, "a_Wqb", [128, 2, 768], BF16)
                Wka = SB(st, "a_Wka", [128, 8, 96], BF16); Wv = SB(st, "a_Wv", [128, 512], BF16)
                BT = SB(st, "a_BT", [128, 8, 640], BF16)
                braw = SB(st, "a_braw", [128, 640])
                xg = SB(st, "a_xg", [128, 8, 512]); sq = SB(st, "a_sq", [128, 8, 512], BF16)
                hT = SB(st, "a_hT", [128, 8, 512], BF16)
                rstd = SB(st, "a_rstd", [128, 512]); lnt = SB(st, "a_lnt", [128, 512])
                qTa = SB(st, "a_qTa", [128, 4, 512], BF16)
                kTa = [SB(st, f"a_kTa{i}", [128, 4, 512], BF16) for i in range(2)]
                Va = [SB(st, f"a_Va{i}", [128, 4, 8, 128], BF16) for i in range(2)]
                PT = [SB(st, f"a_PT{i}", [128, 640], BF16) for i in range(2)]
                recA = SB(st, "a_recA", [128, 512])
                yAT = SB(st, "a_yAT", [128, 4, 512], BF16)
                cqT = SB(st, "a_cqT", [128, 2, 512], BF16); ckvT = SB(st, "a_ckvT", [128, 512], BF16)
                ckrT = SB(st, "a_ckrT", [32, 512], BF16)
                sqq = SB(st, "a_sqq", [128, 2, 512], BF16); sqkv = SB(st, "a_sqkv", [128, 512], BF16)
                rq = SB(st, "a_rq", [128, 512]); rkv = SB(st, "a_rkv", [128, 512]); rkt = SB(st, "a_rkt", [128, 4])
                lnt2 = SB(st, "a_lnt2", [128, 512]); lnt3 = SB(st, "a_lnt3", [128, 4])
                cosg = SB(st, "a_cosg", [96, 512]); sing = SB(st, "a_sing", [96, 512])
                cxq = SB(st, "a_cxq", [96, 512]); sxq = SB(st, "a_sxq", [96, 512]); cxk = SB(st, "a_cxk", [96, 512])
                t1 = [SB(st, f"a_t1{i}", [96, 512]) for i in range(2)]
                t2 = [SB(st, f"a_t2{i}", [96, 512]) for i in range(2)]
                t2k = SB(st, "a_t2k", [96, 512])
                qst = SB(st, "a_qst", [96, 8, 512], BF16); kst = SB(st, "a_kst", [96, 8, 512], BF16)
                vst = SB(st, "a_vst", [128, 4, 8, 128], BF16)
                pp = [PS(st, f"a_pp{i}") for i in range(2)]
                pss = PS(st, "a_pss")
                psAs = [PS(st, f"a_psA{i}", (128, 1024)) for i in range(2)]
                pacc = PS(st, "a_pacc")
                pc = pp

                win = Wd["w_in"][l].rearrange("(c p) n -> p c n", p=128)
                for kc in range(8):
                    load_cast(stg, Wa[:, kc, 0:1536], win[:, kc, 0:1536], 1536, dstkey="Wa")
                    load_cast(stg, Wa[:, kc, 1536:1952], win[:, kc, 3584:4000], 416, dstkey="Wa")
                dma(g1[:], Wd["norm_mix_g"][l].rearrange("(c p) -> p c", p=128), [], ["wprep"], nc_ok=True)
                dma(gq[:], Wd["mla_q_norm_g"][l].rearrange("(c p) -> p c", p=128), [], ["wprep"], nc_ok=True)
                dma(gkv[:], Wd["mla_kv_norm_g"][l].rearrange("(c p) -> p c", p=128), [], ["wprep"], nc_ok=True)
                wuq = Wd["mla_w_uq"][l].rearrange("(c p) n -> p c n", p=128)
                P.pool(lambda e: e.memset(Wqb[:], 0.0), [], ["Wqb"])
                P.pool(lambda e: e.memset(Wka[:], 0.0), [], ["Wka"])
                for c2 in range(2):
                    load_cast(stg, Wqa[:, c2, :], wuq[:, c2, :], 768, scale_ap=gq[:, c2:c2 + 1], dstkey="Wqa")
                    wa3 = Wqa[:, c2, :].rearrange("p (h d) -> p h d", d=96)
                    wb3 = Wqb[:, c2, :].rearrange("p (h d) -> p h d", d=96)
                    ts("dve", wb3[:, :, 64:80], wa3[:, :, 80:96], -1.0, None, ALU.mult, None, ["Wqa"], ["Wqb"])
                    P.dve(lambda e, o=wb3[:, :, 80:96], i=wa3[:, :, 64:80]: e.tensor_copy(o, i), ["Wqa"], ["Wqb"])
                wukv = Wd["mla_w_ukv"][l]
                i_ = stage_rr["i"]; stage_rr["i"] += 1
                s_ = stg[i_ % 2]; sk_ = f"stg{i_ % 2}"
                dma(s_[:, 0:1024], wukv[:, :], [], [sk_])
                s3 = s_[:, 0:1024].rearrange("p (h d) -> p h d", d=128)
                ts("dve", Wka[:, :, 0:64], s3[:, :, 0:64], gkv[:, 0:1], None, ALU.mult, None, [sk_, "wprep", "Wka"], ["Wka"])
                ts("dve", Wv[:, :].rearrange("p (h d) -> p h d", d=64), s3[:, :, 64:128], gkv[:, 0:1], None,
                   ALU.mult, None, [sk_, "wprep"], ["Wv"])
                rb = Wd["rel_bias"][l]
                rbs = SB(st, "a_rbs", [8, 257]); ttsb = SB(st, "a_ttsb", [8, 768])
                dma(rbs[:], rb[:, :], [], ["rbs"])
                P.pool(lambda e: e.memset(ttsb[:], 0.0), [], ["ttsb"])
                P.dve(lambda e: e.tensor_copy(ttsb[:, 0:256], rbs[:, 1:257]), ["rbs", "ttsb"], ["ttsb"])
                ts("dve", ttsb[:, 256:767], ttsb[:, 256:767], rbs[:, 256:257], None, ALU.add, None, ["rbs", "ttsb"], ["ttsb"])
                dma(tt_d[:, :], ttsb[:], ["ttsb"], ["tt"])
                for h in range(8):
                    src = bass.AP(tensor=tt_d.tensor, offset=tt_d[h:h + 1, 0:1].offset, ap=[[0, 128], [1, 767]])
                    dma(DD_d[h, :, 0:767], src, ["tt"], ["DD"])
                for h in range(8):
                    src = bass.AP(tensor=DD_d.tensor, offset=DD_d[h, 0:1, 127:128].offset, ap=[[767, 128], [1, 640]])
                    dma(braw[:], src, ["DD"], ["braw"])
                    for j in range(5):
                        tt("dve", BT[:, h, j * 128:(j + 1) * 128], braw[:, (4 - j) * 128:(5 - j) * 128],
                           C["maskadd"][:, j * 128:(j + 1) * 128], ALU.add, ["braw", "c_maskadd"], ["BT"])
                for i in range(2):
                    P.pool(lambda e, v=Va[i]: e.memset(v[:], 1.0), [], [f"Va{i}"])
                P.pool(lambda e: e.memset(vst[:], 1.0), [], ["vst"])

                scale_c = 96.0 ** -0.5
                k = 0
                pi = 0
                for g in range(NG):
                    slot = g % 2
                    tok = slice(g * 512, (g + 1) * 512)
                    dma(xg[:], xT_d.rearrange("(c p) s -> p c s", p=128)[:, :, tok], [f"xT{g}"], ["xg"])
                    dma(cosg[:], cos_d[:, tok], [f"cos{(g * 512) // min(S, 2048)}"], ["cosg"])
                    dma(sing[:], sin_d[:, tok], [f"sin{(g * 512) // min(S, 2048)}"], ["sing"])
                    actf(sq[:].rearrange("p c s -> p (c s)"), xg[:].rearrange("p c s -> p (c s)"), AF.Square, ["xg"], ["sq"])
                    for c in range(8):
                        mm(pss[:, :], C["ones_b"][:], sq[:, c, :], c == 0, c == 7, ["sq", "c_ones_b"], ["pss"])
                    rstd_from(pss[:, :], rstd[:], lnt[:], 1.0 / D, 128, ["pss"], ["rstd"], "lnt")
                    for c in range(8):
                        stt(hT[:, c, :], xg[:, c, :], g1[:, c:c + 1], rstd[:], ALU.mult, ALU.mult,
                            ["xg", "wprep", "rstd"], [f"hT{c}"])
                    hk = [f"hT{c}" for c in range(8)]
                    dma(hT_d.rearrange("(c p) s -> p c s", p=128)[:, :, tok], hT[:], hk, [f"hTd{g}"], q="pool")

                    def proj_fm(col0, ncol, M=128):
                        nonlocal k
                        b = pp[k % 2]; bk = f"pp{k % 2}"; k += 1
                        for kc in range(8):
                            mm(b[0:M, :], Wa[:, kc, col0:col0 + ncol], hT[:, kc, :], kc == 0, kc == 7, ["Wa", f"hT{kc}"], [bk])
                        return b, bk
                    for c in range(4):
                        b, bk = proj_fm(c * 128, 128)
                        ts("dve", qTa[:, c, :], b[:, :], 0.125, None, ALU.mult, None, [bk], ["qTa"])
                        b, bk = proj_fm(512 + c * 128, 128)
                        copy_op("act", kTa[slot][:, c, :], b[:, :], [bk], [f"kTa{slot}"])
                    for t in range(4):
                        b = pp[k % 2]; bk = f"pp{k % 2}"; k += 1
                        for kc in range(8):
                            mm(b[:, :], hT[:, kc, t * 128:(t + 1) * 128], Wa[:, kc, 1024:1536], kc == 0, kc == 7,
                               ["Wa", f"hT{kc}"], [bk])
                        copy_op(ev_eng(), Va[slot][:, t, :, 0:64], b[:, :].rearrange("p (h d) -> p h d", d=64), [bk], [f"Va{slot}"])
                    pendA = []
                    import os as _os2
                    for ml in range(0 if _os2.environ.get("SKIP_ATT") else 4):
                        m = 4 * g + ml
                        j0 = max(0, 4 - m)
                        for h in range(8):
                            c = h // 2; hp = h % 2
                            ps_ = slice(hp * 64, hp * 64 + 64)
                            pA = psAs[pi % 2]; pAk = f"psA{pi % 2}"
                            pt = PT[pi % 2]; ptk = f"PT{pi % 2}"; pi += 1
                            if j0 < 4:
                                mm(pA[:, j0 * 128:512], C["ident_b"][:], BT[:, h, j0 * 128:512], True, False, ["BT", "c_ident_b"], [pAk])
                            mm(pA[:, 512:640], C["ident_b"][:], BT[:, h, 512:640], True, False, ["BT", "c_ident_b"], [pAk])
                            for j in range(j0, 5):
                                kt = m - 4 + j
                                ks = (kt // 4) % 2; ktl = kt % 4
                                mm(pA[:, j * 128:(j + 1) * 128], kTa[ks][ps_, c, ktl * 128:(ktl + 1) * 128],
                                   qTa[ps_, c, ml * 128:(ml + 1) * 128], False, True,
                                   [f"kTa{ks}", "qTa"], [pAk])
                            actf(pt[:, j0 * 128:640], pA[:, j0 * 128:640], AF.Exp, [pAk], [ptk])

                            def pvA(m=m, ml=ml, h=h, j0=j0, pt=pt, ptk=ptk):
                                hc = h % 4
                                for j in range(j0, 5):
                                    kt = m - 4 + j
                                    ks = (kt // 4) % 2; ktl = kt % 4
                                    mm(pacc[:, hc * 128:(hc + 1) * 128], Va[ks][:, ktl, h, :], pt[:, j * 128:(j + 1) * 128],
                                       j == j0, j == 4, [f"Va{ks}", ptk], ["pacc"])
                                if hc == 3:
                                    recip(recA[64:128, :], pacc[64:128, :], ["pacc"], ["recA"])
                                    for h2 in range(h - 3, h + 1):
                                        c2 = h2 // 2; hp2 = h2 % 2; hc2 = h2 % 4
                                        tt("dve", yAT[hp2 * 64:hp2 * 64 + 64, c2, ml * 128:(ml + 1) * 128],
                                           pacc[0:64, hc2 * 128:(hc2 + 1) * 128], recA[64:128, hc2 * 128:(hc2 + 1) * 128],
                                           ALU.mult, ["pacc", "recA"], ["yAT"])
                            pendA.append(pvA)
                            while len(pendA) > 1:
                                pendA.pop(0)()
                    while pendA:
                        pendA.pop(0)()
                    dma(YT_d[0:512, tok].rearrange("(c p) s -> p c s", p=128), yAT[:], ["yAT"], [f"YTa{g}"], q="pool")

                    if _os2.environ.get("SKIP_C"):
                        continue
                    for c2 in range(2):
                        b, bk = proj_fm(1536 + c2 * 128, 128)
                        copy_op("act", cqT[:, c2, :], b[:, :], [bk], ["cqT"])
                        actf(sqq[:, c2, :], b[:, :], AF.Square, [bk], ["sqq"])
                    b, bk = proj_fm(1792, 128)
                    copy_op("act", ckvT[:, :], b[:, :], [bk], ["ckvT"])
                    actf(sqkv[:, :], b[:, :], AF.Square, [bk], ["sqkv"])
                    b, bk = proj_fm(1920, 32, M=32)
                    copy_op("act", ckrT[:, :], b[0:32, :], [bk], ["ckrT"])
                    for c2 in range(2):
                        mm(pss[:, :], C["ones_b"][:], sqq[:, c2, :], c2 == 0, c2 == 1, ["sqq", "c_ones_b"], ["pss"])
                    rstd_from(pss[:, :], rq[:], lnt2[:], 1.0 / 256, 128, ["pss"], ["rq"], "lnt2")
                    mm(pss[:, :], C["ones_b"][:], sqkv[:, :], True, True, ["sqkv", "c_ones_b"], ["pss"])
                    rstd_from(pss[:, :], rkv[:], lnt2[:], 1.0 / 128, 128, ["pss"], ["rkv"], "lnt2")
                    for t in range(4):
                        mm(pss[:, 2 * t:2 * t + 2], sqkv[:, t * 128:(t + 1) * 128], C["ones_col_b"][:], True, True,
                           ["sqkv", "c_ones_col_b"], ["pss"])
                    rstd_from(pss[:, 0:8].rearrange("p (t two) -> p t two", two=2)[:, :, 0:1].rearrange("p t o -> p (t o)"),
                              rkt[:], lnt3[:], 1.0 / 128, 128, ["pss"], ["rkt"], "lnt3")
                    stt(cxq[:], cosg[:], scale_c, rq[0:96, :], ALU.mult, ALU.mult, ["cosg", "rq"], ["cxq"])
                    stt(sxq[:], sing[:], scale_c, rq[0:96, :], ALU.mult, ALU.mult, ["sing", "rq"], ["sxq"])
                    copy_op("dve", cxk[0:64, :], rkv[0:64, :], ["rkv"], ["cxk_lo"])
                    dma(cxk[64:96, :], cos_d[64:96, tok], [f"cos{(g * 512) // min(S, 2048)}"], ["cxk_hi"])
                    cxkk = ["cxk_lo", "cxk_hi"]
                    bb = pc[0]
                    mm(bb[0:96, :], C["ikb_b"][:], ckrT[:, :], True, True, ["ckrT", "c_ikb_b"], ["pp0"])
                    tt("dve", t2k[:], bb[0:96, :], sing[:], ALU.mult, ["pp0", "sing"], ["t2k"])
                    for h in range(8):
                        ba = pc[0]; bb = pc[1]
                        for c2 in range(2):
                            mm(ba[0:96, :], Wqa[:, c2, h * 96:(h + 1) * 96], cqT[:, c2, :], c2 == 0, c2 == 1, ["Wqa", "cqT"], ["pp0"])
                        for c2 in range(2):
                            mm(bb[0:96, :], Wqb[:, c2, h * 96:(h + 1) * 96], cqT[:, c2, :], c2 == 0, c2 == 1, ["Wqb", "cqT"], ["pp1"])
                        ta = t1[h % 2]; tb = t2[h % 2]
                        tt("dve", ta[:], ba[0:96, :], cxq[:], ALU.mult, ["pp0", "cxq"], [f"t1{h % 2}"])
                        tt("dve", tb[:], bb[0:96, :], sxq[:], ALU.mult, ["pp1", "sxq"], [f"t2{h % 2}"])
                        tt("dve", qst[:, h, :], ta[:], tb[:], ALU.add, [f"t1{h % 2}", f"t2{h % 2}"], ["qst"])
                        b2 = pacc; b2k = "pacc"
                        mm(b2[0:96, :], Wka[:, h, :], ckvT[:, :], True, False, ["Wka", "ckvT"], [b2k])
                        mm(b2[0:96, :], C["ika_b"][:], ckrT[:, :], False, True, ["ckrT", "c_ika_b"], [b2k])
                        tt("dve", ta[:], b2[0:96, :], cxk[:], ALU.mult, [b2k] + cxkk, [f"t1{h % 2}"])
                        tt("dve", kst[:, h, :], ta[:], t2k[:], ALU.add, [f"t1{h % 2}", "t2k"], ["kst"])
                    for t in range(4):
                        b2 = pp[k % 2]; b2k = f"pp{k % 2}"; k += 1
                        mm(b2[:, :], ckvT[:, t * 128:(t + 1) * 128], Wv[:, :], True, True, ["ckvT", "Wv"], [b2k])
                        ts("dve", vst[:, t, :, 0:64], b2[:, :].rearrange("p (h d) -> p h d", d=64), rkt[:, t:t + 1], None,
                           ALU.mult, None, [b2k, "rkt"], ["vst"])
                    dma(qTc_d[:, :, tok].rearrange("h p s -> p h s"), qst[:], ["qst"], [f"qTc{g}"], q="pool")
                    dma(kTc_d[:, :, tok].rearrange("h p s -> p h s"), kst[:], ["kst"], [f"kTc{g}"], q="pool")
                    dma(Vc_d[tok, :].rearrange("(t p) f -> p t f", p=128), vst[:].rearrange("p t h d -> p t (h d)"),
                        ["vst"], [f"Vc{g}"], q="pool")
            P.barrier()

        def phase1b(l):
            import os as _os
            BCUT = int(_os.environ.get("BCUT", "9"))
            st = contextlib.ExitStack()
            with st:
                stg = [SB(st, f"b_stg{i}", [128, 2048]) for i in range(2)]
                Wb = SB(st, "b_Wb", [128, 8, 2048], BF16)
                lbl = SB(st, "b_lbl", [128, 2, 4]); lb = SB(st, "b_lb", [128, 4]); oml = SB(st, "b_oml", [128, 4]); noml = SB(st, "b_noml", [128, 4])
                gn = SB(st, "b_gn", [128, 1])
                hT = SB(st, "b_hT", [128, 8, 512], BF16)
                e1s = [SB(st, f"b_e1{i}", [128, 512]) for i in range(4)]; fs = [SB(st, f"b_f{i}", [128, 512]) for i in range(4)]
                kks = [SB(st, f"b_kk{i}", [128, 512]) for i in range(4)]
                lfs = [SB(st, f"b_lf{i}", [128, 512]) for i in range(4)]; cums = [SB(st, f"b_cum{i}", [128, 512]) for i in range(4)]
                E2s = [SB(st, f"b_E2{i}", [128, 512]) for i in range(4)]; tq2s = [SB(st, f"b_tq2{i}", [128, 512]) for i in range(4)]
                khTfs = [SB(st, f"b_khTf{i}", [128, 512]) for i in range(4)]
                E1 = [SB(st, f"b_E1{c}", [128, 512]) for c in range(4)]
                khT = [SB(st, f"b_khT{c}", [128, 512], BF16) for c in range(4)]
                qhT = [SB(st, f"b_qhT{c}", [128, 512], BF16) for c in range(4)]
                sgT = [SB(st, f"b_sgT{c}", [128, 512]) for c in range(4)]
                kh = SB(st, "b_kh", [128, 4, 512], BF16)
                vB = SB(st, "b_vB", [128, 4, 512], BF16)
                sall = SB(st, "b_sall", [128, 2, 256], BF16)
                tmall = SB(st, "b_tmall", [128, 512])
                stf = SB(st, "b_stf", [128, 4, 128]); stb = SB(st, "b_stb", [128, 4, 128], BF16)
                osqs = [SB(st, f"b_osq{i}", [128, 512], BF16) for i in range(4)]; ros = [SB(st, f"b_ro{i}", [128, 512]) for i in range(4)]
                lnts = [SB(st, f"b_lnt{i}", [128, 512]) for i in range(4)]; yts = [SB(st, f"b_yt{i}", [128, 512]) for i in range(4)]
                yBT = SB(st, "b_yBT", [128, 4, 512], BF16)
                pp = [PS(st, "b_pp0")]
                pp = [pp[0], pp[0]]
                pkv = PS(st, "b_pkv")
                po = [PS(st, f"b_po{i}") for i in range(4)]
                psc = PS(st, "b_psc")
                ptr = PS(st, "b_ptr")

                win = Wd["w_in"][l].rearrange("(c p) n -> p c n", p=128)
                for kc in range(8):
                    load_cast(stg, Wb[:, kc, :], win[:, kc, 1536:3584], 2048, dstkey="Wb")
                if l == 0:
                    P.pool(lambda e: e.memset(lb[:], 0.0), [], ["lb"])
                    P.pool(lambda e: e.memset(oml[:], 1.0), [], ["oml"])
                else:
                    dma(lbl[:, 0, :], Wd["hgrn_lb_logits"][0].rearrange("(c p) -> p c", p=128), [], ["lbl"], nc_ok=True)
                    dma(lbl[:, 1, :], Wd["hgrn_lb_logits"][1].rearrange("(c p) -> p c", p=128), [], ["lbl"], nc_ok=True)
                    tt("dve", lb[:], lbl[:, 0, :], lbl[:, 1, :], ALU.subtract, ["lbl"], ["lb"])
                    actf(lb[:], lb[:], AF.Exp, ["lb"], ["lb"])
                    ts("dve", lb[:], lb[:], 1.0, None, ALU.add, None, ["lb"], ["lb"])
                    recip(lb[:], lb[:], ["lb"], ["lb"])
                    ts("dve", oml[:], lb[:], -1.0, 1.0, ALU.mult, ALU.add, ["lb"], ["oml"])
                ts("dve", noml[:], oml[:], -1.0, None, ALU.mult, None, ["oml"], ["oml"])
                gsrc = Wd["hgrn_norm_g"][l:l + 1, :].rearrange("o d -> d o")
                dma(gn[0:64, :], gsrc, [], ["gn"], nc_ok=True)
                dma(gn[64:128, :], gsrc, [], ["gn"], nc_ok=True)
                P.pool(lambda e: e.memset(stf[:], 0.0), [], ["stf"])
                P.pool(lambda e: e.memset(stb[:], 0.0), [], ["stb"])
                k = 0
                for g in range(NG):
                    tok = slice(g * 512, (g + 1) * 512)
                    dma(hT[:], hT_d.rearrange("(c p) s -> p c s", p=128)[:, :, tok], [f"hTd{g}"], ["hT"])

                    def proj_fm(col0):
                        nonlocal k
                        b = pp[k % 2]; bk = "pp0"; k += 1
                        for kc in range(8):
                            mm(b[:, :], Wb[:, kc, col0:col0 + 128], hT[:, kc, :], kc == 0, kc == 7, ["Wb", "hT"], [bk])
                        return b, bk
                    for t in range(4):
                        b = pp[k % 2]; bk = "pp0"; k += 1
                        for kc in range(8):
                            mm(b[:, :], hT[:, kc, t * 128:(t + 1) * 128], Wb[:, kc, 1024:1536], kc == 0, kc == 7, ["Wb", "hT"], [bk])
                        copy_op(ev_eng(), vB[:, t, :], b[:, :], [bk], ["vB"])
                    T = lambda lst, c: lst[c]
                    for c in range(4):
                        b, bk = proj_fm(512 + c * 128)
                        actf(e1s[c][:], b[:, :], AF.Sigmoid, [bk], [f"e1{c}"])
                        bq, bqk = proj_fm(c * 128)
                        actf(tq2s[c][:], bq[:, :], AF.Silu, [bqk], [f"tq2{c}"])
                        bg_, bgk_ = proj_fm(1536 + c * 128)
                        actf(sgT[c][:], bg_[:, :], AF.Silu, [bgk_], [f"sgT{c}"])
                    for c in range(4):
                        ts("dve", fs[c][:], e1s[c][:], oml[:, c:c + 1], lb[:, c:c + 1], ALU.mult, ALU.add, [f"e1{c}", "oml", "lb"], [f"f{c}"])
                        ts("dve", kks[c][:], e1s[c][:], noml[:, c:c + 1], oml[:, c:c + 1], ALU.mult, ALU.add, [f"e1{c}", "oml"], [f"kk{c}"])
                    for c in range(4):
                        actf(lfs[c][:], fs[c][:], AF.Ln, [f"f{c}"], [f"lf{c}"])
                    for c in range(4):
                        P.dve(lambda e, cum=cums[c], lf=lfs[c]: e.tensor_tensor_scan(cum[:], C["scanmask"][:], lf[:], 0.0, ALU.mult, ALU.add),
                              [f"lf{c}", "c_scanmask"], [f"cum{c}"])
                    for c in range(4):
                        ts("dve", cums[c][:], cums[c][:], -80.0, None, ALU.max, None, [f"cum{c}"], [f"cum{c}"])
                    for c in range(4):
                        actf(E1[c][:], cums[c][:], AF.Exp, [f"cum{c}"], [f"E1{c}"])
                        actf(E2s[c][:], cums[c][:], AF.Exp, [f"cum{c}"], [f"E2{c}"], scale=-1.0)
                    for c in range(4):
                        tt("dve", khTfs[c][:], kks[c][:], E2s[c][:], ALU.mult, [f"kk{c}", f"E2{c}"], [f"khTf{c}"])
                        tt("dve", qhT[c][:], tq2s[c][:], E1[c][:], ALU.mult, [f"tq2{c}", f"E1{c}"], [f"qhT{c}"])
                    for c in range(4):
                        copy_op("act", khT[c][:], khTfs[c][:], [f"khTf{c}"], [f"khT{c}"])
                        for t in range(4):
                            P.pe(lambda e, o=ptr[:, t * 128:(t + 1) * 128], i=khTfs[c][:, t * 128:(t + 1) * 128]:
                                 e.transpose(o, i, C["ident_f"][:]), [f"khTf{c}", "c_ident_f"], ["ptr"])
                        copy_op("dve", kh[:, :, c * 128:(c + 1) * 128],
                                ptr[:, 0:512].rearrange("p (t d) -> p t d", d=128), ["ptr"], ["kh"])
                    for ch in range(8):
                        t = ch // 2; hf = ch % 2
                        rows = slice(hf * 64, hf * 64 + 64)
                        cols = slice(ch * 64, ch * 64 + 64)
                        for c in range(4):
                            for hp in range(2):
                                hs = slice(hp * 64, hp * 64 + 64)
                                mm((psc if hp == 0 else ptr)[rows, c * 64:(c + 1) * 64],
                                   khT[c][hs, cols], qhT[c][hs, cols], True, True, [f"khT{c}", f"qhT{c}"], ["psc" if hp == 0 else "ptr"])
                        for c in range(4):
                            mm(pkv[:, c * 128:(c + 1) * 128], kh[rows, t, c * 128:(c + 1) * 128], vB[rows, t, c * 128:(c + 1) * 128],
                               True, True, ["kh", "vB"], ["pkv"])
                        tt("dve", sall[rows, 0, :], psc[rows, 0:256], C["cm4_f"][rows, :], ALU.mult, ["psc", "c_cm4_f"], ["sall"])
                        tt("dve", sall[rows, 1, :], ptr[rows, 0:256], C["cm4_f"][rows, :], ALU.mult, ["ptr", "c_cm4_f"], ["sall"])
                        for c in range(4):
                            mm(po[c][:, cols], stb[:, c, :], qhT[c][:, cols], True, False, ["stb", f"qhT{c}"], [f"po{c}"])
                            for hp in range(2):
                                hs = slice(hp * 64, hp * 64 + 64)
                                hcol = slice((2 * c + hp) * 64, (2 * c + hp) * 64 + 64)
                                mm(po[c][hs, cols], vB[rows, t, hcol], sall[rows, hp, c * 64:(c + 1) * 64], False, hp == 1,
                                   ["vB", "sall"], [f"po{c}"])
                        tt("dve", tmall[:], pkv[:, :], C["bd4_f"][:], ALU.mult, ["pkv", "c_bd4_f"], ["tmall"])
                        tt("dve", tmall[:], tmall[:], stf[:].rearrange("p c d -> p (c d)"), ALU.add, ["tmall", "stf"], ["tmall"])
                        for c in range(4):
                            ts("dve", stf[:, c, :], tmall[:, c * 128:(c + 1) * 128], E1[c][:, ch * 64 + 63:ch * 64 + 64], None, ALU.mult, None,
                               ["tmall", f"E1{c}"], ["stf"])
                        copy_op("act", stb[:].rearrange("p c d -> p (c d)"), stf[:].rearrange("p c d -> p (c d)"), ["stf"], ["stb"])
                    for c in range(4):
                        actf(osqs[c][:], po[c][:, :], AF.Square, [f"po{c}"], [f"osq{c}"])
                    for c in range(4):
                        b = pp[0]; bk = "pp0"
                        mm(b[:, :], C["bd_b"][:], osqs[c][:], True, True, [f"osq{c}", "c_bd_b"], [bk])
                        actf(lnts[c][:], b[:, :], AF.Ln, [bk, "epsc"], [f"lnt{c}"], scale=1.0 / 64, bias=epsc[:, :])
                    for c in range(4):
                        actf(ros[c][:], lnts[c][:], AF.Exp, [f"lnt{c}"], [f"ro{c}"], scale=-0.5)
                    for c in range(4):
                        stt(yts[c][:], po[c][:, :], gn[:, 0:1], ros[c][:], ALU.mult, ALU.mult, [f"po{c}", "gn", f"ro{c}"], [f"yt{c}"])
                    for c in range(4):
                        tt("dve", yBT[:, c, :], yts[c][:], sgT[c][:], ALU.mult, [f"yt{c}", f"sgT{c}"], ["yBT"])
                    dma(YT_d[512:1024, tok].rearrange("(c p) s -> p c s", p=128), yBT[:], ["yBT"], [f"YTb{g}"], q="pool")
            P.barrier()

        def phase4(l):
            st = contextlib.ExitStack()
            NB = 5
            LA = 3
            with st:
                kT4 = SB(st, "m_kT4", [96, 4, S], BF16)
                V4 = SB(st, "m_V4", [128, NT, 512], BF16)
                q4 = [SB(st, f"m_q4{i}", [96, 4, 512], BF16) for i in range(2)]
                PTm = [SB(st, f"m_PT{i}", [128, 512], BF16) for i in range(NB)]
                rec = [SB(st, f"m_rec{i}", [128, 512]) for i in range(2)]
                ycT = [SB(st, f"m_ycT{i}", [128, 2, 512], BF16) for i in range(2)]
                pS = [PS(st, f"m_pS{i}") for i in range(NB)]
                pO = [PS(st, f"m_pO{i}") for i in range(2)]
                ks_ = 0; os_ = 0; qi = 0
                pend = []

                def flush(n_keep):
                    while len(pend) > n_keep:
                        pend.pop(0)()
                for hg in range(2):
                    flush(0)
                    dma(kT4[:], kTc_d[hg * 4:(hg + 1) * 4, :, :].rearrange("h p s -> p h s"),
                        [f"kTc{g}" for g in range(NG)], ["kT4"])
                    for part in range(0, NT, 16):
                        pe_ = min(NT, part + 16)
                        dma(V4[:, part:pe_, :], Vc_d[part * 128:pe_ * 128, hg * 512:(hg + 1) * 512].rearrange("(t p) f -> p t f", p=128),
                            [f"Vc{g}" for g in range(NG)], ["V4"])
                    for G in range(NG):
                        q_ = q4[qi % 2]; qk = f"q4{qi % 2}"
                        yc = ycT[qi % 2]; yk = f"ycT{qi % 2}"; qi += 1
                        dma(q_[:], qTc_d[hg * 4:(hg + 1) * 4, :, G * 512:(G + 1) * 512].rearrange("h p s -> p h s"),
                            [f"qTc{G}"], [qk])
                        for hl in range(4):
                            o_ = pO[os_ % 2]; ok = f"pO{os_ % 2}"
                            rc = rec[os_ % 2]; rk = f"rec{os_ % 2}"; os_ += 1
                            last = 4 * G + 3
                            for kt in range(last + 1):
                                j = kt - 4 * G
                                q0 = max(0, j) * 128
                                s_ = pS[ks_ % NB]; sk = f"pS{ks_ % NB}"
                                pt = PTm[ks_ % NB]; pk = f"PTm{ks_ % NB}"; ks_ += 1
                                mm(s_[:, q0:512], kT4[:, hl, kt * 128:(kt + 1) * 128], q_[:, hl, q0:512], True, True, ["kT4", qk], [sk])
                                actf(pt[:, q0:512], s_[:, q0:512], AF.Exp, [sk], [pk])
                                if j >= 0:
                                    tt("dve", pt[:, q0:q0 + 128], pt[:, q0:q0 + 128], C["bmask_b"][:], ALU.mult, [pk, "c_bmask_b"], [pk])

                                def pv(o_=o_, ok=ok, pt=pt, pk=pk, kt=kt, hl=hl, q0=q0, last=last, rc=rc, rk=rk, yc=yc, yk=yk):
                                    mm(o_[:, q0:512], V4[:, kt, hl * 128:(hl + 1) * 128], pt[:, q0:512], kt == 0, kt == last, ["V4", pk], [ok])
                                    if kt == last:
                                        recip(rc[64:128, :], o_[64:128, :], [ok], [rk])
                                        tt("dve", yc[(hl % 2) * 64:(hl % 2) * 64 + 64, hl // 2, :], o_[0:64, :], rc[64:128, :], ALU.mult,
                                           [ok, rk], [yk])
                                pend.append(pv)
                                flush(LA)
                        flush(0)
                        r0 = 1024 + hg * 256
                        dma(YT_d[r0:r0 + 256, G * 512:(G + 1) * 512].rearrange("(c p) s -> p c s", p=128), yc[:], [yk],
                            [f"YTc{hg}_{G}"], q="pool")
            P.barrier()

        def phase5a(l):
            st = contextlib.ExitStack()
            with st:
                stg = [SB(st, f"o_stg{i}", [128, 2048]) for i in range(4)]
                Wg = SB(st, "o_Wg", [128, 8, 3072], BF16)
                Wbr = SB(st, "o_Wbr", [128, 12, 1024], BF16)
                Wo = SB(st, "o_Wo", [128, 8, 1024], BF16)
                hT = SB(st, "o_hT", [128, 8, 512], BF16)
                YT = SB(st, "o_YT", [128, 12, 512], BF16)
                xg = SB(st, "o_xg", [128, 8, 512])
                eg = [SB(st, f"o_eg{i}", [128, 512]) for i in range(3)]
                mtb = [SB(st, f"o_mtb{i}", [128, 512]) for i in range(2)]
                maccs = [SB(st, f"o_macc{i}", [128, 512]) for i in range(2)]
                mT = SB(st, "o_mT", [128, 8, 512], BF16)
                pg = [PS(st, f"o_pg{i}") for i in range(3)]
                pu = [PS(st, f"o_pu{i}") for i in range(3)]
                pd = [PS(st, f"o_pd{i}") for i in range(2)]
                win = Wd["w_in"][l].rearrange("(c p) n -> p c n", p=128)
                for kc in range(8):
                    load_cast(stg, Wg[:, kc, 0:2048], win[:, kc, 4000:6048], 2048, dstkey="Wg")
                    load_cast(stg, Wg[:, kc, 2048:3072], win[:, kc, 6048:7072], 1024, dstkey="Wg")
                wbr = Wd["w_branch"][l].rearrange("n (c p) d -> p (n c) d", p=128)
                for c in range(12):
                    load_cast(stg, Wbr[:, c, :], wbr[:, c, :], 1024, dstkey="Wbr")
                wo = Wd["w_out"][l].rearrange("(c p) n -> p c n", p=128)
                for kc in range(8):
                    load_cast(stg, Wo[:, kc, :], wo[:, kc, :], 1024, dstkey="Wo")
                kg = 0
                for g in range(NG):
                    tok = slice(g * 512, (g + 1) * 512)
                    dma(hT[:], hT_d.rearrange("(c p) s -> p c s", p=128)[:, :, tok], [f"hTd{g}"], ["hT"])
                    dma(YT[:], YT_d[:, tok].rearrange("(c p) s -> p c s", p=128),
                        [f"YTa{g}", f"YTb{g}", f"YTc0_{g}", f"YTc1_{g}"], ["YT"])
                    dma(xg[:], xT_d.rearrange("(c p) s -> p c s", p=128)[:, :, tok], [f"xT{g}"], ["xg"])
                    for dc in range(8):
                        macc = maccs[dc % 2]; mck = f"macc{dc % 2}"
                        for n_ in range(3):
                            bg = pg[kg % 3]; bgk = f"pg{kg % 3}"
                            bu = pu[kg % 3]; buk = f"pu{kg % 3}"
                            e_ = eg[kg % 3]; ek = f"eg{kg % 3}"
                            mt_ = mtb[kg % 2]; mk = f"mtb{kg % 2}"; kg += 1
                            col = n_ * 1024 + dc * 128
                            for kc in range(8):
                                mm(bg[:, :], Wg[:, kc, col:col + 128], hT[:, kc, :], kc == 0, kc == 7, ["Wg", "hT"], [bgk])
                            for c4 in range(4):
                                mm(bu[:, :], Wbr[:, n_ * 4 + c4, dc * 128:(dc + 1) * 128], YT[:, n_ * 4 + c4, :], c4 == 0, c4 == 3,
                                   ["Wbr", "YT"], [buk])
                            actf(e_[:], bg[:, :], AF.Sigmoid, [bgk], [ek])
                            if n_ == 0:
                                tt("dve", macc[:], bu[:, :], e_[:], ALU.mult, [buk, ek], [mck])
                            else:
                                tt("dve", mt_[:], bu[:, :], e_[:], ALU.mult, [buk, ek], [mk])
                                if n_ == 1:
                                    tt("dve", macc[:], macc[:], mt_[:], ALU.add, [mck, mk], [mck])
                                else:
                                    tt("dve", mT[:, dc, :], macc[:], mt_[:], ALU.add, [mck, mk], [f"mT{dc}"])
                    for dc in range(8):
                        b = pd[dc % 2]; bk = f"pd{dc % 2}"
                        for kc in range(8):
                            mm(b[:, :], Wo[:, kc, dc * 128:(dc + 1) * 128], mT[:, kc, :], kc == 0, kc == 7, ["Wo", f"mT{kc}"], [bk])
                        tt("dve", xg[:, dc, :], b[:, :], xg[:, dc, :], ALU.add, [bk, "xg"], ["xg"])
                    dma(xT_d.rearrange("(c p) s -> p c s", p=128)[:, :, tok], xg[:], ["xg"], [f"xT{g}"], q="pool")
            P.barrier()

        def phase5b(l, final):
            st = contextlib.ExitStack()
            GT = 512
            with st:
                xg = SB(st, "f_xg", [128, 8, GT])
                xflat = xg[:].rearrange("p c s -> p (c s)")
                stg = [xflat[:, 0:2048], xflat[:, 2048:4096]]
                XK = ["stg0", "stg1"]
                W1 = SB(st, "f_W1", [128, 8, 4096], BF16)
                W2 = SB(st, "f_W2", [128, 32, 1024], BF16)
                g2 = SB(st, "f_g2", [128, 8]); gf = SB(st, "f_gf", [128, 8])
                hT = SB(st, "f_hT", [128, 8, GT], BF16)
                rstd = SB(st, "f_rstd", [128, GT]); lnt = SB(st, "f_lnt", [128, GT])
                rl = [SB(st, f"f_rl{i}", [128, GT]) for i in range(2)]
                aT = SB(st, "f_aT", [128, 32, GT], BF16)
                sq = aT[:, 0:8, :]
                SQK = [f"aT{i}" for i in range(8)]
                aflat = aT[:].rearrange("p c s -> p (c s)").bitcast(F32)
                yo = aflat[:, 0:4096].rearrange("p (c s) -> p c s", s=GT)
                ot = aflat[:, 4096:8192].rearrange("p (t d) -> p t d", d=D)
                YOK = [f"aT{i}" for i in range(16)]; OTK = [f"aT{i}" for i in range(16, 32)]
                pss = PS(st, "f_pss")
                ph = [PS(st, f"f_ph{i}") for i in range(2)]
                po = [PS(st, f"f_po{i}") for i in range(2)]
                ptr = [PS(st, f"f_ptr{i}") for i in range(2)]
                w1 = Wd["w_ff1"][l].rearrange("(c p) n -> p c n", p=128)
                for kc in range(8):
                    for hh in range(2):
                        load_cast(stg, W1[:, kc, hh * 2048:(hh + 1) * 2048], w1[:, kc, hh * 2048:(hh + 1) * 2048], 2048, dstkey="W1")
                w2 = Wd["w_ff2"][l].rearrange("(c p) n -> p c n", p=128)
                for fc in range(0, 32, 2):
                    load_cast(stg, W2[:, fc:fc + 2, :].rearrange("p a b -> p (a b)"), w2[:, fc:fc + 2, :], 2048, dstkey="W2")
                dma(g2[:], Wd["norm_ffn_g"][l].rearrange("(c p) -> p c", p=128), [], ["g2"], nc_ok=True)
                dma(gf[:], Wd["final_norm_g"].rearrange("(c p) -> p c", p=128), [], ["gf"], nc_ok=True)
                kk_ = 0; tk = 0
                for sg in range(S // GT):
                    g = sg
                    tok = slice(sg * GT, (sg + 1) * GT)
                    dma(xg[:], xT_d.rearrange("(c p) s -> p c s", p=128)[:, :, tok], [f"xT{g}"] + XK, XK)
                    actf(sq.rearrange("p c s -> p (c s)") if False else sq, xg[:], AF.Square, XK, SQK)
                    for c in range(8):
                        mm(pss[:, 0:GT], C["ones_b"][:], sq[:, c, :], c == 0, c == 7, SQK + ["c_ones_b"], ["pss"])
                    rstd_from(pss[:, 0:GT], rstd[:], lnt[:], 1.0 / D, 128, ["pss"], ["rstd"], "lnt")
                    for c in range(8):
                        stt(hT[:, c, :], xg[:, c, :], g2[:, c:c + 1], rstd[:], ALU.mult, ALU.mult, XK + ["g2", "rstd"], ["hT"])
                    for fc in range(32):
                        b = ph[kk_ % 2]; bk = f"ph{kk_ % 2}"
                        r_ = rl[kk_ % 2]; rk = f"rl{kk_ % 2}"; kk_ += 1
                        for kc in range(8):
                            mm(b[:, 0:GT], W1[:, kc, fc * 128:(fc + 1) * 128], hT[:, kc, :], kc == 0, kc == 7, ["W1", "hT"], [bk])
                        actf(r_[:], b[:, 0:GT], AF.Relu, [bk], [rk])
                        tt("dve", aT[:, fc, :], r_[:], r_[:], ALU.mult, [rk], [f"aT{fc}"])
                    for dc in range(8):
                        b = po[dc % 2]; bk = f"po{dc % 2}"
                        for fc in range(32):
                            mm(b[:, 0:GT], W2[:, fc, dc * 128:(dc + 1) * 128], aT[:, fc, :], fc == 0, fc == 31, ["W2", f"aT{fc}"], [bk])
                        tt("dve", xg[:, dc, :], b[:, 0:GT], xg[:, dc, :], ALU.add, [bk] + XK, XK)
                    if not final:
                        dma(xT_d.rearrange("(c p) s -> p c s", p=128)[:, :, tok], xg[:], XK, [f"xT{g}"], q="pool")
                    else:
                        actf(sq, xg[:], AF.Square, XK, SQK)
                        for c in range(8):
                            mm(pss[:, 0:GT], C["ones_b"][:], sq[:, c, :], c == 0, c == 7, SQK + ["c_ones_b"], ["pss"])
                        rstd_from(pss[:, 0:GT], rstd[:], lnt[:], 1.0 / D, 128, ["pss"], ["rstd"], "lnt")
                        for c in range(8):
                            stt(yo[:, c, :], xg[:, c, :], gf[:, c:c + 1], rstd[:], ALU.mult, ALU.mult, XK + ["gf", "rstd"] + SQK, YOK)
                        for t in range(GT // 128):
                            for half in range(2):
                                b = ptr[tk % 2]; bk = f"ptr{tk % 2}"; tk += 1
                                for c4 in range(4):
                                    c = half * 4 + c4
                                    P.pe(lambda e, o=b[:, c4 * 128:(c4 + 1) * 128], i=yo[:, c, t * 128:(t + 1) * 128]:
                                         e.transpose(o, i, C["ident_f"][:]), YOK + ["c_ident_f"], [bk])
                                copy_op(ev_eng(), ot[:, t, half * 512:(half + 1) * 512], b[:, :], [bk], OTK)
                        dma(out_d[tok, :].rearrange("(t p) d -> p t d", p=128), ot, OTK, [f"out{sg}"], q="pool")
            P.barrier()

        ph_all = phases or ["0", "1a", "1b", "4", "5a", "5b"]
        if "0" in ph_all:
            phase0()
        for l in range(depth):
            if "1a" in ph_all: phase1a(l)
            if "1b" in ph_all: phase1b(l)
            if "4" in ph_all: phase4(l)
            if "5a" in ph_all: phase5a(l)
            if "5b" in ph_all: phase5b(l, final=(l == depth - 1))
        P.emit()
    return nc, P


_CACHE = {}


def _get_prog(S):
    if S not in _CACHE:
        _CACHE[S] = build(S)
    return _CACHE[S]


def run_cores(xs, poss, weights, S):
    nc, _ = _get_prog(S)
    consts = make_consts()
    in_maps = []
    for x, p in zip(xs, poss):
        m = {"x": np.ascontiguousarray(x, dtype=np.float32), "positions": np.ascontiguousarray(p, dtype=np.int32)}
        for k in WEIGHT_SPECS:
            m[k] = np.ascontiguousarray(weights[k], dtype=np.float32)
        m.update(consts)
        in_maps.append(m)
    res = run_bass_kernel_spmd(nc, in_maps, core_ids=list(range(len(xs))))
    return [r["out"] for r in res.results]


def kernel(**inputs):
    x = np.asarray(inputs["x"])
    pos = np.asarray(inputs["positions"])
    B, S, _ = x.shape
    weights = {k: np.asarray(inputs[k]) for k in WEIGHT_SPECS}
    ncore = 8
    xs = [x[i % B] for i in range(ncore)]
    ps = [pos[i % B][None, :] for i in range(ncore)]
    outs = run_cores(xs, ps, weights, S)
    return np.stack([outs[b] for b in range(B)], axis=0).astype(np.float32)
```
